# Optimizing a Trainium2 kernel written in Bass

```python
import math
import jax
import jax.numpy as jnp
from jax import lax
import numpy as np

D_MODEL = 1024
BATCH = 8
SEQ = 4096
DEPTH = 4

CTX_LEN = 256
GRID_W = 64

DA_HEADS = 4
DA_HD = 64
DA_W = DA_HEADS * 2 * DA_HD
ROPE_BASE = 10000.0
Q_BLOCK = 128
RW_HEADS = 8
RW_HD = 64
RW_W = RW_HEADS * RW_HD
RW_DECAY_R = 64
RW_ICLR_R = 64
RW_GATE_R = 128
RW_GN_EPS = 64e-5
RW_COLS = 3 * RW_W + 2 * RW_DECAY_R + 2 * RW_ICLR_R + RW_GATE_R
SSM_HEADS = 8
SSM_HD = 64
SSM_W = SSM_HEADS * SSM_HD
SSM_GROUPS = 2
SSM_STATE = 128
SSM_CONV = 3
SSM_CHUNK = 128
SSM_XBC = SSM_W + 2 * SSM_GROUPS * SSM_STATE
FFN_HIDDEN = (8 * D_MODEL + 3 * 256 - 1) // (3 * 256) * 256
NORM_EPS = 1e-5
W_IN_SPLIT = (DA_W, DA_W, DA_W, RW_COLS, SSM_W, SSM_XBC, 2 * SSM_HEADS, D_MODEL, D_MODEL, D_MODEL)
W_IN_COLS = sum(W_IN_SPLIT)

kernel_name = 'hybrid_diff_rwkv7_ssd_deepnorm_dit'


def _split_cols(t, sizes):
    idx, acc = [], 0
    for s in sizes[:-1]:
        acc += s
        idx.append(acc)
    return jnp.split(t, idx, axis=-1)


def _layernorm(t, g, b):
    tf = t.astype(jnp.float32)
    mu = jnp.mean(tf, -1, keepdims=True)
    var = jnp.mean(jnp.square(tf - mu), -1, keepdims=True)
    return ((tf - mu) * lax.rsqrt(var + NORM_EPS) * g + b).astype(t.dtype)


def _rms(t, eps):
    t = t.astype(jnp.float32)
    return t * lax.rsqrt(jnp.mean(jnp.square(t), -1, keepdims=True) + eps)


def _swiglu(h, w1, w3, w2):
    return (jax.nn.silu(h @ w1) * (h @ w3)) @ w2


def _centred_shift(u):
    up = jnp.pad(u, ((0, 0), (1, 1), (0, 0)))
    return 0.5 * (up[:, :-2] + up[:, 2:])


def _centred_dwconv(u, w, b):
    pad = w.shape[0] // 2
    y = lax.conv_general_dilated(u, w[:, None, :], window_strides=(1,), padding=[(pad, pad)],
                                 dimension_numbers=('NWC', 'WIO', 'NWC'),
                                 feature_group_count=u.shape[-1])
    return y + b


def _axial_rope_angles(n_tok):
    rows = n_tok // GRID_W
    row = jnp.repeat(jnp.arange(rows), GRID_W).astype(jnp.float32)
    col = jnp.tile(jnp.arange(GRID_W), rows).astype(jnp.float32)
    nf = DA_HD // 4
    inv = ROPE_BASE ** (-jnp.arange(nf, dtype=jnp.float32) / nf)
    return row[:, None] * inv, col[:, None] * inv


def _apply_axial_rope(t, ang_r, ang_c):
    def rot(u, ang):
        cs = jnp.cos(ang)[:, None, None, :]
        sn = jnp.sin(ang)[:, None, None, :]
        u1, u2 = jnp.split(u, 2, axis=-1)
        return jnp.concatenate([u1 * cs - u2 * sn, u1 * sn + u2 * cs], -1)
    t_r, t_c = jnp.split(t, 2, axis=-1)
    return jnp.concatenate([rot(t_r, ang_r), rot(t_c, ang_c)], -1).astype(t.dtype)


def _diff_softmax_attend(q, k, v, lam):
    s = jnp.einsum('bqhmd,bkhmd->bhmqk', q, k, preferred_element_type=jnp.float32) * (DA_HD ** -0.5)
    p = jax.nn.softmax(s, axis=-1)
    a = p[:, :, 0] - lam * p[:, :, 1]
    return jnp.einsum('bhqk,bkhe->bqhe', a.astype(v.dtype), v)


def _diff_attn_latent(q, k, v, kc, vc, lam):
    b, L = q.shape[:2]
    k_all = jnp.concatenate([kc, k], axis=1)
    v_all = jnp.concatenate([vc, v], axis=1)
    nb = L // Q_BLOCK
    qb = q.reshape(b, nb, Q_BLOCK, DA_HEADS, 2, DA_HD).transpose(1, 0, 2, 3, 4, 5)
    out = lax.map(lambda qq: _diff_softmax_attend(qq, k_all, v_all, lam), qb)
    return out.transpose(1, 0, 2, 3, 4).reshape(b, L, DA_HEADS, 2 * DA_HD)


def _diff_attn_branch(q, k, v, lp, layer, side, want_out):
    b, L, _ = q.shape
    q = q.reshape(b, L, DA_HEADS, 2, DA_HD)
    k = k.reshape(b, L, DA_HEADS, 2, DA_HD)
    v = v.reshape(b, L, DA_HEADS, 2 * DA_HD)
    lam_init = 0.8 - 0.6 * math.exp(-0.3 * layer)
    f32 = jnp.float32
    lam = (jnp.exp(jnp.sum(lp['da_lq1'].astype(f32) * lp['da_lk1'].astype(f32)))
           - jnp.exp(jnp.sum(lp['da_lq2'].astype(f32) * lp['da_lk2'].astype(f32))) + lam_init)
    if side is None:
        new_side = (k, v)
        if not want_out:
            return None, new_side
        y = _diff_softmax_attend(q, k, v, lam)
    else:
        ang_r, ang_c = _axial_rope_angles(L)
        q = _apply_axial_rope(q, ang_r, ang_c)
        k = _apply_axial_rope(k, ang_r, ang_c)
        new_side = None
        y = _diff_attn_latent(q, k, v, side[0], side[1], lam)
    y = _rms(y, NORM_EPS) * lp['da_norm_g'] * (1.0 - lam_init)
    return y.reshape(b, L, DA_W), new_side


def _rwkv_scan(r, w, k, v, kk, a, s0, reverse):
    def step(S, inp):
        r_t, w_t, k_t, v_t, kk_t, a_t = inp
        s_kk = jnp.einsum('bhvk,bhk->bhv', S, kk_t)
        S = (S * w_t[:, :, None, :] - s_kk[..., None] * (kk_t * a_t)[:, :, None, :]
             + v_t[..., None] * k_t[:, :, None, :])
        return S, jnp.einsum('bhvk,bhk->bhv', S, r_t)
    xs = tuple(jnp.moveaxis(t, 1, 0) for t in (r, w, k, v, kk, a))
    s_fin, ys = lax.scan(step, s0, xs, reverse=reverse)
    return jnp.moveaxis(ys, 0, 1), s_fin


def _rwkv_branch(u, lp, side, want_out):
    b, L, _ = u.shape
    f32 = jnp.float32
    H, N = RW_HEADS, RW_HD
    u = u + lp['rw_mu'] * (_centred_shift(u) - u)
    r, k, v, wd, ad, gd = _split_cols(u, (RW_W, RW_W, RW_W, 2 * RW_DECAY_R, 2 * RW_ICLR_R, RW_GATE_R))
    r = r.astype(f32).reshape(b, L, H, N)
    k = k.astype(f32).reshape(b, L, H, N)
    v = v.astype(f32).reshape(b, L, H, N)
    wd = jnp.tanh(wd.astype(f32)).reshape(b, L, 2, RW_DECAY_R)
    w_raw = lp['rw_w0'] + jnp.einsum('bldr,drc->bldc', wd, lp['rw_w2'])
    decay = jnp.exp(-jnp.exp(-jax.nn.softplus(-w_raw) - 0.5)).reshape(b, L, 2, H, N)
    ad = ad.astype(f32).reshape(b, L, 2, RW_ICLR_R)
    a = jax.nn.sigmoid(lp['rw_a0'] + jnp.einsum('bldr,drc->bldc', ad, lp['rw_a2'])).reshape(b, L, 2, H, N)
    kk = k * lp['rw_kk'].reshape(H, N)
    kk = kk * lax.rsqrt(jnp.maximum(jnp.sum(jnp.square(kk), -1, keepdims=True), 1e-24))
    k_dir = k[:, :, None] * (1.0 + (a - 1.0) * lp['rw_ka'].reshape(H, N))
    if side is None:
        s0f = jnp.zeros((b, H, N, N), f32)
        s0b = jnp.zeros((b, H, N, N), f32)
    else:
        s0f, s0b = side
    y_f, s_f = _rwkv_scan(r, decay[:, :, 0], k_dir[:, :, 0], v, kk, a[:, :, 0], s0f, False)
    y_b, s_b = _rwkv_scan(r, decay[:, :, 1], k_dir[:, :, 1], v, kk, a[:, :, 1], s0b, True)
    new_side = (s_f, s_b)
    if not want_out:
        return None, new_side
    y = y_f + y_b
    mu = jnp.mean(y, -1, keepdims=True)
    var = jnp.mean(jnp.square(y - mu), -1, keepdims=True)
    y = ((y - mu) * lax.rsqrt(var + RW_GN_EPS)).reshape(b, L, RW_W) * lp['rw_ln_g'] + lp['rw_ln_b']
    k_bonus = 0.5 * (k_dir[:, :, 0] + k_dir[:, :, 1])
    bonus = (jnp.sum(r * k_bonus * lp['rw_rk'], -1, keepdims=True) * v).reshape(b, L, RW_W)
    g = jnp.einsum('blr,rc->blc', jax.nn.sigmoid(gd.astype(f32)), lp['rw_g2'])
    return (y + bonus) * g, new_side


def _segsum_exp(a):
    T = a.shape[-1]
    cs = jnp.cumsum(a, -1)
    diff = cs[..., :, None] - cs[..., None, :]
    mask = jnp.tril(jnp.ones((T, T), dtype=bool))
    return jnp.where(mask, jnp.exp(jnp.where(mask, diff, 0.0)), 0.0)


def _ssd(x, dA, bm, cm, s0):
    b, L, h, p = x.shape
    n = bm.shape[-1]
    c, l = L // SSM_CHUNK, SSM_CHUNK
    x = x.reshape(b, c, l, h, p)
    bm = bm.reshape(b, c, l, h, n)
    cm = cm.reshape(b, c, l, h, n)
    a = dA.reshape(b, c, l, h).transpose(0, 3, 1, 2)
    a_cs = jnp.cumsum(a, -1)
    scores = jnp.einsum('bclhn,bcshn->bhcls', cm, bm) * _segsum_exp(a)
    y_diag = jnp.einsum('bhcls,bcshp->bclhp', scores, x)
    decay_states = jnp.exp(a_cs[..., -1:] - a_cs)
    states = jnp.einsum('bclhn,bhcl,bclhp->bchpn', bm, decay_states, x)
    states = jnp.concatenate([s0[:, None], states], axis=1)
    chunk_decay = _segsum_exp(jnp.pad(a_cs[..., -1], ((0, 0), (0, 0), (1, 0))))
    states = jnp.einsum('bhzc,bchpn->bzhpn', chunk_decay, states)
    prev, final = states[:, :-1], states[:, -1]
    y_off = jnp.einsum('bclhn,bchpn,bhcl->bclhp', cm, prev, jnp.exp(a_cs))
    return (y_diag + y_off).reshape(b, L, h, p), final


def _ssm_branch(z, xbc, dt_raw, lp, side, want_out):
    b, L, _ = z.shape
    f32 = jnp.float32
    xbc = jax.nn.silu(_centred_dwconv(xbc, lp['ssm_conv_w'], lp['ssm_conv_b'])).astype(f32)
    xs, bm, cm = _split_cols(xbc, (SSM_W, SSM_GROUPS * SSM_STATE, SSM_GROUPS * SSM_STATE))
    xs = xs.reshape(b, L, SSM_HEADS, SSM_HD)
    rep = SSM_HEADS // SSM_GROUPS
    bm = jnp.repeat(bm.reshape(b, L, SSM_GROUPS, SSM_STATE), rep, axis=2)
    cm = jnp.repeat(cm.reshape(b, L, SSM_GROUPS, SSM_STATE), rep, axis=2)
    dt = jax.nn.softplus(dt_raw.astype(f32).reshape(b, L, 2, SSM_HEADS) + lp['ssm_dt_bias'])
    A = -jnp.exp(lp['ssm_a_log'].astype(f32))
    if side is None:
        s0f = jnp.zeros((b, SSM_HEADS, SSM_HD, SSM_STATE), f32)
        s0b = jnp.zeros((b, SSM_HEADS, SSM_HD, SSM_STATE), f32)
    else:
        s0f, s0b = side
    flip = lambda t: jnp.flip(t, axis=1)
    y_f, s_f = _ssd(xs * dt[:, :, 0, :, None], dt[:, :, 0] * A[0], bm, cm, s0f)
    y_b, s_b = _ssd(flip(xs * dt[:, :, 1, :, None]), flip(dt[:, :, 1] * A[1]), flip(bm), flip(cm), s0b)
    new_side = (s_f, s_b)
    if not want_out:
        return None, new_side
    y = (y_f + flip(y_b) + lp['ssm_d'][:, None] * xs).reshape(b, L, SSM_W) * jax.nn.silu(z.astype(f32))
    y = _rms(y.reshape(b, L, SSM_GROUPS, SSM_W // SSM_GROUPS), NORM_EPS).reshape(b, L, SSM_W)
    return y * lp['ssm_norm_g'], new_side


def _mixer(h, lp, layer, side, want_out):
    q, k, v, rw, z, xbc, dt_raw, ga, gr, gs = _split_cols(h @ lp['w_in'], W_IN_SPLIT)
    sa, sr, ss = (None, None, None) if side is None else side
    y_a, side_a = _diff_attn_branch(q, k, v, lp, layer, sa, want_out)
    y_r, side_r = _rwkv_branch(rw, lp, sr, want_out)
    y_s, side_s = _ssm_branch(z, xbc, dt_raw, lp, ss, want_out)
    new_side = (side_a, side_r, side_s)
    if not want_out:
        return None, new_side
    dt_ = h.dtype
    m = (jax.nn.sigmoid(ga) * (y_a.astype(dt_) @ lp['p_attn'])
         + jax.nn.sigmoid(gr) * (y_r.astype(dt_) @ lp['p_rwkv'])
         + jax.nn.sigmoid(gs) * (y_s.astype(dt_) @ lp['p_ssm']))
    return m @ lp['w_out'], new_side


def setup_inputs(seed: int = 0) -> dict:
    key = jax.random.key(seed)
    ks = iter(jax.random.split(key, 64))
    f32 = jnp.float32
    nrm = lambda shape, s: jax.random.normal(next(ks), shape, f32) * s
    beta = (8.0 * DEPTH) ** -0.25
    chan = jnp.arange(RW_W, dtype=f32) / (RW_W - 1)
    w0_base = -6.0 + 5.0 * chan ** 0.9 + 0.5
    dt0 = jnp.exp(jax.random.uniform(next(ks), (DEPTH, 2, SSM_HEADS), f32, math.log(1e-3), math.log(1e-1)))
    return {
        'x': nrm((BATCH, SEQ, D_MODEL), 1.0),
        'c': nrm((BATCH, D_MODEL), 1.0),
        'ctx': nrm((BATCH, CTX_LEN, D_MODEL), 1.0),
        'c_ctx': nrm((D_MODEL,), 1.0),
        'ada_w': nrm((DEPTH, D_MODEL, 6 * D_MODEL), 0.5 * D_MODEL ** -0.5),
        'ada_b': nrm((DEPTH, 6 * D_MODEL), 0.01),
        'w_in': nrm((DEPTH, D_MODEL, W_IN_COLS), D_MODEL ** -0.5),
        'da_lq1': nrm((DEPTH, DA_HD), 0.1),
        'da_lk1': nrm((DEPTH, DA_HD), 0.1),
        'da_lq2': nrm((DEPTH, DA_HD), 0.1),
        'da_lk2': nrm((DEPTH, DA_HD), 0.1),
        'da_norm_g': 1.0 + nrm((DEPTH, 2 * DA_HD), 0.01),
        'rw_mu': jax.random.uniform(next(ks), (DEPTH, RW_COLS), f32, 0.2, 0.8),
        'rw_w0': w0_base + nrm((DEPTH, 2, RW_W), 0.1),
        'rw_w2': nrm((DEPTH, 2, RW_DECAY_R, RW_W), 0.5 * RW_DECAY_R ** -0.5),
        'rw_a0': nrm((DEPTH, 2, RW_W), 0.1),
        'rw_a2': nrm((DEPTH, 2, RW_ICLR_R, RW_W), 0.5 * RW_ICLR_R ** -0.5),
        'rw_g2': nrm((DEPTH, RW_GATE_R, RW_W), RW_GATE_R ** -0.5),
        'rw_kk': 0.85 + nrm((DEPTH, RW_W), 0.02),
        'rw_ka': 1.0 + nrm((DEPTH, RW_W), 0.02),
        'rw_rk': nrm((DEPTH, RW_HEADS, RW_HD), 0.1),
        'rw_ln_g': 1.0 + nrm((DEPTH, RW_W), 0.01),
        'rw_ln_b': nrm((DEPTH, RW_W), 0.01),
        'ssm_conv_w': nrm((DEPTH, SSM_CONV, SSM_XBC), SSM_CONV ** -0.5),
        'ssm_conv_b': nrm((DEPTH, SSM_XBC), 0.01),
        'ssm_dt_bias': dt0 + jnp.log(-jnp.expm1(-dt0)),
        'ssm_a_log': jnp.log(jax.random.uniform(next(ks), (DEPTH, 2, SSM_HEADS), f32, 1.0, 16.0)),
        'ssm_d': 1.0 + nrm((DEPTH, SSM_HEADS), 0.1),
        'ssm_norm_g': 1.0 + nrm((DEPTH, SSM_W), 0.01),
        'p_attn': nrm((DEPTH, DA_W, D_MODEL), DA_W ** -0.5),
        'p_rwkv': nrm((DEPTH, RW_W, D_MODEL), RW_W ** -0.5),
        'p_ssm': nrm((DEPTH, SSM_W, D_MODEL), SSM_W ** -0.5),
        'w_out': nrm((DEPTH, D_MODEL, D_MODEL), beta * D_MODEL ** -0.5),
        'ln1_g': 1.0 + nrm((DEPTH, D_MODEL), 0.01),
        'ln1_b': nrm((DEPTH, D_MODEL), 0.01),
        'ffn_w1': nrm((DEPTH, D_MODEL, FFN_HIDDEN), D_MODEL ** -0.5),
        'ffn_w3': nrm((DEPTH, D_MODEL, FFN_HIDDEN), D_MODEL ** -0.5),
        'ffn_w2': nrm((DEPTH, FFN_HIDDEN, D_MODEL), beta * FFN_HIDDEN ** -0.5),
        'ln2_g': 1.0 + nrm((DEPTH, D_MODEL), 0.01),
        'ln2_b': nrm((DEPTH, D_MODEL), 0.01),
    }


def reference(x, c, ctx, c_ctx, ada_w, ada_b, w_in, da_lq1, da_lk1, da_lq2, da_lk2, da_norm_g,
              rw_mu, rw_w0, rw_w2, rw_a0, rw_a2, rw_g2, rw_kk, rw_ka, rw_rk, rw_ln_g, rw_ln_b,
              ssm_conv_w, ssm_conv_b, ssm_dt_bias, ssm_a_log, ssm_d, ssm_norm_g,
              p_attn, p_rwkv, p_ssm, w_out, ln1_g, ln1_b, ffn_w1, ffn_w3, ffn_w2, ln2_g, ln2_b):
    alpha = (2.0 * DEPTH) ** 0.25
    xc = ctx
    for l in range(DEPTH):
        lp = {
            'w_in': w_in[l], 'da_lq1': da_lq1[l], 'da_lk1': da_lk1[l], 'da_lq2': da_lq2[l],
            'da_lk2': da_lk2[l], 'da_norm_g': da_norm_g[l], 'rw_mu': rw_mu[l], 'rw_w0': rw_w0[l],
            'rw_w2': rw_w2[l], 'rw_a0': rw_a0[l], 'rw_a2': rw_a2[l], 'rw_g2': rw_g2[l],
            'rw_kk': rw_kk[l], 'rw_ka': rw_ka[l], 'rw_rk': rw_rk[l], 'rw_ln_g': rw_ln_g[l],
            'rw_ln_b': rw_ln_b[l], 'ssm_conv_w': ssm_conv_w[l], 'ssm_conv_b': ssm_conv_b[l],
            'ssm_dt_bias': ssm_dt_bias[l], 'ssm_a_log': ssm_a_log[l], 'ssm_d': ssm_d[l],
            'ssm_norm_g': ssm_norm_g[l], 'p_attn': p_attn[l], 'p_rwkv': p_rwkv[l],
            'p_ssm': p_ssm[l], 'w_out': w_out[l],
        }
        last = l == DEPTH - 1
        mod_x = (jax.nn.silu(c) @ ada_w[l] + ada_b[l])[:, None, :]
        mod_c = jax.nn.silu(c_ctx) @ ada_w[l] + ada_b[l]
        sh1, sc1, g1, sh2, sc2, g2 = jnp.split(mod_x, 6, axis=-1)
        csh1, csc1, cg1, csh2, csc2, cg2 = jnp.split(mod_c, 6, axis=-1)
        out_c, side = _mixer(xc * (1.0 + csc1) + csh1, lp, l, None, not last)
        out_x, _ = _mixer(x * (1.0 + sc1) + sh1, lp, l, side, True)
        x = _layernorm(alpha * x + g1 * out_x, ln1_g[l], ln1_b[l])
        x = _layernorm(alpha * x + g2 * _swiglu(x * (1.0 + sc2) + sh2, ffn_w1[l], ffn_w3[l], ffn_w2[l]),
                       ln2_g[l], ln2_b[l])
        if not last:
            xc = _layernorm(alpha * xc + cg1 * out_c, ln1_g[l], ln1_b[l])
            xc = _layernorm(alpha * xc + cg2 * _swiglu(xc * (1.0 + csc2) + csh2, ffn_w1[l], ffn_w3[l], ffn_w2[l]),
                            ln2_g[l], ln2_b[l])
    return x
```

```python
from contextlib import ExitStack
from concourse.bass_utils import run_bass_kernel_spmd
import numpy as np
import concourse.bass as bass
import concourse.mybir as mybir

F32 = mybir.dt.float32
BF16 = mybir.dt.bfloat16
AF = mybir.ActivationFunctionType
ALU = mybir.AluOpType

ENGS = ("pe", "act", "dve", "pool", "sp")
SAME_ENGINE_SYNC = True
N_DMA_SEMS = 40


class Buf:
    __slots__ = ("name", "w", "r", "fw")

    def __init__(self, name):
        self.name = name
        self.fw = None
        self.w = {}
        self.r = {}


class Tile:
    __slots__ = ("t", "buf", "shape", "dtype")

    def __init__(self, t, name, shape, dtype):
        self.t = t
        self.buf = Buf(name)
        self.shape = shape
        self.dtype = dtype

    def __getitem__(self, idx):
        return self.t[idx]


class Tok:
    __slots__ = ("kind", "eng", "n", "clock", "seen")

    def __init__(self, kind, eng, n, clock):
        self.kind = kind
        self.eng = eng
        self.n = n
        self.clock = clock
        self.seen = set()


class Prog:
    def __init__(self, nc, stack):
        self.nc = nc
        self.stack = stack
        self.E = {"pe": nc.tensor, "act": nc.scalar, "dve": nc.vector,
                  "pool": nc.gpsimd, "sp": nc.sync}
        self.sem = {e: stack.enter_context(nc.semaphore("s_" + e)) for e in ENGS}
        self.cnt = {e: 0 for e in ENGS}
        self.know = {e: {f: 0 for f in ENGS} for e in ENGS}
        self.dsem = [stack.enter_context(nc.semaphore("d%d" % i)) for i in range(N_DMA_SEMS)]
        self.dcnt = [0] * N_DMA_SEMS
        self.dlast = [None] * N_DMA_SEMS
        self.dnext = 0
        self.nbuf = 0
        self.n_wait = 0
        self.n_ins = 0

    def sb(self, name, shape, dtype=F32):
        self.nname = getattr(self, "nname", 0) + 1
        name = "%s_%d" % (name, self.nname)
        stk = self.scopes[-1] if getattr(self, "scopes", None) else self.stack
        t = stk.enter_context(self.nc.sbuf_tensor(name, list(shape), dtype))
        return Tile(t, name, list(shape), dtype)

    def barrier(self):
        for f in ENGS:
            if f != "sp" and self.cnt[f] > 0:
                self._need("sp", Tok("c", f, self.cnt[f], {}))
        for tok in self.dlast:
            if tok is not None:
                self._need("sp", tok)
        tok = self.op("sp", lambda: self.nc.sync.nop())
        for e in ENGS:
            if e != "sp":
                self._need(e, tok)

    def scope(self):
        import contextlib
        prog = self

        @contextlib.contextmanager
        def cm():
            if not getattr(prog, "scopes", None):
                prog.scopes = [prog.stack]
            es = contextlib.ExitStack()
            prog.scopes.append(es)
            try:
                yield
            finally:
                prog.barrier()
                prog.scopes.pop()
                es.close()
        return cm()

    def ps(self, name, shape, dtype=F32):
        t = self.stack.enter_context(self.nc.psum_tensor(name, list(shape), dtype))
        return Tile(t, name, list(shape), dtype)

    def buf(self, name):
        return Buf(name)

    def _need(self, eng, tok):
        if tok is None:
            return
        if tok.kind == "c":
            if tok.eng == eng:
                if eng == "pe" or not SAME_ENGINE_SYNC:
                    return
            if self.know[eng][tok.eng] >= tok.n:
                return
            self.E[eng].wait_ge(self.sem[tok.eng], tok.n)
            self.n_wait += 1
            k = self.know[eng]
            for f, v in tok.clock.items():
                if v > k[f]:
                    k[f] = v
            if tok.n > k[tok.eng]:
                k[tok.eng] = tok.n
        else:
            if eng in tok.seen:
                return
            self.E[eng].wait_ge(self.dsem[tok.eng], tok.n)
            self.n_wait += 1
            tok.seen.add(eng)
            k = self.know[eng]
            for f, v in tok.clock.items():
                if v > k[f]:
                    k[f] = v

    def _deps(self, eng, reads, writes, pwrites=()):
        for b in reads:
            for t in list(b.w.values()):
                self._need(eng, t)
        for b in writes:
            for t in list(b.w.values()):
                self._need(eng, t)
            for t in list(b.r.values()):
                self._need(eng, t)
        for b in pwrites:
            self._need(eng, b.fw)
            for t in list(b.r.values()):
                self._need(eng, t)

    @staticmethod
    def _key(tok):
        return tok.eng if tok.kind == "c" else ("d", tok.eng)

    def _commit(self, tok, reads, writes, pwrites=()):
        k = self._key(tok)
        for b in reads:
            b.r[k] = tok
        for b in writes:
            b.w = {k: tok}
            b.fw = tok
            b.r = {}
        for b in pwrites:
            b.w[k] = tok

    @staticmethod
    def _bufs(xs):
        out = []
        for x in xs:
            if x is None:
                continue
            if isinstance(x, Buf):
                out.append(x)
            else:
                out.append(x.buf)
        return out

    def op(self, eng, fn, reads=(), writes=(), pwrites=()):
        reads = self._bufs(reads)
        writes = self._bufs(writes)
        pwrites = self._bufs(pwrites)
        self._deps(eng, reads, writes, pwrites)
        ins = fn()
        self.cnt[eng] += 1
        n = self.cnt[eng]
        ins.then_inc(self.sem[eng], 1)
        self.n_ins += 1
        tok = Tok("c", eng, n, dict(self.know[eng]))
        self._commit(tok, reads, writes, pwrites)
        return tok

    def dma(self, q, out, in_, reads=(), writes=(), pwrites=(), **kw):
        reads = self._bufs(reads)
        writes = self._bufs(writes)
        pwrites = self._bufs(pwrites)
        self._deps(q, reads, writes, pwrites)
        si = self.dnext
        self.dnext = (self.dnext + 1) % N_DMA_SEMS
        prev = self.dlast[si]
        if prev is not None:
            self._need(q, prev)
        self.dcnt[si] += 16
        ins = self.E[q].dma_start(out=out, in_=in_, **kw)
        ins.then_inc(self.dsem[si], 16)
        self.n_ins += 1
        tok = Tok("d", si, self.dcnt[si], dict(self.know[q]))
        self.dlast[si] = tok
        self._commit(tok, reads, writes, pwrites)
        return tok

    def finish(self, toks, eng="sp"):
        for t in toks:
            self._need(eng, t)
D = 1024
NQKV = 512
FFN = 2816
RWC = 1920
W_IN_COLS = 8080
PV = {}
_o = 0
for _n, _c in [("mu", 15), ("w0", 8), ("a0", 8), ("kk", 4), ("ka", 4), ("rk", 4), ("lng", 4), ("lnb", 4),
               ("cw", 24), ("cb", 8), ("sd", 4), ("sng", 4), ("dag", 1), ("l1g", 8), ("l1b", 8), ("l2g", 8),
               ("l2b", 8), ("adab", 48), ("dal", 4)]:
    PV[_n] = _o
    _o += _c
NPV = _o
CO = {"ident": 0, "msl": 128, "msu": 256, "mil": 384, "miu": 512, "bo64": 640, "ones": 768}
NCONST = 896


def lam_init_of(layer):
    import math
    return 0.8 - 0.6 * math.exp(-0.3 * layer)


def build(SEQ, CTX, DEPTH, debug=False, stop_after=None):
    T = CTX + SEQ
    NT = T // 128
    NCT = CTX // 128
    alpha = (2.0 * DEPTH) ** 0.25
    nc = bass.Bass("TRN2", target_bir_lowering=False)
    ikind = "ExternalOutput" if debug else "Internal"

    def din(name, shape, dt=F32):
        return nc.dram_tensor(name, list(shape), dt, kind="ExternalInput").ap()

    def dscr(name, shape, dt=F32):
        return nc.dram_tensor(name, list(shape), dt, kind=ikind).ap()

    x_in = din("x", [SEQ, D])
    ctx_in = din("ctx", [CTX, D])
    cc_in = din("cc", [128, 16])
    ada_w = din("ada_w", [DEPTH, D, 6 * D])
    w_in = din("w_in", [DEPTH, D, W_IN_COLS])
    w_perm = din("w_perm", [DEPTH, D, 1024])
    pvec_in = din("pvec", [DEPTH, 128, NPV])
    rowp_in = din("rowp", [DEPTH, 128, 32])
    consts_in = din("consts", [128, NCONST])
    rope_in = din("rope", [2, 128, T])
    lvmask_in = din("lvmask", [128, 14, 128])
    rw_w2 = din("rw_w2", [DEPTH, 128, 512])
    rw_a2 = din("rw_a2", [DEPTH, 128, 512])
    rw_g2 = din("rw_g2", [DEPTH, 128, 512])
    p_attn = din("p_attn", [DEPTH, 512, D])
    p_rwkv = din("p_rwkv", [DEPTH, 512, D])
    p_ssm = din("p_ssm", [DEPTH, 512, D])
    w_out = din("w_out", [DEPTH, D, D])
    ffn_w1 = din("ffn_w1", [DEPTH, D, FFN])
    ffn_w3 = din("ffn_w3", [DEPTH, D, FFN])
    ffn_w2 = din("ffn_w2", [DEPTH, FFN, D])
    y_out = nc.dram_tensor("y", [SEQ, D], F32, kind="ExternalOutput").ap()

    XT = dscr("XT", [D, T])
    QR = dscr("QR", [512, T], BF16)
    KR = dscr("KR", [512, T], BF16)
    VA = dscr("VA", [T, 512], BF16)
    RW = dscr("RW", [RWC, T])
    ZS = dscr("ZS", [512, T])
    XBC = dscr("XBC", [1024, T])
    DTS = dscr("DTS", [T, 16])
    GT = dscr("GT", [3072, T])
    YA = dscr("YA", [512, T], BF16)
    YR = dscr("YR", [512, T], BF16)
    YS = dscr("YS", [512, T], BF16)

    st = ExitStack()
    P = Prog(nc, st)
    V, A, G, PE = nc.vector, nc.scalar, nc.gpsimd, nc.tensor

    def blocks(a, b, n=512):
        out = []
        s = a
        while s < b:
            m = min(n, b - s)
            out.append((s, m))
            s += m
        return out
    tblocks = blocks(0, CTX) + blocks(CTX, T)
    streams = [(0, CTX), (CTX, T)]

    consts = P.sb("consts", [128, NCONST])
    cbf = P.sb("cbf", [128, NCONST], BF16)
    pv = P.sb("pv", [128, NPV])
    rowp = P.sb("rowp", [128, 32])
    mod = P.sb("mod", [128, 96])
    onep = P.sb("onep", [128, 96])
    PS = [P.ps("ps%d" % i, [128, 512]) for i in range(8)]
    psi = [0]

    def nps():
        t = PS[psi[0] % 8]
        psi[0] += 1
        return t

    ident = consts[:, CO["ident"]:CO["ident"] + 128]
    onesF = consts[:, CO["ones"]:CO["ones"] + 128]
    bo64F = consts[:, CO["bo64"]:CO["bo64"] + 128]
    onesB = cbf[:, CO["ones"]:CO["ones"] + 128]

    P.dma("sp", consts[:], consts_in, writes=[consts])
    P.op("dve", lambda: V.tensor_copy(out=cbf[:], in_=consts[:]), reads=[consts], writes=[cbf])

    XTb = [P.buf("XT%d" % c) for c in range(8)]
    XTv = XT.rearrange("(c p) t -> p c t", p=128)

    scX = P.scope()
    scX.__enter__()
    xin = [P.sb("xin%d" % i, [128, D]) for i in range(2)]
    xo = [P.sb("xo%d" % i, [128, 8, 128]) for i in range(2)]
    for i in range(NT):
        src = ctx_in[i * 128:(i + 1) * 128, :] if i < NCT else x_in[(i - NCT) * 128:(i - NCT + 1) * 128, :]
        xi, xoo = xin[i % 2], xo[i % 2]
        P.dma("sp", xi[:], src, writes=[xi])
        for half in range(2):
            ps = nps()
            for q in range(4):
                c = half * 4 + q
                P.op("pe", lambda c=c, q=q, ps=ps: PE.transpose(out=ps[:, q * 128:(q + 1) * 128], in_=xi[:, c * 128:(c + 1) * 128], identity=ident),
                     reads=[xi, consts], pwrites=[ps] if q else [], writes=[] if q else [ps])
            eng = "act" if half else "dve"
            if half:
                P.op("act", lambda ps=ps: A.copy(out=xoo[:, 4:8, :], in_=ps[:].rearrange("p (q t) -> p q t", q=4)), reads=[ps], pwrites=[xoo])
            else:
                P.op("dve", lambda ps=ps: V.tensor_copy(out=xoo[:, 0:4, :], in_=ps[:].rearrange("p (q t) -> p q t", q=4)), reads=[ps], writes=[xoo])
        P.dma("sp", XTv[:, :, i * 128:(i + 1) * 128], xoo[:], reads=[xoo], pwrites=XTb)
    scX.__exit__(None, None, None)
    def acopy(out, in_):
        return A.activation(out=out, in_=in_, func=AF.Copy)

    scc = P.sb("scc", [128, 16])
    epst = P.sb("epst", [128, 4])
    P.op("dve", lambda: V.memset(epst[:, 0:1], 1e-5), writes=[epst])
    P.op("dve", lambda: V.memset(epst[:, 1:2], 64e-5), pwrites=[epst])
    P.op("dve", lambda: V.memset(epst[:, 2:3], 0.0), pwrites=[epst])
    epsc = epst[:, 0:1]
    modL = [P.sb("modL%d" % l, [128, 96]) for l in range(DEPTH)]
    pvx = P.sb("pvx", [128, 32])
    DB = {n: P.buf(n) for n in ("QR", "KR", "VA", "RW", "ZS", "XBC", "DTS", "GT", "YA", "YR", "YS")}
    scM = P.scope()
    scM.__enter__()
    P.dma("sp", scc[:], cc_in, writes=[scc])
    P.op("act", lambda: A.activation(out=scc[:], in_=scc[:], func=AF.Silu), reads=[scc], writes=[scc])
    awb = [P.sb("awb%d" % i, [128, 8, 512]) for i in range(2)]
    for l in range(DEPTH):
        P.dma("sp", pv[:], pvec_in[l], writes=[pv])
        psm = nps()
        for blk in range(12):
            aw = awb[blk % 2]
            P.dma("sp", aw[:], ada_w[l, :, blk * 512:(blk + 1) * 512].rearrange("(k p) n -> p k n", p=128), writes=[aw])
            for jj in range(4):
                j = blk * 4 + jj
                for k in range(8):
                    P.op("pe", lambda jj=jj, j=j, k=k, aw=aw: PE.matmul(psm[:, j * 2:j * 2 + 2], lhsT=aw[:, k, jj * 128:(jj + 1) * 128],
                                                                     rhs=scc[:, k * 2:k * 2 + 2], start=(k == 0), stop=(k == 7)),
                         reads=[aw, scc], writes=[psm] if (j == 0 and k == 0) else [], pwrites=[] if (j == 0 and k == 0) else [psm])
        ml = modL[l]
        for n in range(2):
            P.op("dve", lambda n=n, ml=ml: V.tensor_tensor(out=ml[:].rearrange("p (j n) -> p j n", n=2)[:, :, n],
                                                        in0=psm[:, 0:96].rearrange("p (j n) -> p j n", n=2)[:, :, n],
                                                        in1=pv[:, PV["adab"]:PV["adab"] + 48], op=ALU.add),
                 reads=[psm, pv], pwrites=[ml])

    scM.__exit__(None, None, None)
    cnt = {"w": 0, "row": 0, "obf": 0, "ev": 0}

    def load_cast(dst, dview, src, stg):
        sg, sview = stg
        cnt["w"] += 1
        q = "sp" if cnt["w"] % 2 else "act"
        P.dma(q, sview, src, writes=[sg])
        P.op("pool", lambda: G.tensor_copy(out=dview, in_=sview), reads=[sg], writes=[dst])

    def fm_project(wsrc, row, func, Hsrc, wblk):
        wb = wblk[cnt["w"] % 4]
        sg = wblk[4 + cnt["w"] % 2]
        load_cast(wb, wb[:], wsrc.rearrange("(k p) n -> p k n", p=128), (sg, sg[:]))
        first = True
        for (s, n) in tblocks:
            ps = nps()
            for k in range(8):
                P.op("pe", lambda k=k, s=s, n=n, ps=ps: PE.matmul(ps[:, :n], lhsT=wb[:, k, :], rhs=Hsrc[:, k, s:s + n], start=(k == 0), stop=(k == 7)),
                     reads=[wb, Hsrc], writes=[ps] if k == 0 else [], pwrites=[] if k == 0 else [ps])
            use_act = (func != AF.Copy) or (cnt["ev"] % 2 == 0)
            cnt["ev"] += 1
            if use_act:
                P.op("act", lambda s=s, n=n, ps=ps: A.activation(out=row[:, s:s + n], in_=ps[:, :n], func=func),
                     reads=[ps], writes=[row] if first else [], pwrites=[] if first else [row])
            else:
                P.op("dve", lambda s=s, n=n, ps=ps: V.tensor_copy(out=row[:, s:s + n], in_=ps[:, :n]),
                     reads=[ps], writes=[row] if first else [], pwrites=[] if first else [row])
            first = False

    def layer_front(l):
        ml = modL[l]
        HT = P.sb("HT", [128, 8, T], BF16)
        CTt = P.sb("CTt", [128, T])
        STt = P.sb("STt", [128, T])
        P.dma("sp", CTt[:], rope_in[0], writes=[CTt])
        P.dma("sp", STt[:], rope_in[1], writes=[STt])
        rows = [P.sb("row%d" % i, [128, T]) for i in range(3)]
        obf = [P.sb("obf%d" % i, [128, T], BF16) for i in range(1)]
        wblk = [P.sb("wblk%d" % i, [128, 8, 128], BF16) for i in range(4)] + [P.sb("wstg%d" % i, [128, 8, 128]) for i in range(2)]
        wtm = P.sb("wtm", [128, 8, 512], BF16)
        wtms = P.sb("wtms", [128, 4, 512])
        dtws = P.sb("dtws", [128, 8, 16])
        vst = [P.sb("vst0", [128, 512], BF16), P.sb("vst1", [128, 512], BF16)]
        dtw = P.sb("dtw", [128, 8, 16], BF16)
        dst_ = [P.sb("dst0", [128, 16]), P.sb("dst1", [128, 16])]
        P.dma("sp", pv[:], pvec_in[l], writes=[pv])
        P.dma("sp", rowp[:], rowp_in[l], writes=[rowp])
        P.op("dve", lambda: V.tensor_scalar(out=onep[:], in0=ml[:], scalar1=1.0, scalar2=None, op0=ALU.add), reads=[ml], writes=[onep])
        P.op("dve", lambda: V.tensor_scalar(out=pvx[:, 0:15], in0=pv[:, PV["mu"]:PV["mu"] + 15], scalar1=-1.0, scalar2=1.0, op0=ALU.mult, op1=ALU.add),
             reads=[pv], writes=[pvx])
        P.op("dve", lambda: V.tensor_scalar(out=pvx[:, 15:30], in0=pv[:, PV["mu"]:PV["mu"] + 15], scalar1=0.5, scalar2=None, op0=ALU.mult),
             reads=[pv], pwrites=[pvx])
        for c in range(8):
            xr = rows[c % 3]
            P.dma("sp", xr[:], XT[c * 128:(c + 1) * 128, :], reads=[XTb[c]], writes=[xr])
            for n_, (a, b) in enumerate(((CTX, T), (0, CTX))):
                P.op("act", lambda c=c, n_=n_, a=a, b=b, xr=xr: A.activation(out=HT[:, c, a:b], in_=xr[:, a:b], func=AF.Identity,
                                                                          bias=ml[:, c * 2 + n_:c * 2 + n_ + 1],
                                                                          scale=onep[:, (8 + c) * 2 + n_:(8 + c) * 2 + n_ + 1]),
                     reads=[xr, ml, onep], pwrites=[HT] if (c or n_) else [], writes=[] if (c or n_) else [HT])
        wl = w_in[l]
        wp = w_perm[l]
        ri = [0]

        def nrow():
            t = rows[ri[0] % 3]
            ri[0] += 1
            return t

        def nobf():
            t = obf[0]
            cnt["obf"] += 1
            return t
        for which, (c0, dst) in enumerate(((0, QR), (512, KR))):
            for h in range(4):
                ra, rb = nrow(), nrow()
                fm_project(wl[:, c0 + h * 128:c0 + (h + 1) * 128], ra, AF.Copy, HT, wblk)
                fm_project(wp[:, which * 512 + h * 128:which * 512 + (h + 1) * 128], rb, AF.Copy, HT, wblk)
                ob = nobf()
                P.op("dve", lambda ra=ra: V.tensor_tensor(out=ra[:], in0=ra[:], in1=CTt[:], op=ALU.mult), reads=[ra, CTt], writes=[ra])
                P.op("pool", lambda rb=rb: G.tensor_tensor(out=rb[:], in0=rb[:], in1=STt[:], op=ALU.mult), reads=[rb, STt], writes=[rb])
                P.op("dve", lambda ra=ra, rb=rb, ob=ob: V.tensor_tensor(out=ob[:], in0=ra[:], in1=rb[:], op=ALU.add), reads=[ra, rb], writes=[ob])
                P.dma("sp", dst[h * 128:(h + 1) * 128, :], ob[:], reads=[ob], pwrites=[DB["KR" if which else "QR"]])
        for kh in range(2):
            load_cast(wtm, wtm[:, kh * 4:(kh + 1) * 4, :], wl[kh * 512:(kh + 1) * 512, 1024:1536].rearrange("(k p) n -> p k n", p=128), (wtms, wtms[:]))
        load_cast(dtw, dtw[:], wl[:, 4992:5008].rearrange("(k p) n -> p k n", p=128), (dtws, dtws[:]))
        for i in range(NT):
            ps = nps()
            for k in range(8):
                P.op("pe", lambda k=k, i=i, ps=ps: PE.matmul(ps[:, :], lhsT=HT[:, k, i * 128:(i + 1) * 128], rhs=wtm[:, k, :], start=(k == 0), stop=(k == 7)),
                     reads=[wtm, HT], writes=[ps] if k == 0 else [], pwrites=[] if k == 0 else [ps])
            vs = vst[i % 2]
            P.op("act", lambda ps=ps, vs=vs: acopy(vs[:], ps[:]), reads=[ps], writes=[vs])
            P.dma("sp", VA[i * 128:(i + 1) * 128, :], vs[:], reads=[vs], pwrites=[DB["VA"]])
            ps2 = nps()
            for k in range(8):
                P.op("pe", lambda k=k, i=i, ps2=ps2: PE.matmul(ps2[:, 0:16], lhsT=HT[:, k, i * 128:(i + 1) * 128], rhs=dtw[:, k, :], start=(k == 0), stop=(k == 7)),
                     reads=[dtw, HT], writes=[ps2] if k == 0 else [], pwrites=[] if k == 0 else [ps2])
            ds = dst_[i % 2]
            P.op("dve", lambda ps2=ps2, ds=ds: V.tensor_tensor(out=ds[:], in0=ps2[:, 0:16], in1=rowp[:, 0:16], op=ALU.add), reads=[ps2, rowp], writes=[ds])
            P.op("act", lambda ds=ds: A.activation(out=ds[:], in_=ds[:], func=AF.Exp), reads=[ds], writes=[ds])
            P.op("act", lambda ds=ds: A.activation(out=ds[:], in_=ds[:], func=AF.Ln, bias=1.0), reads=[ds], writes=[ds])
            P.dma("sp", DTS[i * 128:(i + 1) * 128, :], ds[:], reads=[ds], pwrites=[DB["DTS"]])
        for j in range(15):
            ra = nrow()
            sb_ = nrow()
            fm_project(wl[:, 1536 + j * 128:1536 + (j + 1) * 128], ra, AF.Copy, HT, wblk)
            for si, (a, b) in enumerate(streams):
                P.op("pool", lambda a=a, b=b, ra=ra, sb_=sb_: G.tensor_tensor(out=sb_[:, a + 1:b - 1], in0=ra[:, a:b - 2], in1=ra[:, a + 2:b], op=ALU.add),
                     reads=[ra], writes=[sb_] if si == 0 else [], pwrites=[] if si == 0 else [sb_])
                P.op("pool", lambda a=a, b=b, ra=ra, sb_=sb_: G.tensor_copy(out=sb_[:, a:a + 1], in_=ra[:, a + 1:a + 2]), reads=[ra], pwrites=[sb_])
                P.op("pool", lambda a=a, b=b, ra=ra, sb_=sb_: G.tensor_copy(out=sb_[:, b - 1:b], in_=ra[:, b - 2:b - 1]), reads=[ra], pwrites=[sb_])
            P.op("dve", lambda j=j, ra=ra: V.tensor_scalar(out=ra[:], in0=ra[:], scalar1=pvx[:, j:j + 1], scalar2=None, op0=ALU.mult), reads=[ra, pvx], writes=[ra])
            P.op("dve", lambda j=j, ra=ra, sb_=sb_: V.scalar_tensor_tensor(out=ra[:], in0=sb_[:], scalar=pvx[:, 15 + j:16 + j], in1=ra[:], op0=ALU.mult, op1=ALU.add),
                 reads=[ra, sb_, pvx], writes=[ra])
            P.dma("sp", RW[j * 128:(j + 1) * 128, :], ra[:], reads=[ra], pwrites=[DB["RW"]])
        for j in range(4):
            ra = nrow()
            fm_project(wl[:, 3456 + j * 128:3456 + (j + 1) * 128], ra, AF.Silu, HT, wblk)
            P.dma("sp", ZS[j * 128:(j + 1) * 128, :], ra[:], reads=[ra], pwrites=[DB["ZS"]])
        for j in range(8):
            ra = nrow()
            o_ = nrow()
            fm_project(wl[:, 3968 + j * 128:3968 + (j + 1) * 128], ra, AF.Copy, HT, wblk)
            cw = PV["cw"]
            P.op("dve", lambda j=j, ra=ra, o_=o_: V.tensor_scalar(out=o_[:], in0=ra[:], scalar1=pv[:, cw + 8 + j:cw + 9 + j], scalar2=pv[:, PV["cb"] + j:PV["cb"] + j + 1],
                                                               op0=ALU.mult, op1=ALU.add), reads=[ra, pv], writes=[o_])
            for (a, b) in streams:
                P.op("dve", lambda j=j, a=a, b=b, ra=ra, o_=o_: V.scalar_tensor_tensor(out=o_[:, a + 1:b], in0=ra[:, a:b - 1], scalar=pv[:, cw + j:cw + j + 1], in1=o_[:, a + 1:b],
                                                                                    op0=ALU.mult, op1=ALU.add), reads=[ra, pv, o_], writes=[o_])
                P.op("dve", lambda j=j, a=a, b=b, ra=ra, o_=o_: V.scalar_tensor_tensor(out=o_[:, a:b - 1], in0=ra[:, a + 1:b], scalar=pv[:, cw + 16 + j:cw + 17 + j], in1=o_[:, a:b - 1],
                                                                                    op0=ALU.mult, op1=ALU.add), reads=[ra, pv, o_], writes=[o_])
            P.op("act", lambda o_=o_: A.activation(out=o_[:], in_=o_[:], func=AF.Silu), reads=[o_], writes=[o_])
            P.dma("sp", XBC[j * 128:(j + 1) * 128, :], o_[:], reads=[o_], pwrites=[DB["XBC"]])
        for j in range(24):
            ra = nrow()
            fm_project(wl[:, 5008 + j * 128:5008 + (j + 1) * 128], ra, AF.Sigmoid, HT, wblk)
            P.dma("sp", GT[j * 128:(j + 1) * 128, :], ra[:], reads=[ra], pwrites=[DB["GT"]])
    def layer_attn(l):
        lam_init = lam_init_of(l)
        dal = PV["dal"]
        lam = P.sb("lam", [128, 4])
        prod = P.sb("prod", [128, 2])
        P.op("dve", lambda: V.tensor_tensor(out=prod[0:64, 0:1], in0=pv[0:64, dal:dal + 1], in1=pv[0:64, dal + 1:dal + 2], op=ALU.mult), reads=[pv], writes=[prod])
        P.op("dve", lambda: V.tensor_tensor(out=prod[0:64, 1:2], in0=pv[0:64, dal + 2:dal + 3], in1=pv[0:64, dal + 3:dal + 4], op=ALU.mult), reads=[pv], pwrites=[prod])
        psl = PS[0]
        P.op("pe", lambda: PE.matmul(psl[:, 0:2], lhsT=consts[0:64, CO["ones"]:CO["ones"] + 128], rhs=prod[0:64, 0:2], start=True, stop=True), reads=[consts, prod], writes=[psl])
        P.op("act", lambda: A.activation(out=lam[:, 0:2], in_=psl[:, 0:2], func=AF.Exp), reads=[psl], writes=[lam])
        P.op("dve", lambda: V.scalar_tensor_tensor(out=lam[:, 2:3], in0=lam[:, 1:2], scalar=-lam_init, in1=lam[:, 0:1], op0=ALU.add, op1=ALU.subtract), reads=[lam], pwrites=[lam])
        neglam = lam[:, 2:3]
        KRh = [P.sb("KRh%d" % i, [128, T], BF16) for i in range(2)]
        QRh = [P.sb("QRh%d" % i, [128, T], BF16) for i in range(2)]
        Vh = [P.sb("Vh%d" % i, [128, NT, 128], BF16) for i in range(2)]
        ee = [[P.sb("e%d_%d" % (m, i), [128, 512], BF16) for i in range(2)] for m in range(2)]
        r0 = P.sb("r0", [128, 512])
        r1 = P.sb("r1", [128, 512])
        t0 = P.sb("t0", [128, 512])
        t1 = P.sb("t1", [128, 512])
        sq = P.sb("sq", [128, 512])
        yo = [P.sb("yo%d" % i, [128, 512], BF16) for i in range(2)]
        acc = [PS[4], PS[5]]
        zz = [PS[6], PS[7]]
        VAv = VA.rearrange("(i p) e -> p i e", p=128)
        qbs = [(s, n, NCT) for (s, n) in blocks(0, CTX)] + [(s, n, NT) for (s, n) in blocks(CTX, T)]
        step = 0
        for h in range(4):
            kr, qr, vh = KRh[h % 2], QRh[h % 2], Vh[h % 2]
            P.dma("sp", kr[:], KR[h * 128:(h + 1) * 128, :], reads=[DB["KR"]], writes=[kr])
            P.dma("sp", qr[:], QR[h * 128:(h + 1) * 128, :], reads=[DB["QR"]], writes=[qr])
            P.dma("sp", vh[:], VAv[:, :, h * 128:(h + 1) * 128], reads=[DB["VA"]], writes=[vh])
            for bi, (s, n, nk) in enumerate(qbs):
                for kt in range(nk):
                    pr = step % 2
                    step += 1
                    pss = [PS[pr * 2], PS[pr * 2 + 1]]
                    es = [ee[0][pr], ee[1][pr]]
                    for m in range(2):
                        P.op("pe", lambda m=m, kt=kt, s=s, n=n, pss=pss: PE.matmul(pss[m][:, :n], lhsT=kr[m * 64:(m + 1) * 64, kt * 128:(kt + 1) * 128],
                                                                                  rhs=qr[m * 64:(m + 1) * 64, s:s + n], start=True, stop=True),
                             reads=[kr, qr], writes=[pss[m]])
                    for m in range(2):
                        P.op("act", lambda m=m, n=n, pss=pss, es=es: A.activation(out=es[m][:, :n], in_=pss[m][:, :n], func=AF.Exp, scale=0.125),
                             reads=[pss[m]], writes=[es[m]])
                    for m in range(2):
                        first = (kt == 0)
                        P.op("pe", lambda m=m, kt=kt, n=n, es=es: PE.matmul(acc[m][:, :n], lhsT=vh[:, kt, :], rhs=es[m][:, :n], start=(kt == 0), stop=(kt == nk - 1)),
                             reads=[vh, es[m]], writes=[acc[m]] if first else [], pwrites=[] if first else [acc[m]])
                        P.op("pe", lambda m=m, kt=kt, n=n, es=es: PE.matmul(zz[m][:, :n], lhsT=onesB, rhs=es[m][:, :n], start=(kt == 0), stop=(kt == nk - 1)),
                             reads=[cbf, es[m]], writes=[zz[m]] if first else [], pwrites=[] if first else [zz[m]])
                P.op("dve", lambda n=n: V.reciprocal(out=r0[:, :n], in_=zz[0][:, :n]), reads=[zz[0]], writes=[r0])
                P.op("dve", lambda n=n: V.tensor_tensor(out=t0[:, :n], in0=acc[0][:, :n], in1=r0[:, :n], op=ALU.mult), reads=[acc[0], r0], writes=[t0])
                P.op("dve", lambda n=n: V.reciprocal(out=r1[:, :n], in_=zz[1][:, :n]), reads=[zz[1]], writes=[r1])
                P.op("dve", lambda n=n: V.tensor_tensor(out=t1[:, :n], in0=acc[1][:, :n], in1=r1[:, :n], op=ALU.mult), reads=[acc[1], r1], writes=[t1])
                P.op("dve", lambda n=n: V.scalar_tensor_tensor(out=t0[:, :n], in0=t1[:, :n], scalar=neglam, in1=t0[:, :n], op0=ALU.mult, op1=ALU.add),
                     reads=[t0, t1, lam], writes=[t0])
                P.op("act", lambda n=n: A.activation(out=sq[:, :n], in_=t0[:, :n], func=AF.Square), reads=[t0], writes=[sq])
                pst = PS[(step % 2) * 2]
                P.op("pe", lambda n=n, pst=pst: PE.matmul(pst[:, :n], lhsT=onesF, rhs=sq[:, :n], start=True, stop=True), reads=[consts, sq], writes=[pst])
                P.op("act", lambda n=n, pst=pst: A.activation(out=r0[:, :n], in_=pst[:, :n], func=AF.Sqrt, scale=1.0 / 128, bias=epsc), reads=[pst, epst], writes=[r0])
                P.op("dve", lambda n=n: V.reciprocal(out=r0[:, :n], in_=r0[:, :n]), reads=[r0], writes=[r0])
                P.op("dve", lambda n=n: V.tensor_tensor(out=t0[:, :n], in0=t0[:, :n], in1=r0[:, :n], op=ALU.mult), reads=[t0, r0], writes=[t0])
                yy = yo[bi % 2]
                P.op("dve", lambda n=n, yy=yy: V.tensor_scalar(out=yy[:, :n], in0=t0[:, :n], scalar1=pv[:, PV["dag"]:PV["dag"] + 1], scalar2=1.0 - lam_init,
                                                            op0=ALU.mult, op1=ALU.mult), reads=[t0, pv], writes=[yy])
                P.dma("sp", YA[h * 128:(h + 1) * 128, s:s + n], yy[:, :n], reads=[yy], pwrites=[DB["YA"]])
    _RWSTOP = globals().get("RWSTOP")
    BG = dscr("BG", [512, T])
    GG = dscr("GG", [512, T])
    DB["BG"] = P.buf("BG")
    DB["GG"] = P.buf("GG")

    def layer_rwkv(l):
        sgf = P.sb("sgf", [128, 512])
        w2b = P.sb("w2b", [128, 512], BF16)
        a2b = P.sb("a2b", [128, 512], BF16)
        g2b = P.sb("g2b", [128, 512], BF16)
        for dst_t, src in ((w2b, rw_w2[l]), (a2b, rw_a2[l]), (g2b, rw_g2[l])):
            P.dma("sp", sgf[:], src, writes=[sgf])
            P.op("pool", lambda dst_t=dst_t: G.tensor_copy(out=dst_t[:], in_=sgf[:]), reads=[sgf], writes=[dst_t])
        omka = P.sb("omka", [128, 4])
        P.op("dve", lambda: V.tensor_scalar(out=omka[:], in0=pv[:, PV["ka"]:PV["ka"] + 4], scalar1=-1.0, scalar2=1.0, op0=ALU.mult, op1=ALU.add), reads=[pv], writes=[omka])
        mk = {}
        for nm in ("msl", "msu", "mil", "miu", "ident", "bo64"):
            mt = P.sb("mk_" + nm, [128, 512])
            for q in range(4):
                P.op("pool", lambda q=q, mt=mt, nm=nm: G.tensor_copy(out=mt[:, q * 128:(q + 1) * 128], in_=consts[:, CO[nm]:CO[nm] + 128]), reads=[consts], pwrites=[mt])
            mk[nm] = mt
        lvs = P.sb("lvs", [128, 14, 128])
        lvm = P.sb("lvm", [128, 14, 512], BF16)
        P.dma("sp", lvs[:], lvmask_in, writes=[lvs])
        for q in range(4):
            P.op("pool", lambda q=q: G.tensor_copy(out=lvm[:, :, q * 128:(q + 1) * 128], in_=lvs[:]), reads=[lvs], pwrites=[lvm])
        Rd = P.sb("Rd", [128, T], BF16)
        Bi = P.sb("Bi", [128, T], BF16)
        Ki = P.sb("Ki", [128, T], BF16)
        KKd = P.sb("KKd", [128, T], BF16)
        BiT = P.sb("BiT", [128, NT, 128], BF16)
        KiT = P.sb("KiT", [128, NT, 128], BF16)
        VZ = P.sb("VZ", [128, NT, 2, 128], BF16)
        GCt = P.sb("GCt", [128, NT])
        YT = P.sb("YT", [128, T])
        TT = P.sb("TT", [128, NT, 256], BF16)
        tn = ["r32", "k32", "v32", "wd32", "ad32", "gd32", "kk32", "t32", "rn32", "p32", "a32", "kd32", "kb32", "b32", "pre", "arr", "ex", "x32"]
        tm = {n_: P.sb(n_, [128, 512]) for n_ in tn}
        twb = P.sb("twb", [128, 512], BF16)
        adb = P.sb("adb", [128, 512], BF16)
        sgb = P.sb("sgb", [128, 512], BF16)
        Nb = P.sb("Nb", [128, 512], BF16)
        NTb = P.sb("NTb", [128, 512], BF16)
        Ok = P.sb("Ok", [128, 512], BF16)
        OTk = P.sb("OTk", [128, 512], BF16)
        Wb = P.sb("Wb", [128, 512], BF16)
        W2b = P.sb("W2b", [128, 512], BF16)
        Tm = P.sb("Tm", [128, 512], BF16)
        TTm = P.sb("TTm", [128, 512], BF16)
        AKL = [P.sb("AKL%d" % e, [128, 384], BF16) for e in range(2)]
        mcat = [P.sb("mcat%d" % i, [128, 384]) for i in range(2)]
        RZ = P.sb("RZ", [128, 2, 128], BF16)
        UZ = P.sb("UZ", [128, 2, 128], BF16)
        Ub = P.sb("Ub", [128, 128], BF16)
        M32 = P.sb("M32", [128, 128])
        Mt32 = P.sb("Mt32", [128, 128])
        Mb = P.sb("Mb", [128, 128], BF16)
        yob = [P.sb("yrb%d" % i, [128, 512], BF16) for i in range(2)]

        def dv(fn, reads, writes=(), pwrites=()):
            return P.op("dve", fn, reads=reads, writes=writes, pwrites=pwrites)

        def ac(fn, reads, writes=(), pwrites=()):
            return P.op("act", fn, reads=reads, writes=writes, pwrites=pwrites)

        def pe(fn, reads, writes=(), pwrites=()):
            return P.op("pe", fn, reads=reads, writes=writes, pwrites=pwrites)

        def po(fn, reads, writes=(), pwrites=()):
            return P.op("pool", fn, reads=reads, writes=writes, pwrites=pwrites)

        for i_, (a_, b_) in enumerate((("msu", "miu"), ("msl", "mil"))):
            po(lambda i_=i_, a_=a_: G.tensor_copy(out=mcat[i_][:, 0:128], in_=consts[:, CO[a_]:CO[a_] + 128]), [consts], [], [mcat[i_]])
            po(lambda i_=i_, b_=b_: G.tensor_copy(out=mcat[i_][:, 128:256], in_=consts[:, CO[b_]:CO[b_] + 128]), [consts], [], [mcat[i_]])
            po(lambda i_=i_, b_=b_: G.tensor_copy(out=mcat[i_][:, 256:384], in_=consts[:, CO[b_]:CO[b_] + 128]), [consts], [], [mcat[i_]])
        po(lambda: G.memset(RZ[:], 0.0), [], [RZ])
        po(lambda: G.memset(UZ[:], 0.0), [], [UZ])
        po(lambda: G.memset(VZ[:], 0.0), [], [VZ])

        for hp in range(4):
            hc = slice(hp * 128, (hp + 1) * 128)
            for d in range(2):
                dh = slice(d * 64, (d + 1) * 64)
                sgc = -1.0 if d == 0 else 1.0
                for bi_, (s, n) in enumerate(tblocks):
                    nch = n // 128
                    t0_ = s // 128
                    r32, k32, v32, wd32, ad32, gd32 = (tm[x] for x in ("r32", "k32", "v32", "wd32", "ad32", "gd32"))
                    kk32, t32, rn32, p32, a32, kd32, kb32, b32, pre, arr, ex, x32 = (tm[x] for x in ("kk32", "t32", "rn32", "p32", "a32", "kd32", "kb32", "b32", "pre", "arr", "ex", "x32"))
                    lds = [(r32, hp * 128), (k32, 512 + hp * 128), (wd32, 1536), (ad32, 1664)]
                    if d == 0:
                        lds += [(v32, 1024 + hp * 128), (gd32, 1792)]
                    for ti, (dst_t, r0_) in enumerate(lds):
                        P.dma("sp" if ti % 2 else "act", dst_t[:, :n], RW[r0_:r0_ + 128, s:s + n], reads=[DB["RW"]], writes=[dst_t])
                    ac(lambda: A.activation(out=twb[:, :n], in_=wd32[:, :n], func=AF.Tanh), [wd32], [twb])
                    ac(lambda: A.activation(out=adb[:, :n], in_=ad32[:, :n], func=AF.Copy), [ad32], [adb])
                    dv(lambda: V.tensor_scalar(out=kk32[:, :n], in0=k32[:, :n], scalar1=pv[:, PV["kk"] + hp:PV["kk"] + hp + 1], scalar2=None, op0=ALU.mult), [k32, pv], [kk32])
                    ac(lambda: A.activation(out=t32[:, :n], in_=kk32[:, :n], func=AF.Square), [kk32], [t32])
                    ps = nps()
                    pe(lambda ps=ps: PE.matmul(ps[:, :n], lhsT=bo64F, rhs=t32[:, :n], start=True, stop=True), [consts, t32], [ps])
                    dv(lambda ps=ps: V.tensor_scalar(out=rn32[:, :n], in0=ps[:, :n], scalar1=1e-24, scalar2=None, op0=ALU.max), [ps], [rn32])
                    ac(lambda: A.activation(out=rn32[:, :n], in_=rn32[:, :n], func=AF.Sqrt), [rn32], [rn32])
                    dv(lambda: V.reciprocal(out=rn32[:, :n], in_=rn32[:, :n]), [rn32], [rn32])
                    dv(lambda: V.tensor_tensor(out=kk32[:, :n], in0=kk32[:, :n], in1=rn32[:, :n], op=ALU.mult), [kk32, rn32], [kk32])

                    def a_and_kd(dd, a_out, kd_out):
                        ddh = slice(dd * 64, (dd + 1) * 64)
                        ps_ = nps()
                        pe(lambda: PE.matmul(ps_[:, :n], lhsT=a2b[ddh, hc], rhs=adb[ddh, :n], start=True, stop=True), [a2b, adb], [ps_])
                        ac(lambda: A.activation(out=a_out[:, :n], in_=ps_[:, :n], func=AF.Sigmoid, bias=pv[:, PV["a0"] + dd * 4 + hp:PV["a0"] + dd * 4 + hp + 1]), [ps_, pv], [a_out])
                        dv(lambda: V.tensor_scalar(out=t32[:, :n], in0=a_out[:, :n], scalar1=pv[:, PV["ka"] + hp:PV["ka"] + hp + 1], scalar2=omka[:, hp:hp + 1], op0=ALU.mult, op1=ALU.add),
                           [a_out, pv, omka], [t32])
                        dv(lambda: V.tensor_tensor(out=kd_out[:, :n], in0=t32[:, :n], in1=k32[:, :n], op=ALU.mult), [t32, k32], [kd_out])

                    if d == 0:
                        ac(lambda: A.activation(out=sgb[:, :n], in_=gd32[:, :n], func=AF.Sigmoid), [gd32], [sgb])
                        ps = nps()
                        pe(lambda ps=ps: PE.matmul(ps[:, :n], lhsT=g2b[:, hc], rhs=sgb[:, :n], start=True, stop=True), [g2b, sgb], [ps])
                        ac(lambda ps=ps: A.activation(out=x32[:, :n], in_=ps[:, :n], func=AF.Copy), [ps], [x32])
                        P.dma("sp", GG[hc, s:s + n], x32[:, :n], reads=[x32], pwrites=[DB["GG"]])
                        ps = nps()
                        for ci in range(nch):
                            pe(lambda ps=ps, ci=ci: PE.transpose(out=ps[:, ci * 128:(ci + 1) * 128], in_=v32[:, ci * 128:(ci + 1) * 128], identity=ident), [v32, consts],
                               [ps] if ci == 0 else [], [] if ci == 0 else [ps])
                        for e in range(2):
                            ac(lambda ps=ps, e=e: A.activation(out=VZ[:, t0_:t0_ + nch, e, e * 64:(e + 1) * 64],
                                                              in_=ps[:, :n].rearrange("p (c t) -> p c t", c=nch)[:, :, e * 64:(e + 1) * 64], func=AF.Copy), [ps], [], [VZ])
                        a_and_kd(1, a32, kb32)
                        a_and_kd(0, a32, kd32)
                        po(lambda: G.tensor_tensor(out=kb32[:, :n], in0=kb32[:, :n], in1=kd32[:, :n], op=ALU.add), [kd32, kb32], [kb32])
                        dv(lambda: V.tensor_tensor(out=x32[:, :n], in0=r32[:, :n], in1=kb32[:, :n], op=ALU.mult), [r32, kb32], [x32])
                        dv(lambda: V.tensor_scalar(out=x32[:, :n], in0=x32[:, :n], scalar1=pv[:, PV["rk"] + hp:PV["rk"] + hp + 1], scalar2=0.5, op0=ALU.mult, op1=ALU.mult), [x32, pv], [x32])
                        ps = nps()
                        pe(lambda ps=ps: PE.matmul(ps[:, :n], lhsT=bo64F, rhs=x32[:, :n], start=True, stop=True), [consts, x32], [ps])
                        dv(lambda ps=ps: V.tensor_tensor(out=x32[:, :n], in0=ps[:, :n], in1=v32[:, :n], op=ALU.mult), [ps, v32], [x32])
                        P.dma("sp", BG[hc, s:s + n], x32[:, :n], reads=[x32], pwrites=[DB["BG"]])
                    else:
                        a_and_kd(1, a32, kd32)
                    ps = nps()
                    pe(lambda ps=ps: PE.matmul(ps[:, :n], lhsT=w2b[dh, hc], rhs=twb[dh, :n], start=True, stop=True), [w2b, twb], [ps])
                    ac(lambda ps=ps: A.activation(out=p32[:, :n], in_=ps[:, :n], func=AF.Sigmoid, bias=pv[:, PV["w0"] + d * 4 + hp:PV["w0"] + d * 4 + hp + 1]), [ps, pv], [p32])
                    dv(lambda: V.tensor_scalar(out=p32[:, :n], in0=p32[:, :n], scalar1=0.6065306597126334, scalar2=None, op0=ALU.mult), [p32], [p32])
                    dv(lambda: V.tensor_tensor(out=b32[:, :n], in0=kk32[:, :n], in1=a32[:, :n], op=ALU.mult), [kk32, a32], [b32])
                    for ci in range(nch):
                        cs = slice(ci * 128, (ci + 1) * 128)
                        dv(lambda cs=cs: V.tensor_tensor_scan(out=pre[:, cs], data0=onesF, data1=p32[:, cs], initial=0.0, op0=ALU.mult, op1=ALU.add), [p32, consts],
                           [pre] if ci == 0 else [], [] if ci == 0 else [pre])
                    for ci in range(nch):
                        ac(lambda ci=ci: A.activation(out=GCt[:, t0_ + ci:t0_ + ci + 1], in_=pre[:, ci * 128 + 127:ci * 128 + 128], func=AF.Exp, scale=-1.0), [pre], [], [GCt])
                    if d == 0:
                        cur = pre
                    else:
                        for ci in range(nch):
                            cs = slice(ci * 128, (ci + 1) * 128)
                            dv(lambda cs=cs, ci=ci: V.scalar_tensor_tensor(out=arr[:, cs], in0=pre[:, cs], scalar=pre[:, ci * 128 + 127:ci * 128 + 128], in1=p32[:, cs],
                                                                         op0=ALU.subtract, op1=ALU.subtract), [pre, p32], [arr] if ci == 0 else [], [] if ci == 0 else [arr])
                        cur = arr
                    ac(lambda cur=cur: A.activation(out=ex[:, :n], in_=cur[:, :n], func=AF.Exp, scale=sgc), [cur], [ex])
                    dv(lambda: V.tensor_tensor(out=Rd[:, s:s + n], in0=r32[:, :n], in1=ex[:, :n], op=ALU.mult), [r32, ex], [] if bi_ else [Rd], [Rd] if bi_ else [])
                    ac(lambda cur=cur: A.activation(out=ex[:, :n], in_=cur[:, :n], func=AF.Exp, scale=-sgc), [cur], [ex])
                    for (src32, dstb, dstT) in ((b32, Bi, BiT), (kd32, Ki, KiT)):
                        dv(lambda src32=src32: V.tensor_tensor(out=t32[:, :n], in0=src32[:, :n], in1=ex[:, :n], op=ALU.mult), [src32, ex], [t32])
                        ac(lambda dstb=dstb: A.activation(out=dstb[:, s:s + n], in_=t32[:, :n], func=AF.Copy), [t32], [] if bi_ else [dstb], [dstb] if bi_ else [])
                        ps = nps()
                        for ci in range(nch):
                            pe(lambda ps=ps, ci=ci: PE.transpose(out=ps[:, ci * 128:(ci + 1) * 128], in_=t32[:, ci * 128:(ci + 1) * 128], identity=ident), [t32, consts],
                               [ps] if ci == 0 else [], [] if ci == 0 else [ps])
                        dv(lambda ps=ps, dstT=dstT: V.tensor_copy(out=dstT[:, t0_:t0_ + nch, :], in_=ps[:, :n].rearrange("p (c t) -> p c t", c=nch)), [ps], [], [dstT])
                    dv(lambda cur=cur: V.scalar_tensor_tensor(out=t32[:, :n], in0=p32[:, :n], scalar=sgc, in1=cur[:, :n], op0=ALU.mult, op1=ALU.add), [p32, cur], [t32])
                    ac(lambda: A.activation(out=ex[:, :n], in_=t32[:, :n], func=AF.Exp, scale=sgc), [t32], [ex])
                    dv(lambda: V.tensor_tensor(out=KKd[:, s:s + n], in0=kk32[:, :n], in1=ex[:, :n], op=ALU.mult), [kk32, ex], [] if bi_ else [KKd], [KKd] if bi_ else [])
                if _RWSTOP == 1:
                    return
                if d == 0:
                    order = list(range(NT))
                    mN, mT, mc = mk["msl"], mk["msu"], mcat[0]
                    lv_o, lv_ot = 0, 7
                else:
                    order = list(range(NCT - 1, -1, -1)) + list(range(NT - 1, NCT - 1, -1))
                    mN, mT, mc = mk["msu"], mk["msl"], mcat[1]
                    lv_o, lv_ot = 7, 0
                for g0 in range(0, NT, 2):
                    grp = list(range(g0, min(g0 + 2, NT)))
                    ng = len(grp)
                    W_ = ng * 128
                    bN = [nps(), nps()]
                    bT = [nps(), nps()]
                    for e in range(2):
                        ph = slice(e * 64, (e + 1) * 64)
                        for cj, c in enumerate(grp):
                            tc = slice(c * 128, (c + 1) * 128)
                            oc = slice(cj * 128, (cj + 1) * 128)
                            pe(lambda e=e, tc=tc, oc=oc, ph=ph: PE.matmul(bN[e][:, oc], lhsT=KKd[ph, tc], rhs=Bi[ph, tc], start=True, stop=True), [KKd, Bi],
                               [bN[e]] if cj == 0 else [], [] if cj == 0 else [bN[e]])
                            pe(lambda e=e, tc=tc, oc=oc, ph=ph: PE.matmul(bT[e][:, oc], lhsT=Bi[ph, tc], rhs=KKd[ph, tc], start=True, stop=True), [KKd, Bi],
                               [bT[e]] if cj == 0 else [], [] if cj == 0 else [bT[e]])
                    if ng < 2:
                        po(lambda: G.memset(Nb[:], 0.0), [], [Nb])
                        po(lambda: G.memset(NTb[:], 0.0), [], [NTb])
                    for e in range(2):
                        full = (e == 0 and ng == 2)
                        dv(lambda e=e: V.tensor_tensor(out=Nb[:, e * 256:e * 256 + W_], in0=bN[e][:, :W_], in1=mN[:, :W_], op=ALU.mult), [bN[e], mN], [Nb] if full else [], [] if full else [Nb])
                        dv(lambda e=e: V.tensor_tensor(out=NTb[:, e * 256:e * 256 + W_], in0=bT[e][:, :W_], in1=mT[:, :W_], op=ALU.mult), [bT[e], mT], [NTb] if full else [], [] if full else [NTb])
                    mats = [(e, cj) for e in range(2) for cj in range(ng)]
                    po(lambda: G.tensor_tensor(out=Ok[:], in0=Nb[:], in1=lvm[:, lv_o, :], op=ALU.mult), [Nb, lvm], [Ok])
                    po(lambda: G.tensor_tensor(out=OTk[:], in0=NTb[:], in1=lvm[:, lv_ot, :], op=ALU.mult), [NTb, lvm], [OTk])
                    po(lambda: G.tensor_tensor(out=Tm[:], in0=mk["ident"][:], in1=Ok[:], op=ALU.subtract), [mk["ident"], Ok], [Tm])
                    po(lambda: G.tensor_tensor(out=TTm[:], in0=mk["ident"][:], in1=OTk[:], op=ALU.subtract), [mk["ident"], OTk], [TTm])
                    for lev in range(1, 7):
                        last = (lev == 6)
                        po(lambda lev=lev: G.tensor_tensor(out=Ok[:], in0=Nb[:], in1=lvm[:, lv_o + lev, :], op=ALU.mult), [Nb, lvm], [Ok])
                        if not last:
                            po(lambda lev=lev: G.tensor_tensor(out=OTk[:], in0=NTb[:], in1=lvm[:, lv_ot + lev, :], op=ALU.mult), [NTb, lvm], [OTk])
                        psW2 = nps()
                        for qi, (e, cj) in enumerate(mats):
                            oc = slice(e * 256 + cj * 128, e * 256 + (cj + 1) * 128)
                            pe(lambda oc=oc, psW2=psW2: PE.matmul(psW2[:, oc], lhsT=Ok[:, oc], rhs=TTm[:, oc], start=True, stop=True), [Ok, TTm], [psW2] if qi == 0 else [], [] if qi == 0 else [psW2])
                        if not last:
                            psW = nps()
                            for qi, (e, cj) in enumerate(mats):
                                oc = slice(e * 256 + cj * 128, e * 256 + (cj + 1) * 128)
                                pe(lambda oc=oc, psW=psW: PE.matmul(psW[:, oc], lhsT=OTk[:, oc], rhs=Tm[:, oc], start=True, stop=True), [OTk, Tm], [psW] if qi == 0 else [], [] if qi == 0 else [psW])
                        ac(lambda psW2=psW2: A.activation(out=W2b[:], in_=psW2[:], func=AF.Copy), [psW2], [W2b])
                        if not last:
                            dv(lambda psW=psW: V.tensor_copy(out=Wb[:], in_=psW[:]), [psW], [Wb])
                        psX2 = nps()
                        for qi, (e, cj) in enumerate(mats):
                            oc = slice(e * 256 + cj * 128, e * 256 + (cj + 1) * 128)
                            pe(lambda oc=oc, psX2=psX2: PE.matmul(psX2[:, oc], lhsT=Tm[:, oc], rhs=W2b[:, oc], start=True, stop=True), [Tm, W2b], [psX2] if qi == 0 else [], [] if qi == 0 else [psX2])
                        if not last:
                            psX = nps()
                            for qi, (e, cj) in enumerate(mats):
                                oc = slice(e * 256 + cj * 128, e * 256 + (cj + 1) * 128)
                                pe(lambda oc=oc, psX=psX: PE.matmul(psX[:, oc], lhsT=TTm[:, oc], rhs=Wb[:, oc], start=True, stop=True), [TTm, Wb], [psX] if qi == 0 else [], [] if qi == 0 else [psX])
                            dv(lambda psX2=psX2: V.tensor_tensor(out=TTm[:], in0=TTm[:], in1=psX2[:], op=ALU.subtract), [psX2, TTm], [TTm])
                            dv(lambda psX=psX: V.tensor_tensor(out=Tm[:], in0=Tm[:], in1=psX[:], op=ALU.subtract), [psX, Tm], [Tm])
                        else:
                            for e in range(2):
                                dv(lambda e=e, psX2=psX2: V.tensor_tensor(out=TT[:, g0:g0 + ng, e * 128:(e + 1) * 128],
                                                                        in0=TTm[:, e * 256:e * 256 + W_].rearrange("p (c t) -> p c t", c=ng),
                                                                        in1=psX2[:, e * 256:e * 256 + W_].rearrange("p (c t) -> p c t", c=ng), op=ALU.subtract), [psX2, TTm], [], [TT])
                if _RWSTOP == 2:
                    return
                dv(lambda: V.memset(M32[:], 0.0), [], [M32])
                dv(lambda: V.memset(Mb[:], 0.0), [], [Mb])
                psA0, psA1, psR, psU, psY, psM = PS[0], PS[1], PS[2], PS[3], PS[4], PS[5]
                psAe = [psA0, psA1]
                for c in order:
                    tc = slice(c * 128, (c + 1) * 128)
                    for e in range(2):
                        ph = slice(e * 64, (e + 1) * 64)
                        pe(lambda e=e, ph=ph: PE.matmul(psAe[e][:, 0:128], lhsT=Ki[ph, tc], rhs=KKd[ph, tc], start=True, stop=True), [Ki, KKd], [psAe[e]])
                        pe(lambda e=e, ph=ph: PE.matmul(psAe[e][:, 128:256], lhsT=Bi[ph, tc], rhs=Rd[ph, tc], start=True, stop=True), [Bi, Rd], [], [psAe[e]])
                        pe(lambda e=e, ph=ph: PE.matmul(psAe[e][:, 256:384], lhsT=Ki[ph, tc], rhs=Rd[ph, tc], start=True, stop=True), [Ki, Rd], [], [psAe[e]])
                        dv(lambda e=e: V.tensor_tensor(out=AKL[e][:], in0=psAe[e][:, 0:384], in1=mc[:], op=ALU.mult), [psAe[e], mc], [AKL[e]])
                    if _RWSTOP == 31:
                        return
                    pe(lambda: PE.matmul(psR[:, 0:128], lhsT=KKd[:, tc], rhs=Mb[:], start=True, stop=False), [KKd, Mb], [psR])
                    for e in range(2):
                        pe(lambda e=e: PE.matmul(psR[:, 0:128], lhsT=AKL[e][:, 0:128], rhs=VZ[:, c, e, :], start=False, stop=(e == 1)), [AKL[e], VZ], [], [psR])
                    for e in range(2):
                        ac(lambda e=e: A.activation(out=RZ[:, e, e * 64:(e + 1) * 64], in_=psR[:, e * 64:(e + 1) * 64], func=AF.Identity, scale=-1.0), [psR], [], [RZ])
                    if _RWSTOP == 32:
                        return
                    for e in range(2):
                        pe(lambda e=e: PE.matmul(psU[:, 0:128], lhsT=TT[:, c, e * 128:(e + 1) * 128], rhs=RZ[:, e, :], start=(e == 0), stop=(e == 1)), [TT, RZ],
                           [psU] if e == 0 else [], [] if e == 0 else [psU])
                    if _RWSTOP == 321:
                        return
                    dv(lambda: V.tensor_copy(out=Ub[:], in_=psU[:, 0:128]), [psU], [Ub])
                    if _RWSTOP == 322:
                        return
                    for e in range(2):
                        dv(lambda e=e: V.tensor_copy(out=UZ[:, e, e * 64:(e + 1) * 64], in_=psU[:, e * 64:(e + 1) * 64]), [psU], [], [UZ])
                    if _RWSTOP == 33:
                        return
                    pe(lambda: PE.matmul(psY[:, 0:128], lhsT=Mb[:], rhs=Rd[:, tc], start=True, stop=False), [Mb, Rd], [psY])
                    for e in range(2):
                        pe(lambda e=e: PE.matmul(psY[:, 0:128], lhsT=UZ[:, e, :], rhs=AKL[e][:, 128:256], start=False, stop=False), [UZ, AKL[e]], [], [psY])
                        pe(lambda e=e: PE.matmul(psY[:, 0:128], lhsT=VZ[:, c, e, :], rhs=AKL[e][:, 256:384], start=False, stop=(e == 1)), [VZ, AKL[e]], [], [psY])
                    if d == 0:
                        ac(lambda: A.activation(out=YT[:, tc], in_=psY[:, 0:128], func=AF.Copy), [psY], [], [YT])
                    else:
                        dv(lambda: V.tensor_tensor(out=YT[:, tc], in0=psY[:, 0:128], in1=YT[:, tc], op=ALU.add), [psY, YT], [], [YT])
                    if _RWSTOP == 34:
                        return
                    pe(lambda: PE.matmul(psM[:, 0:128], lhsT=BiT[:, c, :], rhs=Ub[:], start=True, stop=False), [BiT, Ub], [psM])
                    for e in range(2):
                        pe(lambda e=e: PE.matmul(psM[:, 0:128], lhsT=KiT[:, c, :], rhs=VZ[:, c, e, :], start=False, stop=(e == 1)), [KiT, VZ], [], [psM])
                    dv(lambda: V.tensor_tensor(out=Mt32[:], in0=psM[:, 0:128], in1=mk["bo64"][:, 0:128], op=ALU.mult), [psM, mk["bo64"]], [Mt32])
                    dv(lambda: V.tensor_tensor(out=M32[:], in0=M32[:], in1=Mt32[:], op=ALU.add), [M32, Mt32], [M32])
                    dv(lambda: V.tensor_scalar(out=M32[:], in0=M32[:], scalar1=GCt[:, c:c + 1], scalar2=None, op0=ALU.mult), [M32, GCt], [M32])
                    ac(lambda: A.activation(out=Mb[:], in_=M32[:], func=AF.Copy), [M32], [Mb])
                    if _RWSTOP == 35:
                        return
            if _RWSTOP == 3:
                return
            for bi_, (s, n) in enumerate(tblocks):
                t32, x32, rn32, ex, a32 = tm["t32"], tm["x32"], tm["rn32"], tm["ex"], tm["a32"]
                ps = nps()
                pe(lambda ps=ps: PE.matmul(ps[:, :n], lhsT=bo64F, rhs=YT[:, s:s + n], start=True, stop=True), [consts, YT], [ps])
                dv(lambda ps=ps: V.scalar_tensor_tensor(out=t32[:, :n], in0=ps[:, :n], scalar=-1.0 / 64, in1=YT[:, s:s + n], op0=ALU.mult, op1=ALU.add), [ps, YT], [t32])
                ac(lambda: A.activation(out=x32[:, :n], in_=t32[:, :n], func=AF.Square), [t32], [x32])
                ps = nps()
                pe(lambda ps=ps: PE.matmul(ps[:, :n], lhsT=bo64F, rhs=x32[:, :n], start=True, stop=True), [consts, x32], [ps])
                ac(lambda ps=ps: A.activation(out=rn32[:, :n], in_=ps[:, :n], func=AF.Sqrt, scale=1.0 / 64, bias=epst[:, 1:2]), [ps, epst], [rn32])
                dv(lambda: V.reciprocal(out=rn32[:, :n], in_=rn32[:, :n]), [rn32], [rn32])
                dv(lambda: V.tensor_tensor(out=t32[:, :n], in0=t32[:, :n], in1=rn32[:, :n], op=ALU.mult), [t32, rn32], [t32])
                dv(lambda: V.tensor_scalar(out=t32[:, :n], in0=t32[:, :n], scalar1=pv[:, PV["lng"] + hp:PV["lng"] + hp + 1], scalar2=pv[:, PV["lnb"] + hp:PV["lnb"] + hp + 1],
                                            op0=ALU.mult, op1=ALU.add), [t32, pv], [t32])
                P.dma("sp", ex[:, :n], BG[hc, s:s + n], reads=[DB["BG"]], writes=[ex])
                P.dma("act", a32[:, :n], GG[hc, s:s + n], reads=[DB["GG"]], writes=[a32])
                dv(lambda: V.tensor_tensor(out=t32[:, :n], in0=t32[:, :n], in1=ex[:, :n], op=ALU.add), [t32, ex], [t32])
                yb = yob[bi_ % 2]
                dv(lambda yb=yb: V.tensor_tensor(out=yb[:, :n], in0=t32[:, :n], in1=a32[:, :n], op=ALU.mult), [t32, a32], [yb])
                P.dma("sp", YR[hc, s:s + n], yb[:, :n], reads=[yb], pwrites=[DB["YR"]])
    def layer_ssm(l):
        def dv(fn, reads, writes=(), pwrites=()):
            return P.op("dve", fn, reads=reads, writes=writes, pwrites=pwrites)

        def ac(fn, reads, writes=(), pwrites=()):
            return P.op("act", fn, reads=reads, writes=writes, pwrites=pwrites)

        def pe(fn, reads, writes=(), pwrites=()):
            return P.op("pe", fn, reads=reads, writes=writes, pwrites=pwrites)

        def po(fn, reads, writes=(), pwrites=()):
            return P.op("pool", fn, reads=reads, writes=writes, pwrites=pwrites)

        Bf = P.sb("Bf", [128, 2, T], BF16)
        Cb = P.sb("Cb", [128, 2, T], BF16)
        BT = P.sb("BT", [128, NT, 256], BF16)
        YSa = P.sb("YSa", [128, 4, T])
        DT = P.sb("DT", [128, NT, 16])
        Atm = P.sb("Atm", [128, NT, 16])
        nexpA = P.sb("nexpA", [128, 16])
        stg = [P.sb("sstg%d" % i, [128, 512]) for i in range(2)]
        xsc = [P.sb("xsc%d" % i, [128, 4, 128]) for i in range(2)]
        acs = P.sb("acs", [128, 16])
        nacs = P.sb("nacs", [128, 8])
        coef = P.sb("coef", [128, 8])
        gtot = P.sb("gtot", [128, 8])
        GM = P.sb("GM", [128, 256])
        trA = [P.sb("trA%d" % i, [128, 128]) for i in range(2)]
        df = [P.sb("df%d" % i, [128, 128]) for i in range(2)]
        Lm = [P.sb("Lm%d" % i, [128, 128]) for i in range(2)]
        EB = [P.sb("EB%d" % i, [128, 128]) for i in range(2)]
        WT = [P.sb("WT%d" % i, [128, 128], BF16) for i in range(4)]
        Cd = [P.sb("Cd%d" % i, [128, 128], BF16) for i in range(4)]
        xZ = [P.sb("xZ%d" % i, [128, 2, 128], BF16) for i in range(2)]
        xd = [P.sb("xd%d" % i, [128, 64], BF16) for i in range(2)]
        S32 = P.sb("S32", [128, 8, 64])
        SbZ = P.sb("SbZ", [128, 8, 128], BF16)
        triF = {0: consts[:, CO["miu"]:CO["miu"] + 128], 1: consts[:, CO["mil"]:CO["mil"] + 128]}
        for t_ in xZ:
            po(lambda t_=t_: G.memset(t_[:], 0.0), [], [t_])
        for g in range(2):
            for (dst_t, r0_) in ((Bf, 512 + g * 128), (Cb, 768 + g * 128)):
                for bi_, (s, n) in enumerate(tblocks):
                    sg = stg[bi_ % 2]
                    P.dma("sp" if bi_ % 2 else "act", sg[:, :n], XBC[r0_:r0_ + 128, s:s + n], reads=[DB["XBC"]], writes=[sg])
                    po(lambda dst_t=dst_t, sg=sg, s=s, n=n, g=g: G.tensor_copy(out=dst_t[:, g, s:s + n], in_=sg[:, :n]), [sg], [], [dst_t])
                    if dst_t is Bf:
                        nch = n // 128
                        ps = nps()
                        for ci in range(nch):
                            pe(lambda ps=ps, ci=ci, sg=sg: PE.transpose(out=ps[:, ci * 128:(ci + 1) * 128], in_=sg[:, ci * 128:(ci + 1) * 128], identity=ident), [sg, consts],
                               [ps] if ci == 0 else [], [] if ci == 0 else [ps])
                        dv(lambda ps=ps, s=s, n=n, g=g, nch=nch: V.tensor_copy(out=BT[:, s // 128:s // 128 + nch, g * 128:(g + 1) * 128],
                                                                               in_=ps[:, :n].rearrange("p (c t) -> p c t", c=nch)), [ps], [], [BT])
        P.dma("sp", DT[:], DTS.rearrange("(c p) j -> p c j", p=128), reads=[DB["DTS"]], writes=[DT])
        ac(lambda: A.activation(out=nexpA[:], in_=rowp[:, 16:32], func=AF.Exp), [rowp], [nexpA])
        dv(lambda: V.tensor_scalar(out=nexpA[:], in0=nexpA[:], scalar1=-1.0, scalar2=None, op0=ALU.mult), [nexpA], [nexpA])
        for c in range(NT):
            dv(lambda c=c: V.tensor_tensor(out=Atm[:, c, :], in0=DT[:, c, :], in1=nexpA[:], op=ALU.mult), [DT, nexpA], [], [Atm])
        XBv = XBC.rearrange("(q p) t -> p q t", p=128)
        for d in range(2):
            order = list(range(NT)) if d == 0 else (list(range(NCT - 1, -1, -1)) + list(range(NT - 1, NCT - 1, -1)))
            tri = triF[d]
            dv(lambda: V.memset(S32[:], 0.0), [], [S32])
            po(lambda: G.memset(SbZ[:], 0.0), [], [SbZ])
            for ci_, c in enumerate(order):
                tc = slice(c * 128, (c + 1) * 128)
                xc = xsc[ci_ % 2]
                P.dma("sp", xc[:], XBv[:, 0:4, tc], reads=[DB["XBC"]], writes=[xc])
                psX = nps()
                for q in range(4):
                    pe(lambda q=q, xc=xc, psX=psX: PE.transpose(out=psX[:, q * 128:(q + 1) * 128], in_=xc[:, q, :], identity=ident), [xc, consts],
                       [psX] if q == 0 else [], [] if q == 0 else [psX])
                psS = nps()
                pe(lambda psS=psS: PE.matmul(psS[:, 0:8], lhsT=tri, rhs=Atm[:, c, d * 8:(d + 1) * 8], start=True, stop=True), [consts, Atm], [psS])
                pe(lambda psS=psS: PE.matmul(psS[:, 8:16], lhsT=onesF, rhs=Atm[:, c, d * 8:(d + 1) * 8], start=True, stop=True), [consts, Atm], [], [psS])
                dv(lambda psS=psS: V.tensor_copy(out=acs[:], in_=psS[:, 0:16]), [psS], [acs])
                dv(lambda: V.tensor_scalar(out=nacs[:], in0=acs[:, 0:8], scalar1=-1.0, scalar2=None, op0=ALU.mult), [acs], [nacs])
                dv(lambda: V.tensor_tensor(out=coef[:], in0=acs[:, 8:16], in1=acs[:, 0:8], op=ALU.subtract), [acs], [coef])
                ac(lambda: A.activation(out=coef[:], in_=coef[:], func=AF.Exp), [coef], [coef])
                dv(lambda: V.tensor_tensor(out=coef[:], in0=coef[:], in1=DT[:, c, d * 8:(d + 1) * 8], op=ALU.mult), [coef, DT], [coef])
                ac(lambda: A.activation(out=gtot[:], in_=acs[:, 8:16], func=AF.Exp), [acs], [gtot])
                psG = nps()
                for g in range(2):
                    pe(lambda g=g, psG=psG: PE.matmul(psG[:, g * 128:(g + 1) * 128], lhsT=Bf[:, g, tc], rhs=Cb[:, g, tc], start=True, stop=True), [Bf, Cb],
                       [psG] if g == 0 else [], [] if g == 0 else [psG])
                for g in range(2):
                    dv(lambda g=g, psG=psG: V.tensor_tensor(out=GM[:, g * 128:(g + 1) * 128], in0=psG[:, g * 128:(g + 1) * 128], in1=tri, op=ALU.mult), [psG, consts],
                       [GM] if g == 0 else [], [] if g == 0 else [GM])
                psB = [nps(), nps()]
                psY = nps()
                psSt = nps()
                for h in range(8):
                    j = d * 8 + h
                    g = h // 4
                    e = h % 2
                    hp = h // 2
                    bcol = slice((h % 4) * 128, (h % 4 + 1) * 128)
                    pb = psB[h // 4]
                    ta, dfx, lm, eb = trA[h % 2], df[h % 2], Lm[h % 2], EB[h % 2]
                    wt, cd = WT[h % 4], Cd[h % 4]
                    dv(lambda ta=ta, j=j: V.tensor_scalar(out=ta[:], in0=tri, scalar1=Atm[:, c, j:j + 1], scalar2=None, op0=ALU.mult), [consts, Atm], [ta])
                    pe(lambda ta=ta, pb=pb, bcol=bcol: PE.matmul(pb[:, bcol], lhsT=onesF, rhs=ta[:], start=True, stop=True), [consts, ta],
                       [pb] if h % 4 == 0 else [], [] if h % 4 == 0 else [pb])
                    dv(lambda dfx=dfx, pb=pb, bcol=bcol, h=h: V.tensor_scalar(out=dfx[:], in0=pb[:, bcol], scalar1=nacs[:, h:h + 1], scalar2=0.0, op0=ALU.add, op1=ALU.min), [pb, nacs], [dfx])
                    ac(lambda dfx=dfx, lm=lm: A.activation(out=lm[:], in_=dfx[:], func=AF.Exp), [dfx], [lm])
                    po(lambda lm=lm, wt=wt, g=g: G.tensor_tensor(out=wt[:], in0=lm[:], in1=GM[:, g * 128:(g + 1) * 128], op=ALU.mult), [lm, GM], [wt])
                    ac(lambda eb=eb, pb=pb, bcol=bcol: A.activation(out=eb[:], in_=pb[:, bcol], func=AF.Exp), [pb], [eb])
                    po(lambda eb=eb, cd=cd, g=g: G.tensor_tensor(out=cd[:], in0=Cb[:, g, tc], in1=eb[:], op=ALU.mult), [eb, Cb], [cd])
                    xz = xZ[hp % 2]
                    dv(lambda xz=xz, e=e, h=h, j=j, psX=psX: V.tensor_scalar(out=xz[:, e, e * 64:(e + 1) * 64], in0=psX[:, h * 64:(h + 1) * 64], scalar1=DT[:, c, j:j + 1], scalar2=None, op0=ALU.mult),
                       [psX, DT], [], [xz])
                    xdd = xd[h % 2]
                    dv(lambda xdd=xdd, h=h, psX=psX: V.tensor_scalar(out=xdd[:], in0=psX[:, h * 64:(h + 1) * 64], scalar1=coef[:, h:h + 1], scalar2=None, op0=ALU.mult), [psX, coef], [xdd])
                    ycol = slice(hp * 128, (hp + 1) * 128)
                    firsty = (h == 0)
                    pe(lambda xz=xz, e=e, wt=wt, ycol=ycol: PE.matmul(psY[:, ycol], lhsT=xz[:, e, :], rhs=wt[:], start=(e == 0), stop=False), [xz, wt],
                       [psY] if firsty else [], [] if firsty else [psY])
                    pe(lambda h=h, cd=cd, ycol=ycol, e=e: PE.matmul(psY[:, ycol], lhsT=SbZ[:, h, :], rhs=cd[:], start=False, stop=(e == 1)), [SbZ, cd], [], [psY])
                    pe(lambda h=h, g=g, xdd=xdd: PE.matmul(psSt[:, h * 64:(h + 1) * 64], lhsT=BT[:, c, g * 128:(g + 1) * 128], rhs=xdd[:], start=True, stop=True), [BT, xdd],
                       [psSt] if h == 0 else [], [] if h == 0 else [psSt])
                    dv(lambda h=h: V.scalar_tensor_tensor(out=S32[:, h, :], in0=S32[:, h, :], scalar=gtot[:, h:h + 1], in1=psSt[:, h * 64:(h + 1) * 64], op0=ALU.mult, op1=ALU.add),
                       [S32, gtot, psSt], [], [S32])
                    ac(lambda h=h, e=e: A.activation(out=SbZ[:, h, e * 64:(e + 1) * 64], in_=S32[:, h, :], func=AF.Copy), [S32], [], [SbZ])
                if d == 0:
                    ac(lambda psY=psY: A.activation(out=YSa[:, :, tc], in_=psY[:].rearrange("p (q t) -> p q t", q=4), func=AF.Copy), [psY], [], [YSa])
                else:
                    dv(lambda psY=psY: V.tensor_tensor(out=YSa[:, :, tc], in0=psY[:].rearrange("p (q t) -> p q t", q=4), in1=YSa[:, :, tc], op=ALU.add), [psY, YSa], [], [YSa])
        tt_ = [P.sb("sst%d" % i, [128, 512]) for i in range(2)]
        sq_ = [P.sb("ssq%d" % i, [128, 512]) for i in range(2)]
        xl = [P.sb("sxl%d" % i, [128, 512]) for i in range(2)]
        zl = [P.sb("szl%d" % i, [128, 512]) for i in range(2)]
        rs = P.sb("srs", [128, 512])
        yb_ = [P.sb("syb%d" % i, [128, 512], BF16) for i in range(2)]
        for (s, n) in tblocks:
            for gg in range(2):
                psn = nps()
                for i_ in range(2):
                    hp = gg * 2 + i_
                    hc = slice(hp * 128, (hp + 1) * 128)
                    P.dma("sp", xl[i_][:, :n], XBC[hc, s:s + n], reads=[DB["XBC"]], writes=[xl[i_]])
                    P.dma("act", zl[i_][:, :n], ZS[hc, s:s + n], reads=[DB["ZS"]], writes=[zl[i_]])
                    dv(lambda i_=i_, hp=hp: V.scalar_tensor_tensor(out=tt_[i_][:, :n], in0=xl[i_][:, :n], scalar=pv[:, PV["sd"] + hp:PV["sd"] + hp + 1], in1=YSa[:, hp, s:s + n],
                                                                 op0=ALU.mult, op1=ALU.add), [xl[i_], pv, YSa], [tt_[i_]])
                    dv(lambda i_=i_: V.tensor_tensor(out=tt_[i_][:, :n], in0=tt_[i_][:, :n], in1=zl[i_][:, :n], op=ALU.mult), [tt_[i_], zl[i_]], [tt_[i_]])
                    ac(lambda i_=i_: A.activation(out=sq_[i_][:, :n], in_=tt_[i_][:, :n], func=AF.Square), [tt_[i_]], [sq_[i_]])
                    pe(lambda i_=i_, psn=psn: PE.matmul(psn[:, :n], lhsT=onesF, rhs=sq_[i_][:, :n], start=(i_ == 0), stop=(i_ == 1)), [consts, sq_[i_]],
                       [psn] if i_ == 0 else [], [] if i_ == 0 else [psn])
                ac(lambda psn=psn: A.activation(out=rs[:, :n], in_=psn[:, :n], func=AF.Sqrt, scale=1.0 / 256, bias=epst[:, 0:1]), [psn, epst], [rs])
                dv(lambda: V.reciprocal(out=rs[:, :n], in_=rs[:, :n]), [rs], [rs])
                for i_ in range(2):
                    hp = gg * 2 + i_
                    hc = slice(hp * 128, (hp + 1) * 128)
                    dv(lambda i_=i_: V.tensor_tensor(out=tt_[i_][:, :n], in0=tt_[i_][:, :n], in1=rs[:, :n], op=ALU.mult), [tt_[i_], rs], [tt_[i_]])
                    dv(lambda i_=i_, hp=hp: V.tensor_scalar(out=yb_[i_][:, :n], in0=tt_[i_][:, :n], scalar1=pv[:, PV["sng"] + hp:PV["sng"] + hp + 1], scalar2=None, op0=ALU.mult),
                       [tt_[i_], pv], [yb_[i_]])
                    P.dma("sp", YS[hc, s:s + n], yb_[i_][:, :n], reads=[yb_[i_]], pwrites=[DB["YS"]])
    GTv = GT.rearrange("(b q p) t -> p b q t", b=3, q=8)
    H2D = dscr("H2D", [D, T], BF16)
    H2v = H2D.rearrange("(c p) t -> p c t", p=128)
    DB["H2"] = P.buf("H2")

    def layer_back(l):
        ml = modL[l]

        def dv(fn, reads, writes=(), pwrites=()):
            return P.op("dve", fn, reads=reads, writes=writes, pwrites=pwrites)

        def ac(fn, reads, writes=(), pwrites=()):
            return P.op("act", fn, reads=reads, writes=writes, pwrites=pwrites)

        def pe(fn, reads, writes=(), pwrites=()):
            return P.op("pe", fn, reads=reads, writes=writes, pwrites=pwrites)

        def po(fn, reads, writes=(), pwrites=()):
            return P.op("pool", fn, reads=reads, writes=writes, pwrites=pwrites)

        h2s = P.sb("h2s", [128, 8, 512], BF16)
        Rt = P.sb("Rt", [128, 8, 512])
        sqr = [P.sb("sqr%d" % i, [128, 512]) for i in range(2)]
        mean = P.sb("mean", [128, 512])
        rstd = P.sb("rstd", [128, 512])
        xa = [P.sb("xa%d" % i, [128, 512]) for i in range(2)]

        def layer_norm(s, n, gname, bname, first):
            nidx = 0 if s >= CTX else 1
            psm = nps()
            for oc in range(8):
                pe(lambda oc=oc: PE.matmul(psm[:, :n], lhsT=onesF, rhs=Rt[:, oc, :n], start=(oc == 0), stop=(oc == 7)), [consts, Rt], [psm] if oc == 0 else [], [] if oc == 0 else [psm])
            ac(lambda: A.activation(out=mean[:, :n], in_=psm[:, :n], func=AF.Copy, scale=-1.0 / 1024) if False else A.activation(out=mean[:, :n], in_=psm[:, :n], func=AF.Identity, scale=-1.0 / 1024),
               [psm], [mean])
            psv = nps()
            for oc in range(8):
                po(lambda oc=oc: G.tensor_tensor(out=Rt[:, oc, :n], in0=Rt[:, oc, :n], in1=mean[:, :n], op=ALU.add), [Rt, mean], [], [Rt])
                sq = sqr[oc % 2]
                ac(lambda oc=oc, sq=sq: A.activation(out=sq[:, :n], in_=Rt[:, oc, :n], func=AF.Square), [Rt], [sq])
                pe(lambda oc=oc, sq=sq: PE.matmul(psv[:, :n], lhsT=onesF, rhs=sq[:, :n], start=(oc == 0), stop=(oc == 7)), [consts, sq], [psv] if oc == 0 else [], [] if oc == 0 else [psv])
            ac(lambda: A.activation(out=rstd[:, :n], in_=psv[:, :n], func=AF.Sqrt, scale=1.0 / 1024, bias=epst[:, 0:1]), [psv, epst], [rstd])
            dv(lambda: V.reciprocal(out=rstd[:, :n], in_=rstd[:, :n]), [rstd], [rstd])
            for oc in range(8):
                dv(lambda oc=oc: V.tensor_tensor(out=Rt[:, oc, :n], in0=Rt[:, oc, :n], in1=rstd[:, :n], op=ALU.mult), [Rt, rstd], [], [Rt])
                dv(lambda oc=oc: V.tensor_scalar(out=Rt[:, oc, :n], in0=Rt[:, oc, :n], scalar1=pv[:, PV[gname] + oc:PV[gname] + oc + 1], scalar2=pv[:, PV[bname] + oc:PV[bname] + oc + 1],
                                                 op0=ALU.mult, op1=ALU.add), [Rt, pv], [], [Rt])
                if first:
                    ac(lambda oc=oc: A.activation(out=h2s[:, oc, :n], in_=Rt[:, oc, :n], func=AF.Identity,
                                                   bias=ml[:, (24 + oc) * 2 + nidx:(24 + oc) * 2 + nidx + 1], scale=onep[:, (32 + oc) * 2 + nidx:(32 + oc) * 2 + nidx + 1]),
                       [Rt, ml, onep], [h2s] if oc == 0 else [], [] if oc == 0 else [h2s])
            P.dma("sp", XTv[:, :, s:s + n], Rt[:, :, :n], reads=[Rt], pwrites=XTb)
            if first:
                P.dma("act", H2v[:, :, s:s + n], h2s[:, :, :n], reads=[h2s], pwrites=[DB["H2"]])

        with P.scope():
            pw = P.sb("pw", [128, 12, 1024], BF16)
            wo = P.sb("wo", [128, 8, 1024], BF16)
            wst = [P.sb("wst%d" % i, [128, 4, 1024]) for i in range(2)]
            srcs = [(pw, 0, p_attn[l]), (pw, 4, p_rwkv[l]), (pw, 8, p_ssm[l]), (wo, 0, w_out[l][0:512, :]), (wo, 4, w_out[l][512:1024, :])]
            for i_, (dst_t, k0, src) in enumerate(srcs):
                sg = wst[i_ % 2]
                P.dma("sp" if i_ % 2 else "act", sg[:], src.rearrange("(k p) n -> p k n", p=128), writes=[sg])
                po(lambda dst_t=dst_t, k0=k0, sg=sg: G.tensor_copy(out=dst_t[:, k0:k0 + 4, :], in_=sg[:]), [sg], [], [dst_t])
            yb3 = [P.sb("y3_%d" % i, [128, 4, 512], BF16) for i in range(3)]
            gts = [P.sb("gts%d" % i, [128, 3, 512]) for i in range(2)]
            MT = P.sb("MT", [128, 8, 512], BF16)
            m1 = P.sb("m1", [128, 512])
            m2 = P.sb("m2", [128, 512])
            for (s, n) in tblocks:
                nidx = 0 if s >= CTX else 1
                for b_, (src, nm) in enumerate(((YA, "YA"), (YR, "YR"), (YS, "YS"))):
                    P.dma("sp" if b_ % 2 else "act", yb3[b_][:, :, :n], src.rearrange("(q p) t -> p q t", p=128)[:, :, s:s + n], reads=[DB[nm]], writes=[yb3[b_]])
                for oc in range(8):
                    gt_ = gts[oc % 2]
                    P.dma("sp" if oc % 2 else "act", gt_[:, :, :n], GTv[:, :, oc, s:s + n], reads=[DB["GT"]], writes=[gt_])
                    pss = [nps(), nps(), nps()]
                    for b_ in range(3):
                        for k in range(4):
                            pe(lambda b_=b_, k=k, oc=oc, pss=pss: PE.matmul(pss[b_][:, :n], lhsT=pw[:, b_ * 4 + k, oc * 128:(oc + 1) * 128], rhs=yb3[b_][:, k, :n], start=(k == 0), stop=(k == 3)),
                               [pw, yb3[b_]], [pss[b_]] if k == 0 else [], [] if k == 0 else [pss[b_]])
                    dv(lambda pss=pss, gt_=gt_: V.tensor_tensor(out=m1[:, :n], in0=pss[0][:, :n], in1=gt_[:, 0, :n], op=ALU.mult), [pss[0], gt_], [m1])
                    dv(lambda pss=pss, gt_=gt_: V.tensor_tensor(out=m2[:, :n], in0=pss[1][:, :n], in1=gt_[:, 1, :n], op=ALU.mult), [pss[1], gt_], [m2])
                    po(lambda: G.tensor_tensor(out=m1[:, :n], in0=m1[:, :n], in1=m2[:, :n], op=ALU.add), [m1, m2], [m1])
                    dv(lambda pss=pss, gt_=gt_: V.tensor_tensor(out=m2[:, :n], in0=pss[2][:, :n], in1=gt_[:, 2, :n], op=ALU.mult), [pss[2], gt_], [m2])
                    po(lambda oc=oc: G.tensor_tensor(out=MT[:, oc, :n], in0=m1[:, :n], in1=m2[:, :n], op=ALU.add), [m1, m2], [MT] if oc == 0 else [], [] if oc == 0 else [MT])
                for oc in range(8):
                    ps = nps()
                    for k in range(8):
                        pe(lambda k=k, oc=oc, ps=ps: PE.matmul(ps[:, :n], lhsT=wo[:, k, oc * 128:(oc + 1) * 128], rhs=MT[:, k, :n], start=(k == 0), stop=(k == 7)), [wo, MT],
                           [ps] if k == 0 else [], [] if k == 0 else [ps])
                    x_ = xa[oc % 2]
                    P.dma("sp" if oc % 2 else "act", x_[:, :n], XT[oc * 128:(oc + 1) * 128, s:s + n], reads=[XTb[oc]], writes=[x_])
                    po(lambda x_=x_: G.tensor_scalar(out=x_[:, :n], in0=x_[:, :n], scalar1=alpha, scalar2=None, op0=ALU.mult), [x_], [x_])
                    dv(lambda oc=oc, ps=ps, x_=x_: V.scalar_tensor_tensor(out=Rt[:, oc, :n], in0=ps[:, :n], scalar=ml[:, (16 + oc) * 2 + nidx:(16 + oc) * 2 + nidx + 1], in1=x_[:, :n],
                                                                        op0=ALU.mult, op1=ALU.add), [ps, ml, x_], [Rt] if oc == 0 else [], [] if oc == 0 else [Rt])
                layer_norm(s, n, "l1g", "l1b", True)
        if stop_after == "LN1":
            return
        with P.scope():
            groups = [[(s, n) for (s, n) in blocks(0, CTX)]]
            lat = blocks(CTX, T)
            for i in range(0, len(lat), 2):
                groups.append(lat[i:i + 2])
            HID = P.sb("HID", [128, 22, 1024], BF16)
            H2g = P.sb("H2g", [128, 8, 1024], BF16)
            RG = P.sb("RG", [128, 8, 1024])
            w13 = [P.sb("w13_%d" % i, [128, 8, 128], BF16) for i in range(4)]
            w13s = [P.sb("w13s%d" % i, [128, 8, 128]) for i in range(2)]
            w2b_ = [P.sb("w2b_%d" % i, [128, 22, 128], BF16) for i in range(2)]
            w2s = P.sb("w2s", [128, 22, 128])
            sl = [P.sb("sl%d" % i, [128, 512]) for i in range(2)]
            wc = [0]
            for grp in groups:
                g0 = grp[0][0]
                gn = sum(n_ for (_, n_) in grp)
                P.dma("sp", H2g[:, :, :gn], H2v[:, :, g0:g0 + gn], reads=[DB["H2"]], writes=[H2g])
                for j in range(22):
                    ws = []
                    for wsrc in (ffn_w1, ffn_w3):
                        wb = w13[wc[0] % 4]
                        sg = w13s[wc[0] % 2]
                        P.dma("sp" if wc[0] % 2 else "act", sg[:], wsrc[l][:, j * 128:(j + 1) * 128].rearrange("(k p) n -> p k n", p=128), writes=[sg])
                        po(lambda wb=wb, sg=sg: G.tensor_copy(out=wb[:], in_=sg[:]), [sg], [wb])
                        wc[0] += 1
                        ws.append(wb)
                    for bi_, (s, n) in enumerate(grp):
                        ps1, ps3 = nps(), nps()
                        for k in range(8):
                            pe(lambda k=k, ps1=ps1, s=s, n=n: PE.matmul(ps1[:, :n], lhsT=ws[0][:, k, :], rhs=H2g[:, k, s - g0:s - g0 + n], start=(k == 0), stop=(k == 7)), [ws[0], H2g],
                               [ps1] if k == 0 else [], [] if k == 0 else [ps1])
                        for k in range(8):
                            pe(lambda k=k, ps3=ps3, s=s, n=n: PE.matmul(ps3[:, :n], lhsT=ws[1][:, k, :], rhs=H2g[:, k, s - g0:s - g0 + n], start=(k == 0), stop=(k == 7)), [ws[1], H2g],
                               [ps3] if k == 0 else [], [] if k == 0 else [ps3])
                        st_ = sl[bi_ % 2]
                        ac(lambda ps1=ps1, st_=st_, n=n: A.activation(out=st_[:, :n], in_=ps1[:, :n], func=AF.Silu), [ps1], [st_])
                        dv(lambda ps3=ps3, st_=st_, s=s, n=n, j=j: V.tensor_tensor(out=HID[:, j, s - g0:s - g0 + n], in0=ps3[:, :n], in1=st_[:, :n], op=ALU.mult), [ps3, st_], [], [HID])
                for oc in range(8):
                    wb = w2b_[oc % 2]
                    P.dma("sp", w2s[:], ffn_w2[l][:, oc * 128:(oc + 1) * 128].rearrange("(j p) n -> p j n", p=128), writes=[w2s])
                    po(lambda wb=wb: G.tensor_copy(out=wb[:], in_=w2s[:]), [w2s], [wb])
                    for (s, n) in grp:
                        nidx = 0 if s >= CTX else 1
                        ps = nps()
                        for j in range(22):
                            pe(lambda j=j, ps=ps, s=s, n=n, wb=wb: PE.matmul(ps[:, :n], lhsT=wb[:, j, :], rhs=HID[:, j, s - g0:s - g0 + n], start=(j == 0), stop=(j == 21)), [wb, HID],
                               [ps] if j == 0 else [], [] if j == 0 else [ps])
                        x_ = xa[oc % 2]
                        P.dma("act", x_[:, :n], XT[oc * 128:(oc + 1) * 128, s:s + n], reads=[XTb[oc]], writes=[x_])
                        po(lambda x_=x_, n=n: G.tensor_scalar(out=x_[:, :n], in0=x_[:, :n], scalar1=alpha, scalar2=None, op0=ALU.mult), [x_], [x_])
                        dv(lambda oc=oc, ps=ps, x_=x_, s=s, n=n, nidx=nidx: V.scalar_tensor_tensor(out=RG[:, oc, s - g0:s - g0 + n], in0=ps[:, :n],
                                                                                               scalar=ml[:, (40 + oc) * 2 + nidx:(40 + oc) * 2 + nidx + 1], in1=x_[:, :n],
                                                                                               op0=ALU.mult, op1=ALU.add), [ps, ml, x_], [], [RG])
                for (s, n) in grp:
                    for oc in range(8):
                        po(lambda oc=oc, s=s, n=n: G.tensor_copy(out=Rt[:, oc, :n], in_=RG[:, oc, s - g0:s - g0 + n]), [RG], [Rt] if oc == 0 else [], [] if oc == 0 else [Rt])
                    layer_norm(s, n, "l2g", "l2b", False)

    def final_out():
        xf = [P.sb("xf%d" % i, [128, 8, 128]) for i in range(2)]
        yo_ = [P.sb("yof%d" % i, [128, D]) for i in range(2)]
        toks = []
        for i in range(NCT, NT):
            xi, yo2 = xf[i % 2], yo_[i % 2]
            P.dma("sp", xi[:], XTv[:, :, i * 128:(i + 1) * 128], reads=XTb, writes=[xi])
            for half in range(2):
                ps = nps()
                for q in range(4):
                    c = half * 4 + q
                    P.op("pe", lambda c=c, q=q, ps=ps, xi=xi: PE.transpose(out=ps[:, q * 128:(q + 1) * 128], in_=xi[:, c, :], identity=ident),
                         reads=[xi, consts], pwrites=[ps] if q else [], writes=[] if q else [ps])
                if half:
                    P.op("act", lambda ps=ps, yo2=yo2: A.activation(out=yo2[:, 512:1024], in_=ps[:], func=AF.Copy), reads=[ps], pwrites=[yo2])
                else:
                    P.op("dve", lambda ps=ps, yo2=yo2: V.tensor_copy(out=yo2[:, 0:512], in_=ps[:]), reads=[ps], writes=[yo2])
            toks.append(P.dma("sp", y_out[(i - NCT) * 128:(i - NCT + 1) * 128, :], yo2[:], reads=[yo2]))
        return toks
    for l in range(DEPTH):
        with P.scope():
            layer_front(l)
        with P.scope():
            layer_attn(l)
        with P.scope():
            layer_rwkv(l)
        with P.scope():
            layer_ssm(l)
        with P.scope():
            layer_back(l)
    with P.scope():
        final_out()
    P.barrier()
    return nc, st, P


def _cols(v, n):
    return np.ascontiguousarray(np.asarray(v, np.float32).reshape(n, 128).T)


def _rope_perm():
    perm = np.zeros(128, np.int64)
    for base in range(0, 128, 32):
        for j in range(16):
            perm[base + j] = base + j + 16
            perm[base + 16 + j] = base + j
    return perm


def host_prep(inp, SEQ, CTX, DEPTH, GRID_W=64):
    T = CTX + SEQ
    perm = _rope_perm()
    common = {}
    consts = np.zeros((128, NCONST), np.float32)
    idx = np.arange(128)
    consts[:, CO["ident"]:CO["ident"] + 128] = np.eye(128, dtype=np.float32)
    consts[:, CO["msl"]:CO["msl"] + 128] = (idx[:, None] > idx[None, :])
    consts[:, CO["msu"]:CO["msu"] + 128] = (idx[:, None] < idx[None, :])
    consts[:, CO["mil"]:CO["mil"] + 128] = (idx[:, None] >= idx[None, :])
    consts[:, CO["miu"]:CO["miu"] + 128] = (idx[:, None] <= idx[None, :])
    consts[:, CO["bo64"]:CO["bo64"] + 128] = ((idx[:, None] // 64) == (idx[None, :] // 64))
    consts[:, CO["ones"]:CO["ones"] + 128] = 1.0
    common["consts"] = consts
    lv = np.zeros((128, 14, 128), np.float32)
    for k_ in range(7):
        bsz = 1 << k_
        blk = idx // (2 * bsz)
        half = (idx // bsz) % 2
        mo = ((blk[:, None] == blk[None, :]) & (half[:, None] == 1) & (half[None, :] == 0)).astype(np.float32)
        lv[:, k_, :] = mo
        lv[:, 7 + k_, :] = mo.T
    common["lvmask"] = lv
    rows_ = SEQ // GRID_W
    row = np.repeat(np.arange(rows_), GRID_W).astype(np.float32)
    col = np.tile(np.arange(GRID_W), rows_).astype(np.float32)
    nf = 16
    inv = (10000.0 ** (-np.arange(nf, dtype=np.float32) / nf)).astype(np.float32)
    ang_r = row[:, None] * inv
    ang_c = col[:, None] * inv
    Ct = np.ones((128, T), np.float32)
    St = np.zeros((128, T), np.float32)
    for p in range(128):
        d = p % 64
        ang = ang_r if d < 32 else ang_c
        j = d % 32
        f = j % 16
        Ct[p, CTX:] = np.cos(ang[:, f])
        St[p, CTX:] = (-np.sin(ang[:, f])) if j < 16 else np.sin(ang[:, f])
    common["rope"] = np.stack([Ct, St])
    f32 = lambda a: np.ascontiguousarray(np.asarray(a, np.float32))
    w_in = f32(inp["w_in"])
    common["w_in"] = w_in
    wq = w_in[:, :, 0:512].reshape(DEPTH, 1024, 4, 128)[:, :, :, perm].reshape(DEPTH, 1024, 512)
    wk = w_in[:, :, 512:1024].reshape(DEPTH, 1024, 4, 128)[:, :, :, perm].reshape(DEPTH, 1024, 512)
    common["w_perm"] = np.ascontiguousarray(np.concatenate([wq, wk], axis=2))
    common["ada_w"] = f32(inp["ada_w"])
    pvec = np.zeros((DEPTH, 128, NPV), np.float32)
    rowp = np.zeros((DEPTH, 128, 32), np.float32)
    for l in range(DEPTH):
        def put(name, v, n):
            pvec[l, :, PV[name]:PV[name] + n] = _cols(v, n)
        put("mu", inp["rw_mu"][l], 15)
        put("w0", np.asarray(inp["rw_w0"][l]).reshape(-1), 8)
        put("a0", np.asarray(inp["rw_a0"][l]).reshape(-1), 8)
        put("kk", inp["rw_kk"][l], 4)
        put("ka", inp["rw_ka"][l], 4)
        put("rk", np.asarray(inp["rw_rk"][l]).reshape(-1), 4)
        put("lng", inp["rw_ln_g"][l], 4)
        put("lnb", inp["rw_ln_b"][l], 4)
        put("cw", np.asarray(inp["ssm_conv_w"][l]).reshape(-1), 24)
        put("cb", inp["ssm_conv_b"][l], 8)
        put("sd", np.repeat(np.asarray(inp["ssm_d"][l]), 64), 4)
        put("sng", inp["ssm_norm_g"][l], 4)
        put("dag", inp["da_norm_g"][l], 1)
        put("l1g", inp["ln1_g"][l], 8)
        put("l1b", inp["ln1_b"][l], 8)
        put("l2g", inp["ln2_g"][l], 8)
        put("l2b", inp["ln2_b"][l], 8)
        put("adab", inp["ada_b"][l], 48)
        for i, nm in enumerate(("da_lq1", "da_lk1", "da_lq2", "da_lk2")):
            pvec[l, 0:64, PV["dal"] + i] = np.asarray(inp[nm][l], np.float32)
        rowp[l, :, 0:16] = np.broadcast_to(np.asarray(inp["ssm_dt_bias"][l], np.float32).reshape(1, 16), (128, 16))
        rowp[l, :, 16:32] = np.broadcast_to(np.asarray(inp["ssm_a_log"][l], np.float32).reshape(1, 16), (128, 16))
    common["pvec"] = pvec
    common["rowp"] = rowp
    common["rw_w2"] = f32(inp["rw_w2"]).reshape(DEPTH, 128, 512)
    common["rw_a2"] = f32(inp["rw_a2"]).reshape(DEPTH, 128, 512)
    common["rw_g2"] = f32(inp["rw_g2"])
    for nm in ("p_attn", "p_rwkv", "p_ssm", "w_out", "ffn_w1", "ffn_w3", "ffn_w2"):
        common[nm] = f32(inp[nm])
    in_maps = []
    x = np.asarray(inp["x"], np.float32)
    c = np.asarray(inp["c"], np.float32)
    ctx = np.asarray(inp["ctx"], np.float32)
    c_ctx = np.asarray(inp["c_ctx"], np.float32)
    for b in range(x.shape[0]):
        m = dict(common)
        m["x"] = np.ascontiguousarray(x[b])
        m["ctx"] = np.ascontiguousarray(ctx[b])
        cc = np.zeros((128, 16), np.float32)
        cc[:, 0::2] = _cols(c[b], 8)
        cc[:, 1::2] = _cols(c_ctx, 8)
        m["cc"] = cc
        in_maps.append(m)
    return in_maps


SEQ_FULL, CTX_FULL, DEPTH_FULL = 4096, 256, 4


def kernel(**inputs):
    in_maps = host_prep(inputs, SEQ_FULL, CTX_FULL, DEPTH_FULL)
    nc, st, P = build(SEQ_FULL, CTX_FULL, DEPTH_FULL)
    res = run_bass_kernel_spmd(nc, in_maps, core_ids=list(range(len(in_maps))))
    y = np.stack([np.asarray(r["y"], dtype=np.float32) for r in res.results], axis=0)
    return y
```

```python
from contextlib import ExitStack
from concourse.bass_utils import run_bass_kernel_spmd
import numpy as np
import concourse.bass as bass
import concourse.mybir as mybir

F32 = mybir.dt.float32
BF16 = mybir.dt.bfloat16
AF = mybir.ActivationFunctionType
ALU = mybir.AluOpType

ENGS = ("pe", "act", "dve", "pool", "sp")
SAME_ENGINE_SYNC = True
N_DMA_SEMS = 40


class Buf:
    __slots__ = ("name", "w", "r", "fw")

    def __init__(self, name):
        self.name = name
        self.fw = None
        self.w = {}
        self.r = {}


class Tile:
    __slots__ = ("t", "buf", "shape", "dtype")

    def __init__(self, t, name, shape, dtype):
        self.t = t
        self.buf = Buf(name)
        self.shape = shape
        self.dtype = dtype

    def __getitem__(self, idx):
        return self.t[idx]


class Tok:
    __slots__ = ("kind", "eng", "n", "clock", "seen")

    def __init__(self, kind, eng, n, clock):
        self.kind = kind
        self.eng = eng
        self.n = n
        self.clock = clock
        self.seen = set()


class Prog:
    def __init__(self, nc, stack):
        self.nc = nc
        self.stack = stack
        self.E = {"pe": nc.tensor, "act": nc.scalar, "dve": nc.vector,
                  "pool": nc.gpsimd, "sp": nc.sync}
        self.sem = {e: stack.enter_context(nc.semaphore("s_" + e)) for e in ENGS}
        self.cnt = {e: 0 for e in ENGS}
        self.know = {e: {f: 0 for f in ENGS} for e in ENGS}
        self.dsem = [stack.enter_context(nc.semaphore("d%d" % i)) for i in range(N_DMA_SEMS)]
        self.dcnt = [0] * N_DMA_SEMS
        self.dlast = [None] * N_DMA_SEMS
        self.dnext = 0
        self.nbuf = 0
        self.pend = {e: [] for e in ENGS}
        self.n_wait = 0
        self.n_ins = 0

    def sb(self, name, shape, dtype=F32):
        self.nname = getattr(self, "nname", 0) + 1
        name = "%s_%d" % (name, self.nname)
        stk = self.scopes[-1] if getattr(self, "scopes", None) else self.stack
        t = stk.enter_context(self.nc.sbuf_tensor(name, list(shape), dtype))
        return Tile(t, name, list(shape), dtype)

    def barrier(self):
        for f in ENGS:
            if f != "sp" and self.cnt[f] > 0:
                self._need("sp", Tok("c", f, self.cnt[f], {}))
        for tok in self.dlast:
            if tok is not None:
                self._need("sp", tok)
        tok = self.op("sp", lambda: self.nc.sync.nop())
        for e in ENGS:
            if e != "sp":
                self._need(e, tok)

    def scope(self):
        import contextlib
        prog = self

        @contextlib.contextmanager
        def cm():
            if not getattr(prog, "scopes", None):
                prog.scopes = [prog.stack]
            es = contextlib.ExitStack()
            prog.scopes.append(es)
            try:
                yield
            finally:
                prog.barrier()
                prog.scopes.pop()
                es.close()
        return cm()

    def ps(self, name, shape, dtype=F32):
        t = self.stack.enter_context(self.nc.psum_tensor(name, list(shape), dtype))
        return Tile(t, name, list(shape), dtype)

    def buf(self, name):
        return Buf(name)

    def _need(self, eng, tok):
        if tok is None:
            return
        if tok.kind == "c":
            if tok.eng == eng:
                if eng == "pe" or not SAME_ENGINE_SYNC:
                    return
            if self.know[eng][tok.eng] >= tok.n:
                return
            self.pend[eng].append((self.sem[tok.eng], tok.n))
            self.n_wait += 1
            k = self.know[eng]
            for f, v in tok.clock.items():
                if v > k[f]:
                    k[f] = v
            if tok.n > k[tok.eng]:
                k[tok.eng] = tok.n
        else:
            if eng in tok.seen:
                return
            self.pend[eng].append((self.dsem[tok.eng], tok.n))
            self.n_wait += 1
            tok.seen.add(eng)
            k = self.know[eng]
            for f, v in tok.clock.items():
                if v > k[f]:
                    k[f] = v

    def _deps(self, eng, reads, writes, pwrites=()):
        for b in reads:
            for t in list(b.w.values()):
                self._need(eng, t)
        for b in writes:
            for t in list(b.w.values()):
                self._need(eng, t)
            for t in list(b.r.values()):
                self._need(eng, t)
        for b in pwrites:
            self._need(eng, b.fw)
            for t in list(b.r.values()):
                self._need(eng, t)

    @staticmethod
    def _key(tok):
        return tok.eng if tok.kind == "c" else ("d", tok.eng)

    def _commit(self, tok, reads, writes, pwrites=()):
        k = self._key(tok)
        for b in reads:
            b.r[k] = tok
        for b in writes:
            b.w = {k: tok}
            b.fw = tok
            b.r = {}
        for b in pwrites:
            b.w[k] = tok

    @staticmethod
    def _bufs(xs):
        out = []
        for x in xs:
            if x is None:
                continue
            if isinstance(x, Buf):
                out.append(x)
            else:
                out.append(x.buf)
        return out

    def op(self, eng, fn, reads=(), writes=(), pwrites=()):
        reads = self._bufs(reads)
        writes = self._bufs(writes)
        pwrites = self._bufs(pwrites)
        self._deps(eng, reads, writes, pwrites)
        waits = self.pend[eng]
        self.pend[eng] = []
        for (sm, vv) in waits[:-1]:
            self.E[eng].wait_ge(sm, vv)
        ins = fn()
        if waits:
            ins._wait_ge(waits[-1][0], waits[-1][1])
        self.cnt[eng] += 1
        n = self.cnt[eng]
        ins.then_inc(self.sem[eng], 1)
        self.n_ins += 1
        tok = Tok("c", eng, n, dict(self.know[eng]))
        self._commit(tok, reads, writes, pwrites)
        return tok

    def dma(self, q, out, in_, reads=(), writes=(), pwrites=(), **kw):
        reads = self._bufs(reads)
        writes = self._bufs(writes)
        pwrites = self._bufs(pwrites)
        self._deps(q, reads, writes, pwrites)
        si = self.dnext
        self.dnext = (self.dnext + 1) % N_DMA_SEMS
        prev = self.dlast[si]
        if prev is not None:
            self._need(q, prev)
        self.dcnt[si] += 16
        for (sm, vv) in self.pend[q]:
            self.E[q].wait_ge(sm, vv)
        self.pend[q] = []
        ins = self.E[q].dma_start(out=out, in_=in_, **kw)
        ins.then_inc(self.dsem[si], 16)
        self.n_ins += 1
        tok = Tok("d", si, self.dcnt[si], dict(self.know[q]))
        self.dlast[si] = tok
        self._commit(tok, reads, writes, pwrites)
        return tok

    def finish(self, toks, eng="sp"):
        for t in toks:
            self._need(eng, t)
D = 1024
NQKV = 512
FFN = 2816
RWC = 1920
W_IN_COLS = 8080
PV = {}
_o = 0
for _n, _c in [("mu", 15), ("w0", 8), ("a0", 8), ("kk", 4), ("ka", 4), ("rk", 4), ("lng", 4), ("lnb", 4),
               ("cw", 24), ("cb", 8), ("sd", 4), ("sng", 4), ("dag", 1), ("l1g", 8), ("l1b", 8), ("l2g", 8),
               ("l2b", 8), ("adab", 48), ("dal", 4)]:
    PV[_n] = _o
    _o += _c
NPV = _o
CO = {"ident": 0, "msl": 128, "msu": 256, "mil": 384, "miu": 512, "bo64": 640, "ones": 768}
NCONST = 896


def lam_init_of(layer):
    import math
    return 0.8 - 0.6 * math.exp(-0.3 * layer)


def build(SEQ, CTX, DEPTH, debug=False, stop_after=None):
    T = CTX + SEQ
    NT = T // 128
    NCT = CTX // 128
    alpha = (2.0 * DEPTH) ** 0.25
    nc = bass.Bass("TRN2", target_bir_lowering=False)
    ikind = "ExternalOutput" if debug else "Internal"

    def din(name, shape, dt=F32):
        return nc.dram_tensor(name, list(shape), dt, kind="ExternalInput").ap()

    def dscr(name, shape, dt=F32):
        return nc.dram_tensor(name, list(shape), dt, kind=ikind).ap()

    x_in = din("x", [SEQ, D])
    ctx_in = din("ctx", [CTX, D])
    cc_in = din("cc", [128, 16])
    ada_w = din("ada_w", [DEPTH, D, 6 * D])
    w_in = din("w_in", [DEPTH, D, W_IN_COLS])
    w_perm = din("w_perm", [DEPTH, D, 1024])
    pvec_in = din("pvec", [DEPTH, 128, NPV])
    rowp_in = din("rowp", [DEPTH, 128, 32])
    consts_in = din("consts", [128, NCONST])
    rope_in = din("rope", [2, 128, T])
    lvmask_in = din("lvmask", [128, 14, 128])
    rw_w2 = din("rw_w2", [DEPTH, 128, 512])
    rw_a2 = din("rw_a2", [DEPTH, 128, 512])
    rw_g2 = din("rw_g2", [DEPTH, 128, 512])
    p_attn = din("p_attn", [DEPTH, 512, D])
    p_rwkv = din("p_rwkv", [DEPTH, 512, D])
    p_ssm = din("p_ssm", [DEPTH, 512, D])
    w_out = din("w_out", [DEPTH, D, D])
    ffn_w1 = din("ffn_w1", [DEPTH, D, FFN])
    ffn_w3 = din("ffn_w3", [DEPTH, D, FFN])
    ffn_w2 = din("ffn_w2", [DEPTH, FFN, D])
    y_out = nc.dram_tensor("y", [SEQ, D], F32, kind="ExternalOutput").ap()

    XT = dscr("XT", [D, T])
    QR = dscr("QR", [512, T], BF16)
    KR = dscr("KR", [512, T], BF16)
    VA = dscr("VA", [T, 512], BF16)
    RW = dscr("RW", [RWC, T])
    ZS = dscr("ZS", [512, T])
    XBC = dscr("XBC", [1024, T])
    DTS = dscr("DTS", [T, 16])
    GT = dscr("GT", [3072, T])
    YA = dscr("YA", [512, T], BF16)
    YR = dscr("YR", [512, T], BF16)
    YS = dscr("YS", [512, T], BF16)

    st = ExitStack()
    P = Prog(nc, st)
    V, A, G, PE = nc.vector, nc.scalar, nc.gpsimd, nc.tensor

    def blocks(a, b, n=512):
        out = []
        s = a
        while s < b:
            m = min(n, b - s)
            out.append((s, m))
            s += m
        return out
    tblocks = blocks(0, CTX) + blocks(CTX, T)
    streams = [(0, CTX), (CTX, T)]

    consts = P.sb("consts", [128, NCONST])
    cbf = P.sb("cbf", [128, NCONST], BF16)
    pv = P.sb("pv", [128, NPV])
    rowp = P.sb("rowp", [128, 32])
    mod = P.sb("mod", [128, 96])
    onep = P.sb("onep", [128, 96])
    PS = [P.ps("ps%d" % i, [128, 512]) for i in range(8)]
    psi = [0]

    def nps():
        t = PS[psi[0] % 8]
        psi[0] += 1
        return t

    ident = consts[:, CO["ident"]:CO["ident"] + 128]
    onesF = consts[:, CO["ones"]:CO["ones"] + 128]
    bo64F = consts[:, CO["bo64"]:CO["bo64"] + 128]
    onesB = cbf[:, CO["ones"]:CO["ones"] + 128]

    P.dma("sp", consts[:], consts_in, writes=[consts])
    P.op("dve", lambda: V.tensor_copy(out=cbf[:], in_=consts[:]), reads=[consts], writes=[cbf])

    XTb = [P.buf("XT%d" % c) for c in range(8)]
    XTv = XT.rearrange("(c p) t -> p c t", p=128)

    scX = P.scope()
    scX.__enter__()
    xin = [P.sb("xin%d" % i, [128, D]) for i in range(2)]
    xo = [P.sb("xo%d" % i, [128, 8, 128]) for i in range(2)]
    for i in range(NT):
        src = ctx_in[i * 128:(i + 1) * 128, :] if i < NCT else x_in[(i - NCT) * 128:(i - NCT + 1) * 128, :]
        xi, xoo = xin[i % 2], xo[i % 2]
        P.dma("sp", xi[:], src, writes=[xi])
        for half in range(2):
            ps = nps()
            for q in range(4):
                c = half * 4 + q
                P.op("pe", lambda c=c, q=q, ps=ps: PE.transpose(out=ps[:, q * 128:(q + 1) * 128], in_=xi[:, c * 128:(c + 1) * 128], identity=ident),
                     reads=[xi, consts], pwrites=[ps] if q else [], writes=[] if q else [ps])
            eng = "act" if half else "dve"
            if half:
                P.op("act", lambda ps=ps: A.copy(out=xoo[:, 4:8, :], in_=ps[:].rearrange("p (q t) -> p q t", q=4)), reads=[ps], pwrites=[xoo])
            else:
                P.op("dve", lambda ps=ps: V.tensor_copy(out=xoo[:, 0:4, :], in_=ps[:].rearrange("p (q t) -> p q t", q=4)), reads=[ps], writes=[xoo])
        P.dma("sp", XTv[:, :, i * 128:(i + 1) * 128], xoo[:], reads=[xoo], pwrites=XTb)
    scX.__exit__(None, None, None)
    def acopy(out, in_):
        return A.activation(out=out, in_=in_, func=AF.Copy)

    scc = P.sb("scc", [128, 16])
    epst = P.sb("epst", [128, 4])
    P.op("dve", lambda: V.memset(epst[:, 0:1], 1e-5), writes=[epst])
    P.op("dve", lambda: V.memset(epst[:, 1:2], 64e-5), pwrites=[epst])
    P.op("dve", lambda: V.memset(epst[:, 2:3], 0.0), pwrites=[epst])
    epsc = epst[:, 0:1]
    modL = [P.sb("modL%d" % l, [128, 96]) for l in range(DEPTH)]
    pvx = P.sb("pvx", [128, 32])
    DB = {n: P.buf(n) for n in ("QR", "KR", "VA", "RW", "ZS", "XBC", "DTS", "GT", "YA", "YR", "YS")}
    scM = P.scope()
    scM.__enter__()
    P.dma("sp", scc[:], cc_in, writes=[scc])
    P.op("act", lambda: A.activation(out=scc[:], in_=scc[:], func=AF.Silu), reads=[scc], writes=[scc])
    awb = [P.sb("awb%d" % i, [128, 8, 512]) for i in range(2)]
    for l in range(DEPTH):
        P.dma("sp", pv[:], pvec_in[l], writes=[pv])
        psm = nps()
        for blk in range(12):
            aw = awb[blk % 2]
            P.dma("sp", aw[:], ada_w[l, :, blk * 512:(blk + 1) * 512].rearrange("(k p) n -> p k n", p=128), writes=[aw])
            for jj in range(4):
                j = blk * 4 + jj
                for k in range(8):
                    P.op("pe", lambda jj=jj, j=j, k=k, aw=aw: PE.matmul(psm[:, j * 2:j * 2 + 2], lhsT=aw[:, k, jj * 128:(jj + 1) * 128],
                                                                     rhs=scc[:, k * 2:k * 2 + 2], start=(k == 0), stop=(k == 7)),
                         reads=[aw, scc], writes=[psm] if (j == 0 and k == 0) else [], pwrites=[] if (j == 0 and k == 0) else [psm])
        ml = modL[l]
        for n in range(2):
            P.op("dve", lambda n=n, ml=ml: V.tensor_tensor(out=ml[:].rearrange("p (j n) -> p j n", n=2)[:, :, n],
                                                        in0=psm[:, 0:96].rearrange("p (j n) -> p j n", n=2)[:, :, n],
                                                        in1=pv[:, PV["adab"]:PV["adab"] + 48], op=ALU.add),
                 reads=[psm, pv], pwrites=[ml])

    scM.__exit__(None, None, None)
    cnt = {"w": 0, "row": 0, "obf": 0, "ev": 0}

    def load_cast(dst, dview, src, stg):
        sg, sview = stg
        cnt["w"] += 1
        q = "sp"
        P.dma(q, sview, src, writes=[sg])
        P.op("pool", lambda: G.tensor_copy(out=dview, in_=sview), reads=[sg], writes=[dst])

    def fm_project(wsrc, row, func, Hsrc, wblk):
        wb = wblk[cnt["w"] % 4]
        sg = wblk[4 + cnt["w"] % 2]
        load_cast(wb, wb[:], wsrc.rearrange("(k p) n -> p k n", p=128), (sg, sg[:]))
        first = True
        for (s, n) in tblocks:
            ps = nps()
            for k in range(8):
                P.op("pe", lambda k=k, s=s, n=n, ps=ps: PE.matmul(ps[:, :n], lhsT=wb[:, k, :], rhs=Hsrc[:, k, s:s + n], start=(k == 0), stop=(k == 7)),
                     reads=[wb, Hsrc], writes=[ps] if k == 0 else [], pwrites=[] if k == 0 else [ps])
            use_act = (func != AF.Copy) or (cnt["ev"] % 2 == 0)
            cnt["ev"] += 1
            if use_act:
                P.op("act", lambda s=s, n=n, ps=ps: A.activation(out=row[:, s:s + n], in_=ps[:, :n], func=func),
                     reads=[ps], writes=[row] if first else [], pwrites=[] if first else [row])
            else:
                P.op("dve", lambda s=s, n=n, ps=ps: V.tensor_copy(out=row[:, s:s + n], in_=ps[:, :n]),
                     reads=[ps], writes=[row] if first else [], pwrites=[] if first else [row])
            first = False

    def layer_front(l):
        ml = modL[l]
        HT = P.sb("HT", [128, 8, T], BF16)
        CTt = P.sb("CTt", [128, T])
        STt = P.sb("STt", [128, T])
        P.dma("sp", CTt[:], rope_in[0], writes=[CTt])
        P.dma("sp", STt[:], rope_in[1], writes=[STt])
        rows = [P.sb("row%d" % i, [128, T]) for i in range(3)]
        obf = [P.sb("obf%d" % i, [128, T], BF16) for i in range(1)]
        wblk = [P.sb("wblk%d" % i, [128, 8, 128], BF16) for i in range(4)] + [P.sb("wstg%d" % i, [128, 8, 128]) for i in range(2)]
        wtm = P.sb("wtm", [128, 8, 512], BF16)
        wtms = P.sb("wtms", [128, 4, 512])
        dtws = P.sb("dtws", [128, 8, 16])
        vst = [P.sb("vst0", [128, 512], BF16), P.sb("vst1", [128, 512], BF16)]
        dtw = P.sb("dtw", [128, 8, 16], BF16)
        dst_ = [P.sb("dst0", [128, 16]), P.sb("dst1", [128, 16])]
        P.dma("sp", pv[:], pvec_in[l], writes=[pv])
        P.dma("sp", rowp[:], rowp_in[l], writes=[rowp])
        P.op("dve", lambda: V.tensor_scalar(out=onep[:], in0=ml[:], scalar1=1.0, scalar2=None, op0=ALU.add), reads=[ml], writes=[onep])
        P.op("dve", lambda: V.tensor_scalar(out=pvx[:, 0:15], in0=pv[:, PV["mu"]:PV["mu"] + 15], scalar1=-1.0, scalar2=1.0, op0=ALU.mult, op1=ALU.add),
             reads=[pv], writes=[pvx])
        P.op("dve", lambda: V.tensor_scalar(out=pvx[:, 15:30], in0=pv[:, PV["mu"]:PV["mu"] + 15], scalar1=0.5, scalar2=None, op0=ALU.mult),
             reads=[pv], pwrites=[pvx])
        for c in range(8):
            xr = rows[c % 3]
            P.dma("sp", xr[:], XT[c * 128:(c + 1) * 128, :], reads=[XTb[c]], writes=[xr])
            for n_, (a, b) in enumerate(((CTX, T), (0, CTX))):
                P.op("act", lambda c=c, n_=n_, a=a, b=b, xr=xr: A.activation(out=HT[:, c, a:b], in_=xr[:, a:b], func=AF.Identity,
                                                                          bias=ml[:, c * 2 + n_:c * 2 + n_ + 1],
                                                                          scale=onep[:, (8 + c) * 2 + n_:(8 + c) * 2 + n_ + 1]),
                     reads=[xr, ml, onep], pwrites=[HT] if (c or n_) else [], writes=[] if (c or n_) else [HT])
        wl = w_in[l]
        wp = w_perm[l]
        ri = [0]

        def nrow():
            t = rows[ri[0] % 3]
            ri[0] += 1
            return t

        def nobf():
            t = obf[0]
            cnt["obf"] += 1
            return t
        for which, (c0, dst) in enumerate(((0, QR), (512, KR))):
            for h in range(4):
                ra, rb = nrow(), nrow()
                fm_project(wl[:, c0 + h * 128:c0 + (h + 1) * 128], ra, AF.Copy, HT, wblk)
                fm_project(wp[:, which * 512 + h * 128:which * 512 + (h + 1) * 128], rb, AF.Copy, HT, wblk)
                ob = nobf()
                P.op("dve", lambda ra=ra: V.tensor_tensor(out=ra[:], in0=ra[:], in1=CTt[:], op=ALU.mult), reads=[ra, CTt], writes=[ra])
                P.op("pool", lambda rb=rb: G.tensor_tensor(out=rb[:], in0=rb[:], in1=STt[:], op=ALU.mult), reads=[rb, STt], writes=[rb])
                P.op("dve", lambda ra=ra, rb=rb, ob=ob: V.tensor_tensor(out=ob[:], in0=ra[:], in1=rb[:], op=ALU.add), reads=[ra, rb], writes=[ob])
                P.dma("sp", dst[h * 128:(h + 1) * 128, :], ob[:], reads=[ob], pwrites=[DB["KR" if which else "QR"]])
        for kh in range(2):
            load_cast(wtm, wtm[:, kh * 4:(kh + 1) * 4, :], wl[kh * 512:(kh + 1) * 512, 1024:1536].rearrange("(k p) n -> p k n", p=128), (wtms, wtms[:]))
        load_cast(dtw, dtw[:], wl[:, 4992:5008].rearrange("(k p) n -> p k n", p=128), (dtws, dtws[:]))
        for i in range(NT):
            ps = nps()
            for k in range(8):
                P.op("pe", lambda k=k, i=i, ps=ps: PE.matmul(ps[:, :], lhsT=HT[:, k, i * 128:(i + 1) * 128], rhs=wtm[:, k, :], start=(k == 0), stop=(k == 7)),
                     reads=[wtm, HT], writes=[ps] if k == 0 else [], pwrites=[] if k == 0 else [ps])
            vs = vst[i % 2]
            P.op("act", lambda ps=ps, vs=vs: acopy(vs[:], ps[:]), reads=[ps], writes=[vs])
            P.dma("sp", VA[i * 128:(i + 1) * 128, :], vs[:], reads=[vs], pwrites=[DB["VA"]])
            ps2 = nps()
            for k in range(8):
                P.op("pe", lambda k=k, i=i, ps2=ps2: PE.matmul(ps2[:, 0:16], lhsT=HT[:, k, i * 128:(i + 1) * 128], rhs=dtw[:, k, :], start=(k == 0), stop=(k == 7)),
                     reads=[dtw, HT], writes=[ps2] if k == 0 else [], pwrites=[] if k == 0 else [ps2])
            ds = dst_[i % 2]
            P.op("dve", lambda ps2=ps2, ds=ds: V.tensor_tensor(out=ds[:], in0=ps2[:, 0:16], in1=rowp[:, 0:16], op=ALU.add), reads=[ps2, rowp], writes=[ds])
            P.op("act", lambda ds=ds: A.activation(out=ds[:], in_=ds[:], func=AF.Exp), reads=[ds], writes=[ds])
            P.op("act", lambda ds=ds: A.activation(out=ds[:], in_=ds[:], func=AF.Ln, bias=1.0), reads=[ds], writes=[ds])
            P.dma("sp", DTS[i * 128:(i + 1) * 128, :], ds[:], reads=[ds], pwrites=[DB["DTS"]])
        for j in range(15):
            ra = nrow()
            sb_ = nrow()
            fm_project(wl[:, 1536 + j * 128:1536 + (j + 1) * 128], ra, AF.Copy, HT, wblk)
            for si, (a, b) in enumerate(streams):
                P.op("pool", lambda a=a, b=b, ra=ra, sb_=sb_: G.tensor_tensor(out=sb_[:, a + 1:b - 1], in0=ra[:, a:b - 2], in1=ra[:, a + 2:b], op=ALU.add),
                     reads=[ra], writes=[sb_] if si == 0 else [], pwrites=[] if si == 0 else [sb_])
                P.op("pool", lambda a=a, b=b, ra=ra, sb_=sb_: G.tensor_copy(out=sb_[:, a:a + 1], in_=ra[:, a + 1:a + 2]), reads=[ra], pwrites=[sb_])
                P.op("pool", lambda a=a, b=b, ra=ra, sb_=sb_: G.tensor_copy(out=sb_[:, b - 1:b], in_=ra[:, b - 2:b - 1]), reads=[ra], pwrites=[sb_])
            P.op("dve", lambda j=j, ra=ra: V.tensor_scalar(out=ra[:], in0=ra[:], scalar1=pvx[:, j:j + 1], scalar2=None, op0=ALU.mult), reads=[ra, pvx], writes=[ra])
            P.op("dve", lambda j=j, ra=ra, sb_=sb_: V.scalar_tensor_tensor(out=ra[:], in0=sb_[:], scalar=pvx[:, 15 + j:16 + j], in1=ra[:], op0=ALU.mult, op1=ALU.add),
                 reads=[ra, sb_, pvx], writes=[ra])
            P.dma("sp", RW[j * 128:(j + 1) * 128, :], ra[:], reads=[ra], pwrites=[DB["RW"]])
        for j in range(4):
            ra = nrow()
            fm_project(wl[:, 3456 + j * 128:3456 + (j + 1) * 128], ra, AF.Silu, HT, wblk)
            P.dma("sp", ZS[j * 128:(j + 1) * 128, :], ra[:], reads=[ra], pwrites=[DB["ZS"]])
        for j in range(8):
            ra = nrow()
            o_ = nrow()
            fm_project(wl[:, 3968 + j * 128:3968 + (j + 1) * 128], ra, AF.Copy, HT, wblk)
            cw = PV["cw"]
            P.op("dve", lambda j=j, ra=ra, o_=o_: V.tensor_scalar(out=o_[:], in0=ra[:], scalar1=pv[:, cw + 8 + j:cw + 9 + j], scalar2=pv[:, PV["cb"] + j:PV["cb"] + j + 1],
                                                               op0=ALU.mult, op1=ALU.add), reads=[ra, pv], writes=[o_])
            for (a, b) in streams:
                P.op("dve", lambda j=j, a=a, b=b, ra=ra, o_=o_: V.scalar_tensor_tensor(out=o_[:, a + 1:b], in0=ra[:, a:b - 1], scalar=pv[:, cw + j:cw + j + 1], in1=o_[:, a + 1:b],
                                                                                    op0=ALU.mult, op1=ALU.add), reads=[ra, pv, o_], writes=[o_])
                P.op("dve", lambda j=j, a=a, b=b, ra=ra, o_=o_: V.scalar_tensor_tensor(out=o_[:, a:b - 1], in0=ra[:, a + 1:b], scalar=pv[:, cw + 16 + j:cw + 17 + j], in1=o_[:, a:b - 1],
                                                                                    op0=ALU.mult, op1=ALU.add), reads=[ra, pv, o_], writes=[o_])
            P.op("act", lambda o_=o_: A.activation(out=o_[:], in_=o_[:], func=AF.Silu), reads=[o_], writes=[o_])
            P.dma("sp", XBC[j * 128:(j + 1) * 128, :], o_[:], reads=[o_], pwrites=[DB["XBC"]])
        for j in range(24):
            ra = nrow()
            fm_project(wl[:, 5008 + j * 128:5008 + (j + 1) * 128], ra, AF.Sigmoid, HT, wblk)
            P.dma("sp", GT[j * 128:(j + 1) * 128, :], ra[:], reads=[ra], pwrites=[DB["GT"]])
    def layer_attn(l):
        lam_init = lam_init_of(l)
        dal = PV["dal"]
        lam = P.sb("lam", [128, 4])
        prod = P.sb("prod", [128, 2])
        P.op("dve", lambda: V.tensor_tensor(out=prod[0:64, 0:1], in0=pv[0:64, dal:dal + 1], in1=pv[0:64, dal + 1:dal + 2], op=ALU.mult), reads=[pv], writes=[prod])
        P.op("dve", lambda: V.tensor_tensor(out=prod[0:64, 1:2], in0=pv[0:64, dal + 2:dal + 3], in1=pv[0:64, dal + 3:dal + 4], op=ALU.mult), reads=[pv], pwrites=[prod])
        psl = PS[0]
        P.op("pe", lambda: PE.matmul(psl[:, 0:2], lhsT=consts[0:64, CO["ones"]:CO["ones"] + 128], rhs=prod[0:64, 0:2], start=True, stop=True), reads=[consts, prod], writes=[psl])
        P.op("act", lambda: A.activation(out=lam[:, 0:2], in_=psl[:, 0:2], func=AF.Exp), reads=[psl], writes=[lam])
        P.op("dve", lambda: V.scalar_tensor_tensor(out=lam[:, 2:3], in0=lam[:, 1:2], scalar=-lam_init, in1=lam[:, 0:1], op0=ALU.add, op1=ALU.subtract), reads=[lam], pwrites=[lam])
        neglam = lam[:, 2:3]
        KRh = [P.sb("KRh%d" % i, [128, T], BF16) for i in range(2)]
        QRh = [P.sb("QRh%d" % i, [128, T], BF16) for i in range(2)]
        Vh = [P.sb("Vh%d" % i, [128, NT, 128], BF16) for i in range(2)]
        ee = [[P.sb("e%d_%d" % (m, i), [128, 512], BF16) for i in range(2)] for m in range(2)]
        r0 = P.sb("r0", [128, 512])
        r1 = P.sb("r1", [128, 512])
        t0 = P.sb("t0", [128, 512])
        t1 = P.sb("t1", [128, 512])
        sq = P.sb("sq", [128, 512])
        yo = [P.sb("yo%d" % i, [128, 512], BF16) for i in range(2)]
        acc = [PS[4], PS[5]]
        zz = [PS[6], PS[7]]
        VAv = VA.rearrange("(i p) e -> p i e", p=128)
        qbs = [(s, n, NCT) for (s, n) in blocks(0, CTX)] + [(s, n, NT) for (s, n) in blocks(CTX, T)]
        step = 0
        for h in range(4):
            kr, qr, vh = KRh[h % 2], QRh[h % 2], Vh[h % 2]
            P.dma("sp", kr[:], KR[h * 128:(h + 1) * 128, :], reads=[DB["KR"]], writes=[kr])
            P.dma("sp", qr[:], QR[h * 128:(h + 1) * 128, :], reads=[DB["QR"]], writes=[qr])
            P.dma("sp", vh[:], VAv[:, :, h * 128:(h + 1) * 128], reads=[DB["VA"]], writes=[vh])
            for bi, (s, n, nk) in enumerate(qbs):
                def emit_s(kt, pr):
                    pss = [PS[pr * 2], PS[pr * 2 + 1]]
                    es = [ee[0][pr], ee[1][pr]]
                    for m in range(2):
                        P.op("pe", lambda m=m: PE.matmul(pss[m][:, :n], lhsT=kr[m * 64:(m + 1) * 64, kt * 128:(kt + 1) * 128],
                                                        rhs=qr[m * 64:(m + 1) * 64, s:s + n], start=True, stop=True),
                             reads=[kr, qr], writes=[pss[m]])
                    for m in range(2):
                        P.op("act", lambda m=m: A.activation(out=es[m][:, :n], in_=pss[m][:, :n], func=AF.Exp, scale=0.125),
                             reads=[pss[m]], writes=[es[m]])

                def emit_av(kt, pr):
                    es = [ee[0][pr], ee[1][pr]]
                    first = (kt == 0)
                    for m in range(2):
                        P.op("pe", lambda m=m: PE.matmul(acc[m][:, :n], lhsT=vh[:, kt, :], rhs=es[m][:, :n], start=(kt == 0), stop=(kt == nk - 1)),
                             reads=[vh, es[m]], writes=[acc[m]] if first else [], pwrites=[] if first else [acc[m]])
                        P.op("pe", lambda m=m: PE.matmul(zz[m][:, :n], lhsT=onesB, rhs=es[m][:, :n], start=(kt == 0), stop=(kt == nk - 1)),
                             reads=[cbf, es[m]], writes=[zz[m]] if first else [], pwrites=[] if first else [zz[m]])

                emit_s(0, step % 2)
                for kt in range(nk):
                    pr = step % 2
                    step += 1
                    if kt + 1 < nk:
                        emit_s(kt + 1, step % 2)
                    emit_av(kt, pr)
                P.op("dve", lambda n=n: V.reciprocal(out=r0[:, :n], in_=zz[0][:, :n]), reads=[zz[0]], writes=[r0])
                P.op("dve", lambda n=n: V.tensor_tensor(out=t0[:, :n], in0=acc[0][:, :n], in1=r0[:, :n], op=ALU.mult), reads=[acc[0], r0], writes=[t0])
                P.op("dve", lambda n=n: V.reciprocal(out=r1[:, :n], in_=zz[1][:, :n]), reads=[zz[1]], writes=[r1])
                P.op("dve", lambda n=n: V.tensor_tensor(out=t1[:, :n], in0=acc[1][:, :n], in1=r1[:, :n], op=ALU.mult), reads=[acc[1], r1], writes=[t1])
                P.op("dve", lambda n=n: V.scalar_tensor_tensor(out=t0[:, :n], in0=t1[:, :n], scalar=neglam, in1=t0[:, :n], op0=ALU.mult, op1=ALU.add),
                     reads=[t0, t1, lam], writes=[t0])
                P.op("act", lambda n=n: A.activation(out=sq[:, :n], in_=t0[:, :n], func=AF.Square), reads=[t0], writes=[sq])
                pst = PS[(step % 2) * 2]
                P.op("pe", lambda n=n, pst=pst: PE.matmul(pst[:, :n], lhsT=onesF, rhs=sq[:, :n], start=True, stop=True), reads=[consts, sq], writes=[pst])
                P.op("act", lambda n=n, pst=pst: A.activation(out=r0[:, :n], in_=pst[:, :n], func=AF.Sqrt, scale=1.0 / 128, bias=epsc), reads=[pst, epst], writes=[r0])
                P.op("dve", lambda n=n: V.reciprocal(out=r0[:, :n], in_=r0[:, :n]), reads=[r0], writes=[r0])
                P.op("dve", lambda n=n: V.tensor_tensor(out=t0[:, :n], in0=t0[:, :n], in1=r0[:, :n], op=ALU.mult), reads=[t0, r0], writes=[t0])
                yy = yo[bi % 2]
                P.op("dve", lambda n=n, yy=yy: V.tensor_scalar(out=yy[:, :n], in0=t0[:, :n], scalar1=pv[:, PV["dag"]:PV["dag"] + 1], scalar2=1.0 - lam_init,
                                                            op0=ALU.mult, op1=ALU.mult), reads=[t0, pv], writes=[yy])
                P.dma("sp", YA[h * 128:(h + 1) * 128, s:s + n], yy[:, :n], reads=[yy], pwrites=[DB["YA"]])
    _RWSTOP = globals().get("RWSTOP")
    BG = dscr("BG", [512, T])
    GG = dscr("GG", [512, T])
    DB["BG"] = P.buf("BG")
    DB["GG"] = P.buf("GG")

    def layer_rwkv(l):
        sgf = P.sb("sgf", [128, 512])
        w2b = P.sb("w2b", [128, 512], BF16)
        a2b = P.sb("a2b", [128, 512], BF16)
        g2b = P.sb("g2b", [128, 512], BF16)
        for dst_t, src in ((w2b, rw_w2[l]), (a2b, rw_a2[l]), (g2b, rw_g2[l])):
            P.dma("sp", sgf[:], src, writes=[sgf])
            P.op("pool", lambda dst_t=dst_t: G.tensor_copy(out=dst_t[:], in_=sgf[:]), reads=[sgf], writes=[dst_t])
        omka = P.sb("omka", [128, 4])
        P.op("dve", lambda: V.tensor_scalar(out=omka[:], in0=pv[:, PV["ka"]:PV["ka"] + 4], scalar1=-1.0, scalar2=1.0, op0=ALU.mult, op1=ALU.add), reads=[pv], writes=[omka])
        mk = {}
        for nm in ("msl", "msu", "mil", "miu", "ident", "bo64"):
            mt = P.sb("mk_" + nm, [128, 512])
            for q in range(4):
                P.op("pool", lambda q=q, mt=mt, nm=nm: G.tensor_copy(out=mt[:, q * 128:(q + 1) * 128], in_=consts[:, CO[nm]:CO[nm] + 128]), reads=[consts], pwrites=[mt])
            mk[nm] = mt
        lvs = P.sb("lvs", [128, 14, 128])
        lvm = P.sb("lvm", [128, 14, 512], BF16)
        P.dma("sp", lvs[:], lvmask_in, writes=[lvs])
        for q in range(4):
            P.op("pool", lambda q=q: G.tensor_copy(out=lvm[:, :, q * 128:(q + 1) * 128], in_=lvs[:]), reads=[lvs], pwrites=[lvm])
        Rd = P.sb("Rd", [128, T], BF16)
        Bi = P.sb("Bi", [128, T], BF16)
        Ki = P.sb("Ki", [128, T], BF16)
        KKd = P.sb("KKd", [128, T], BF16)
        BiT = P.sb("BiT", [128, NT, 128], BF16)
        KiT = P.sb("KiT", [128, NT, 128], BF16)
        VZ = P.sb("VZ", [128, NT, 2, 128], BF16)
        GCt = P.sb("GCt", [128, NT])
        YT = P.sb("YT", [128, T])
        TT = P.sb("TT", [128, NT, 256], BF16)
        tn = ["r32", "k32", "v32", "wd32", "ad32", "gd32", "kk32", "t32", "rn32", "p32", "a32", "kd32", "kb32", "b32", "pre", "arr", "ex", "x32"]
        tm = {n_: P.sb(n_, [128, 512]) for n_ in tn}
        twb = P.sb("twb", [128, 512], BF16)
        adb = P.sb("adb", [128, 512], BF16)
        sgb = P.sb("sgb", [128, 512], BF16)
        Nb = P.sb("Nb", [128, 512], BF16)
        NTb = P.sb("NTb", [128, 512], BF16)
        Ok = P.sb("Ok", [128, 512], BF16)
        OTk = P.sb("OTk", [128, 512], BF16)
        Wb = P.sb("Wb", [128, 512], BF16)
        W2b = P.sb("W2b", [128, 512], BF16)
        Tm = P.sb("Tm", [128, 512], BF16)
        TTm = P.sb("TTm", [128, 512], BF16)
        AKL = [P.sb("AKL%d" % e, [128, 384], BF16) for e in range(2)]
        mcat = [P.sb("mcat%d" % i, [128, 384]) for i in range(2)]
        RZ = P.sb("RZ", [128, 2, 128], BF16)
        UZ = P.sb("UZ", [128, 2, 128], BF16)
        Ub = P.sb("Ub", [128, 128], BF16)
        M32 = P.sb("M32", [128, 128])
        Mt32 = P.sb("Mt32", [128, 128])
        Mb = P.sb("Mb", [128, 128], BF16)
        yob = [P.sb("yrb%d" % i, [128, 512], BF16) for i in range(2)]

        def dv(fn, reads, writes=(), pwrites=()):
            return P.op("dve", fn, reads=reads, writes=writes, pwrites=pwrites)

        def ac(fn, reads, writes=(), pwrites=()):
            return P.op("act", fn, reads=reads, writes=writes, pwrites=pwrites)

        def pe(fn, reads, writes=(), pwrites=()):
            return P.op("pe", fn, reads=reads, writes=writes, pwrites=pwrites)

        def po(fn, reads, writes=(), pwrites=()):
            return P.op("pool", fn, reads=reads, writes=writes, pwrites=pwrites)

        for i_, (a_, b_) in enumerate((("msu", "miu"), ("msl", "mil"))):
            po(lambda i_=i_, a_=a_: G.tensor_copy(out=mcat[i_][:, 0:128], in_=consts[:, CO[a_]:CO[a_] + 128]), [consts], [], [mcat[i_]])
            po(lambda i_=i_, b_=b_: G.tensor_copy(out=mcat[i_][:, 128:256], in_=consts[:, CO[b_]:CO[b_] + 128]), [consts], [], [mcat[i_]])
            po(lambda i_=i_, b_=b_: G.tensor_copy(out=mcat[i_][:, 256:384], in_=consts[:, CO[b_]:CO[b_] + 128]), [consts], [], [mcat[i_]])
        po(lambda: G.memset(RZ[:], 0.0), [], [RZ])
        po(lambda: G.memset(UZ[:], 0.0), [], [UZ])
        po(lambda: G.memset(VZ[:], 0.0), [], [VZ])

        for hp in range(4):
            hc = slice(hp * 128, (hp + 1) * 128)
            for d in range(2):
                dh = slice(d * 64, (d + 1) * 64)
                sgc = -1.0 if d == 0 else 1.0
                for bi_, (s, n) in enumerate(tblocks):
                    nch = n // 128
                    t0_ = s // 128
                    r32, k32, v32, wd32, ad32, gd32 = (tm[x] for x in ("r32", "k32", "v32", "wd32", "ad32", "gd32"))
                    kk32, t32, rn32, p32, a32, kd32, kb32, b32, pre, arr, ex, x32 = (tm[x] for x in ("kk32", "t32", "rn32", "p32", "a32", "kd32", "kb32", "b32", "pre", "arr", "ex", "x32"))
                    lds = [(r32, hp * 128), (k32, 512 + hp * 128), (wd32, 1536), (ad32, 1664)]
                    if d == 0:
                        lds += [(v32, 1024 + hp * 128), (gd32, 1792)]
                    for ti, (dst_t, r0_) in enumerate(lds):
                        P.dma("sp", dst_t[:, :n], RW[r0_:r0_ + 128, s:s + n], reads=[DB["RW"]], writes=[dst_t])
                    ac(lambda: A.activation(out=twb[:, :n], in_=wd32[:, :n], func=AF.Tanh), [wd32], [twb])
                    ac(lambda: A.activation(out=adb[:, :n], in_=ad32[:, :n], func=AF.Copy), [ad32], [adb])
                    dv(lambda: V.tensor_scalar(out=kk32[:, :n], in0=k32[:, :n], scalar1=pv[:, PV["kk"] + hp:PV["kk"] + hp + 1], scalar2=None, op0=ALU.mult), [k32, pv], [kk32])
                    ac(lambda: A.activation(out=t32[:, :n], in_=kk32[:, :n], func=AF.Square), [kk32], [t32])
                    ps = nps()
                    pe(lambda ps=ps: PE.matmul(ps[:, :n], lhsT=bo64F, rhs=t32[:, :n], start=True, stop=True), [consts, t32], [ps])
                    dv(lambda ps=ps: V.tensor_scalar(out=rn32[:, :n], in0=ps[:, :n], scalar1=1e-24, scalar2=None, op0=ALU.max), [ps], [rn32])
                    ac(lambda: A.activation(out=rn32[:, :n], in_=rn32[:, :n], func=AF.Sqrt), [rn32], [rn32])
                    dv(lambda: V.reciprocal(out=rn32[:, :n], in_=rn32[:, :n]), [rn32], [rn32])
                    dv(lambda: V.tensor_tensor(out=kk32[:, :n], in0=kk32[:, :n], in1=rn32[:, :n], op=ALU.mult), [kk32, rn32], [kk32])

                    def a_and_kd(dd, a_out, kd_out):
                        ddh = slice(dd * 64, (dd + 1) * 64)
                        ps_ = nps()
                        pe(lambda: PE.matmul(ps_[:, :n], lhsT=a2b[ddh, hc], rhs=adb[ddh, :n], start=True, stop=True), [a2b, adb], [ps_])
                        ac(lambda: A.activation(out=a_out[:, :n], in_=ps_[:, :n], func=AF.Sigmoid, bias=pv[:, PV["a0"] + dd * 4 + hp:PV["a0"] + dd * 4 + hp + 1]), [ps_, pv], [a_out])
                        dv(lambda: V.tensor_scalar(out=t32[:, :n], in0=a_out[:, :n], scalar1=pv[:, PV["ka"] + hp:PV["ka"] + hp + 1], scalar2=omka[:, hp:hp + 1], op0=ALU.mult, op1=ALU.add),
                           [a_out, pv, omka], [t32])
                        dv(lambda: V.tensor_tensor(out=kd_out[:, :n], in0=t32[:, :n], in1=k32[:, :n], op=ALU.mult), [t32, k32], [kd_out])

                    if d == 0:
                        ac(lambda: A.activation(out=sgb[:, :n], in_=gd32[:, :n], func=AF.Sigmoid), [gd32], [sgb])
                        ps = nps()
                        pe(lambda ps=ps: PE.matmul(ps[:, :n], lhsT=g2b[:, hc], rhs=sgb[:, :n], start=True, stop=True), [g2b, sgb], [ps])
                        ac(lambda ps=ps: A.activation(out=x32[:, :n], in_=ps[:, :n], func=AF.Copy), [ps], [x32])
                        P.dma("sp", GG[hc, s:s + n], x32[:, :n], reads=[x32], pwrites=[DB["GG"]])
                        ps = nps()
                        for ci in range(nch):
                            pe(lambda ps=ps, ci=ci: PE.transpose(out=ps[:, ci * 128:(ci + 1) * 128], in_=v32[:, ci * 128:(ci + 1) * 128], identity=ident), [v32, consts],
                               [ps] if ci == 0 else [], [] if ci == 0 else [ps])
                        for e in range(2):
                            ac(lambda ps=ps, e=e: A.activation(out=VZ[:, t0_:t0_ + nch, e, e * 64:(e + 1) * 64],
                                                              in_=ps[:, :n].rearrange("p (c t) -> p c t", c=nch)[:, :, e * 64:(e + 1) * 64], func=AF.Copy), [ps], [], [VZ])
                        a_and_kd(1, a32, kb32)
                        a_and_kd(0, a32, kd32)
                        po(lambda: G.tensor_tensor(out=kb32[:, :n], in0=kb32[:, :n], in1=kd32[:, :n], op=ALU.add), [kd32, kb32], [kb32])
                        dv(lambda: V.tensor_tensor(out=x32[:, :n], in0=r32[:, :n], in1=kb32[:, :n], op=ALU.mult), [r32, kb32], [x32])
                        dv(lambda: V.tensor_scalar(out=x32[:, :n], in0=x32[:, :n], scalar1=pv[:, PV["rk"] + hp:PV["rk"] + hp + 1], scalar2=0.5, op0=ALU.mult, op1=ALU.mult), [x32, pv], [x32])
                        ps = nps()
                        pe(lambda ps=ps: PE.matmul(ps[:, :n], lhsT=bo64F, rhs=x32[:, :n], start=True, stop=True), [consts, x32], [ps])
                        dv(lambda ps=ps: V.tensor_tensor(out=x32[:, :n], in0=ps[:, :n], in1=v32[:, :n], op=ALU.mult), [ps, v32], [x32])
                        P.dma("sp", BG[hc, s:s + n], x32[:, :n], reads=[x32], pwrites=[DB["BG"]])
                    else:
                        a_and_kd(1, a32, kd32)
                    ps = nps()
                    pe(lambda ps=ps: PE.matmul(ps[:, :n], lhsT=w2b[dh, hc], rhs=twb[dh, :n], start=True, stop=True), [w2b, twb], [ps])
                    ac(lambda ps=ps: A.activation(out=p32[:, :n], in_=ps[:, :n], func=AF.Sigmoid, bias=pv[:, PV["w0"] + d * 4 + hp:PV["w0"] + d * 4 + hp + 1]), [ps, pv], [p32])
                    dv(lambda: V.tensor_scalar(out=p32[:, :n], in0=p32[:, :n], scalar1=0.6065306597126334, scalar2=None, op0=ALU.mult), [p32], [p32])
                    dv(lambda: V.tensor_tensor(out=b32[:, :n], in0=kk32[:, :n], in1=a32[:, :n], op=ALU.mult), [kk32, a32], [b32])
                    for ci in range(nch):
                        cs = slice(ci * 128, (ci + 1) * 128)
                        dv(lambda cs=cs: V.tensor_tensor_scan(out=pre[:, cs], data0=onesF, data1=p32[:, cs], initial=0.0, op0=ALU.mult, op1=ALU.add), [p32, consts],
                           [pre] if ci == 0 else [], [] if ci == 0 else [pre])
                    for ci in range(nch):
                        ac(lambda ci=ci: A.activation(out=GCt[:, t0_ + ci:t0_ + ci + 1], in_=pre[:, ci * 128 + 127:ci * 128 + 128], func=AF.Exp, scale=-1.0), [pre], [], [GCt])
                    if d == 0:
                        cur = pre
                    else:
                        for ci in range(nch):
                            cs = slice(ci * 128, (ci + 1) * 128)
                            dv(lambda cs=cs, ci=ci: V.scalar_tensor_tensor(out=arr[:, cs], in0=pre[:, cs], scalar=pre[:, ci * 128 + 127:ci * 128 + 128], in1=p32[:, cs],
                                                                         op0=ALU.subtract, op1=ALU.subtract), [pre, p32], [arr] if ci == 0 else [], [] if ci == 0 else [arr])
                        cur = arr
                    ac(lambda cur=cur: A.activation(out=ex[:, :n], in_=cur[:, :n], func=AF.Exp, scale=sgc), [cur], [ex])
                    dv(lambda: V.tensor_tensor(out=Rd[:, s:s + n], in0=r32[:, :n], in1=ex[:, :n], op=ALU.mult), [r32, ex], [] if bi_ else [Rd], [Rd] if bi_ else [])
                    ac(lambda cur=cur: A.activation(out=ex[:, :n], in_=cur[:, :n], func=AF.Exp, scale=-sgc), [cur], [ex])
                    for (src32, dstb, dstT) in ((b32, Bi, BiT), (kd32, Ki, KiT)):
                        dv(lambda src32=src32: V.tensor_tensor(out=t32[:, :n], in0=src32[:, :n], in1=ex[:, :n], op=ALU.mult), [src32, ex], [t32])
                        ac(lambda dstb=dstb: A.activation(out=dstb[:, s:s + n], in_=t32[:, :n], func=AF.Copy), [t32], [] if bi_ else [dstb], [dstb] if bi_ else [])
                        ps = nps()
                        for ci in range(nch):
                            pe(lambda ps=ps, ci=ci: PE.transpose(out=ps[:, ci * 128:(ci + 1) * 128], in_=t32[:, ci * 128:(ci + 1) * 128], identity=ident), [t32, consts],
                               [ps] if ci == 0 else [], [] if ci == 0 else [ps])
                        dv(lambda ps=ps, dstT=dstT: V.tensor_copy(out=dstT[:, t0_:t0_ + nch, :], in_=ps[:, :n].rearrange("p (c t) -> p c t", c=nch)), [ps], [], [dstT])
                    dv(lambda cur=cur: V.scalar_tensor_tensor(out=t32[:, :n], in0=p32[:, :n], scalar=sgc, in1=cur[:, :n], op0=ALU.mult, op1=ALU.add), [p32, cur], [t32])
                    ac(lambda: A.activation(out=ex[:, :n], in_=t32[:, :n], func=AF.Exp, scale=sgc), [t32], [ex])
                    dv(lambda: V.tensor_tensor(out=KKd[:, s:s + n], in0=kk32[:, :n], in1=ex[:, :n], op=ALU.mult), [kk32, ex], [] if bi_ else [KKd], [KKd] if bi_ else [])
                if _RWSTOP == 1:
                    return
                if d == 0:
                    order = list(range(NT))
                    mN, mT, mc = mk["msl"], mk["msu"], mcat[0]
                    lv_o, lv_ot = 0, 7
                else:
                    order = list(range(NCT - 1, -1, -1)) + list(range(NT - 1, NCT - 1, -1))
                    mN, mT, mc = mk["msu"], mk["msl"], mcat[1]
                    lv_o, lv_ot = 7, 0
                for g0 in range(0, NT, 2):
                    grp = list(range(g0, min(g0 + 2, NT)))
                    ng = len(grp)
                    W_ = ng * 128
                    bN = [nps(), nps()]
                    bT = [nps(), nps()]
                    for e in range(2):
                        ph = slice(e * 64, (e + 1) * 64)
                        for cj, c in enumerate(grp):
                            tc = slice(c * 128, (c + 1) * 128)
                            oc = slice(cj * 128, (cj + 1) * 128)
                            pe(lambda e=e, tc=tc, oc=oc, ph=ph: PE.matmul(bN[e][:, oc], lhsT=KKd[ph, tc], rhs=Bi[ph, tc], start=True, stop=True), [KKd, Bi],
                               [bN[e]] if cj == 0 else [], [] if cj == 0 else [bN[e]])
                            pe(lambda e=e, tc=tc, oc=oc, ph=ph: PE.matmul(bT[e][:, oc], lhsT=Bi[ph, tc], rhs=KKd[ph, tc], start=True, stop=True), [KKd, Bi],
                               [bT[e]] if cj == 0 else [], [] if cj == 0 else [bT[e]])
                    if ng < 2:
                        po(lambda: G.memset(Nb[:], 0.0), [], [Nb])
                        po(lambda: G.memset(NTb[:], 0.0), [], [NTb])
                    for e in range(2):
                        full = (e == 0 and ng == 2)
                        dv(lambda e=e: V.tensor_tensor(out=Nb[:, e * 256:e * 256 + W_], in0=bN[e][:, :W_], in1=mN[:, :W_], op=ALU.mult), [bN[e], mN], [Nb] if full else [], [] if full else [Nb])
                        dv(lambda e=e: V.tensor_tensor(out=NTb[:, e * 256:e * 256 + W_], in0=bT[e][:, :W_], in1=mT[:, :W_], op=ALU.mult), [bT[e], mT], [NTb] if full else [], [] if full else [NTb])
                    mats = [(e, cj) for e in range(2) for cj in range(ng)]
                    po(lambda: G.tensor_tensor(out=Ok[:], in0=Nb[:], in1=lvm[:, lv_o, :], op=ALU.mult), [Nb, lvm], [Ok])
                    po(lambda: G.tensor_tensor(out=OTk[:], in0=NTb[:], in1=lvm[:, lv_ot, :], op=ALU.mult), [NTb, lvm], [OTk])
                    po(lambda: G.tensor_tensor(out=Tm[:], in0=mk["ident"][:], in1=Ok[:], op=ALU.subtract), [mk["ident"], Ok], [Tm])
                    po(lambda: G.tensor_tensor(out=TTm[:], in0=mk["ident"][:], in1=OTk[:], op=ALU.subtract), [mk["ident"], OTk], [TTm])
                    for lev in range(1, 7):
                        last = (lev == 6)
                        po(lambda lev=lev: G.tensor_tensor(out=Ok[:], in0=Nb[:], in1=lvm[:, lv_o + lev, :], op=ALU.mult), [Nb, lvm], [Ok])
                        if not last:
                            po(lambda lev=lev: G.tensor_tensor(out=OTk[:], in0=NTb[:], in1=lvm[:, lv_ot + lev, :], op=ALU.mult), [NTb, lvm], [OTk])
                        psW2 = nps()
                        for qi, (e, cj) in enumerate(mats):
                            oc = slice(e * 256 + cj * 128, e * 256 + (cj + 1) * 128)
                            pe(lambda oc=oc, psW2=psW2: PE.matmul(psW2[:, oc], lhsT=Ok[:, oc], rhs=TTm[:, oc], start=True, stop=True), [Ok, TTm], [psW2] if qi == 0 else [], [] if qi == 0 else [psW2])
                        if not last:
                            psW = nps()
                            for qi, (e, cj) in enumerate(mats):
                                oc = slice(e * 256 + cj * 128, e * 256 + (cj + 1) * 128)
                                pe(lambda oc=oc, psW=psW: PE.matmul(psW[:, oc], lhsT=OTk[:, oc], rhs=Tm[:, oc], start=True, stop=True), [OTk, Tm], [psW] if qi == 0 else [], [] if qi == 0 else [psW])
                        ac(lambda psW2=psW2: A.activation(out=W2b[:], in_=psW2[:], func=AF.Copy), [psW2], [W2b])
                        if not last:
                            dv(lambda psW=psW: V.tensor_copy(out=Wb[:], in_=psW[:]), [psW], [Wb])
                        psX2 = nps()
                        for qi, (e, cj) in enumerate(mats):
                            oc = slice(e * 256 + cj * 128, e * 256 + (cj + 1) * 128)
                            pe(lambda oc=oc, psX2=psX2: PE.matmul(psX2[:, oc], lhsT=Tm[:, oc], rhs=W2b[:, oc], start=True, stop=True), [Tm, W2b], [psX2] if qi == 0 else [], [] if qi == 0 else [psX2])
                        if not last:
                            psX = nps()
                            for qi, (e, cj) in enumerate(mats):
                                oc = slice(e * 256 + cj * 128, e * 256 + (cj + 1) * 128)
                                pe(lambda oc=oc, psX=psX: PE.matmul(psX[:, oc], lhsT=TTm[:, oc], rhs=Wb[:, oc], start=True, stop=True), [TTm, Wb], [psX] if qi == 0 else [], [] if qi == 0 else [psX])
                            dv(lambda psX2=psX2: V.tensor_tensor(out=TTm[:], in0=TTm[:], in1=psX2[:], op=ALU.subtract), [psX2, TTm], [TTm])
                            dv(lambda psX=psX: V.tensor_tensor(out=Tm[:], in0=Tm[:], in1=psX[:], op=ALU.subtract), [psX, Tm], [Tm])
                        else:
                            for e in range(2):
                                dv(lambda e=e, psX2=psX2: V.tensor_tensor(out=TT[:, g0:g0 + ng, e * 128:(e + 1) * 128],
                                                                        in0=TTm[:, e * 256:e * 256 + W_].rearrange("p (c t) -> p c t", c=ng),
                                                                        in1=psX2[:, e * 256:e * 256 + W_].rearrange("p (c t) -> p c t", c=ng), op=ALU.subtract), [psX2, TTm], [], [TT])
                if _RWSTOP == 2:
                    return
                dv(lambda: V.memset(M32[:], 0.0), [], [M32])
                dv(lambda: V.memset(Mb[:], 0.0), [], [Mb])
                psA0, psA1, psR, psU, psY, psM = PS[0], PS[1], PS[2], PS[3], PS[4], PS[5]
                psAe = [psA0, psA1]
                for c in order:
                    tc = slice(c * 128, (c + 1) * 128)
                    for e in range(2):
                        ph = slice(e * 64, (e + 1) * 64)
                        pe(lambda e=e, ph=ph: PE.matmul(psAe[e][:, 0:128], lhsT=Ki[ph, tc], rhs=KKd[ph, tc], start=True, stop=True), [Ki, KKd], [psAe[e]])
                        pe(lambda e=e, ph=ph: PE.matmul(psAe[e][:, 128:256], lhsT=Bi[ph, tc], rhs=Rd[ph, tc], start=True, stop=True), [Bi, Rd], [], [psAe[e]])
                        pe(lambda e=e, ph=ph: PE.matmul(psAe[e][:, 256:384], lhsT=Ki[ph, tc], rhs=Rd[ph, tc], start=True, stop=True), [Ki, Rd], [], [psAe[e]])
                        dv(lambda e=e: V.tensor_tensor(out=AKL[e][:], in0=psAe[e][:, 0:384], in1=mc[:], op=ALU.mult), [psAe[e], mc], [AKL[e]])
                    if _RWSTOP == 31:
                        return
                    pe(lambda: PE.matmul(psR[:, 0:128], lhsT=KKd[:, tc], rhs=Mb[:], start=True, stop=False), [KKd, Mb], [psR])
                    for e in range(2):
                        pe(lambda e=e: PE.matmul(psR[:, 0:128], lhsT=AKL[e][:, 0:128], rhs=VZ[:, c, e, :], start=False, stop=(e == 1)), [AKL[e], VZ], [], [psR])
                    for e in range(2):
                        ac(lambda e=e: A.activation(out=RZ[:, e, e * 64:(e + 1) * 64], in_=psR[:, e * 64:(e + 1) * 64], func=AF.Identity, scale=-1.0), [psR], [], [RZ])
                    if _RWSTOP == 32:
                        return
                    for e in range(2):
                        pe(lambda e=e: PE.matmul(psU[:, 0:128], lhsT=TT[:, c, e * 128:(e + 1) * 128], rhs=RZ[:, e, :], start=(e == 0), stop=(e == 1)), [TT, RZ],
                           [psU] if e == 0 else [], [] if e == 0 else [psU])
                    if _RWSTOP == 321:
                        return
                    dv(lambda: V.tensor_copy(out=Ub[:], in_=psU[:, 0:128]), [psU], [Ub])
                    if _RWSTOP == 322:
                        return
                    for e in range(2):
                        dv(lambda e=e: V.tensor_copy(out=UZ[:, e, e * 64:(e + 1) * 64], in_=psU[:, e * 64:(e + 1) * 64]), [psU], [], [UZ])
                    if _RWSTOP == 33:
                        return
                    pe(lambda: PE.matmul(psY[:, 0:128], lhsT=Mb[:], rhs=Rd[:, tc], start=True, stop=False), [Mb, Rd], [psY])
                    for e in range(2):
                        pe(lambda e=e: PE.matmul(psY[:, 0:128], lhsT=UZ[:, e, :], rhs=AKL[e][:, 128:256], start=False, stop=False), [UZ, AKL[e]], [], [psY])
                        pe(lambda e=e: PE.matmul(psY[:, 0:128], lhsT=VZ[:, c, e, :], rhs=AKL[e][:, 256:384], start=False, stop=(e == 1)), [VZ, AKL[e]], [], [psY])
                    if d == 0:
                        ac(lambda: A.activation(out=YT[:, tc], in_=psY[:, 0:128], func=AF.Copy), [psY], [], [YT])
                    else:
                        dv(lambda: V.tensor_tensor(out=YT[:, tc], in0=psY[:, 0:128], in1=YT[:, tc], op=ALU.add), [psY, YT], [], [YT])
                    if _RWSTOP == 34:
                        return
                    pe(lambda: PE.matmul(psM[:, 0:128], lhsT=BiT[:, c, :], rhs=Ub[:], start=True, stop=False), [BiT, Ub], [psM])
                    for e in range(2):
                        pe(lambda e=e: PE.matmul(psM[:, 0:128], lhsT=KiT[:, c, :], rhs=VZ[:, c, e, :], start=False, stop=(e == 1)), [KiT, VZ], [], [psM])
                    dv(lambda: V.tensor_tensor(out=Mt32[:], in0=psM[:, 0:128], in1=mk["bo64"][:, 0:128], op=ALU.mult), [psM, mk["bo64"]], [Mt32])
                    dv(lambda: V.tensor_tensor(out=M32[:], in0=M32[:], in1=Mt32[:], op=ALU.add), [M32, Mt32], [M32])
                    dv(lambda: V.tensor_scalar(out=M32[:], in0=M32[:], scalar1=GCt[:, c:c + 1], scalar2=None, op0=ALU.mult), [M32, GCt], [M32])
                    ac(lambda: A.activation(out=Mb[:], in_=M32[:], func=AF.Copy), [M32], [Mb])
                    if _RWSTOP == 35:
                        return
            if _RWSTOP == 3:
                return
            for bi_, (s, n) in enumerate(tblocks):
                t32, x32, rn32, ex, a32 = tm["t32"], tm["x32"], tm["rn32"], tm["ex"], tm["a32"]
                ps = nps()
                pe(lambda ps=ps: PE.matmul(ps[:, :n], lhsT=bo64F, rhs=YT[:, s:s + n], start=True, stop=True), [consts, YT], [ps])
                dv(lambda ps=ps: V.scalar_tensor_tensor(out=t32[:, :n], in0=ps[:, :n], scalar=-1.0 / 64, in1=YT[:, s:s + n], op0=ALU.mult, op1=ALU.add), [ps, YT], [t32])
                ac(lambda: A.activation(out=x32[:, :n], in_=t32[:, :n], func=AF.Square), [t32], [x32])
                ps = nps()
                pe(lambda ps=ps: PE.matmul(ps[:, :n], lhsT=bo64F, rhs=x32[:, :n], start=True, stop=True), [consts, x32], [ps])
                ac(lambda ps=ps: A.activation(out=rn32[:, :n], in_=ps[:, :n], func=AF.Sqrt, scale=1.0 / 64, bias=epst[:, 1:2]), [ps, epst], [rn32])
                dv(lambda: V.reciprocal(out=rn32[:, :n], in_=rn32[:, :n]), [rn32], [rn32])
                dv(lambda: V.tensor_tensor(out=t32[:, :n], in0=t32[:, :n], in1=rn32[:, :n], op=ALU.mult), [t32, rn32], [t32])
                dv(lambda: V.tensor_scalar(out=t32[:, :n], in0=t32[:, :n], scalar1=pv[:, PV["lng"] + hp:PV["lng"] + hp + 1], scalar2=pv[:, PV["lnb"] + hp:PV["lnb"] + hp + 1],
                                            op0=ALU.mult, op1=ALU.add), [t32, pv], [t32])
                P.dma("sp", ex[:, :n], BG[hc, s:s + n], reads=[DB["BG"]], writes=[ex])
                P.dma("sp", a32[:, :n], GG[hc, s:s + n], reads=[DB["GG"]], writes=[a32])
                dv(lambda: V.tensor_tensor(out=t32[:, :n], in0=t32[:, :n], in1=ex[:, :n], op=ALU.add), [t32, ex], [t32])
                yb = yob[bi_ % 2]
                dv(lambda yb=yb: V.tensor_tensor(out=yb[:, :n], in0=t32[:, :n], in1=a32[:, :n], op=ALU.mult), [t32, a32], [yb])
                P.dma("sp", YR[hc, s:s + n], yb[:, :n], reads=[yb], pwrites=[DB["YR"]])
    def layer_ssm(l):
        def dv(fn, reads, writes=(), pwrites=()):
            return P.op("dve", fn, reads=reads, writes=writes, pwrites=pwrites)

        def ac(fn, reads, writes=(), pwrites=()):
            return P.op("act", fn, reads=reads, writes=writes, pwrites=pwrites)

        def pe(fn, reads, writes=(), pwrites=()):
            return P.op("pe", fn, reads=reads, writes=writes, pwrites=pwrites)

        def po(fn, reads, writes=(), pwrites=()):
            return P.op("pool", fn, reads=reads, writes=writes, pwrites=pwrites)

        Bf = P.sb("Bf", [128, 2, T], BF16)
        Cb = P.sb("Cb", [128, 2, T], BF16)
        BT = P.sb("BT", [128, NT, 256], BF16)
        YSa = P.sb("YSa", [128, 4, T])
        DT = P.sb("DT", [128, NT, 16])
        Atm = P.sb("Atm", [128, NT, 16])
        nexpA = P.sb("nexpA", [128, 16])
        stg = [P.sb("sstg%d" % i, [128, 512]) for i in range(2)]
        xsc = [P.sb("xsc%d" % i, [128, 4, 128]) for i in range(2)]
        acs = P.sb("acs", [128, 16])
        nacs = P.sb("nacs", [128, 8])
        coef = P.sb("coef", [128, 8])
        gtot = P.sb("gtot", [128, 8])
        GM = P.sb("GM", [128, 256])
        trA = [P.sb("trA%d" % i, [128, 128]) for i in range(2)]
        df = [P.sb("df%d" % i, [128, 128]) for i in range(2)]
        Lm = [P.sb("Lm%d" % i, [128, 128]) for i in range(2)]
        EB = [P.sb("EB%d" % i, [128, 128]) for i in range(2)]
        WT = [P.sb("WT%d" % i, [128, 128], BF16) for i in range(4)]
        Cd = [P.sb("Cd%d" % i, [128, 128], BF16) for i in range(4)]
        xZ = [P.sb("xZ%d" % i, [128, 2, 128], BF16) for i in range(2)]
        xd = [P.sb("xd%d" % i, [128, 64], BF16) for i in range(2)]
        S32 = P.sb("S32", [128, 8, 64])
        SbZ = P.sb("SbZ", [128, 8, 128], BF16)
        triF = {0: consts[:, CO["miu"]:CO["miu"] + 128], 1: consts[:, CO["mil"]:CO["mil"] + 128]}
        for t_ in xZ:
            po(lambda t_=t_: G.memset(t_[:], 0.0), [], [t_])
        for g in range(2):
            for (dst_t, r0_) in ((Bf, 512 + g * 128), (Cb, 768 + g * 128)):
                for bi_, (s, n) in enumerate(tblocks):
                    sg = stg[bi_ % 2]
                    P.dma("sp", sg[:, :n], XBC[r0_:r0_ + 128, s:s + n], reads=[DB["XBC"]], writes=[sg])
                    po(lambda dst_t=dst_t, sg=sg, s=s, n=n, g=g: G.tensor_copy(out=dst_t[:, g, s:s + n], in_=sg[:, :n]), [sg], [], [dst_t])
                    if dst_t is Bf:
                        nch = n // 128
                        ps = nps()
                        for ci in range(nch):
                            pe(lambda ps=ps, ci=ci, sg=sg: PE.transpose(out=ps[:, ci * 128:(ci + 1) * 128], in_=sg[:, ci * 128:(ci + 1) * 128], identity=ident), [sg, consts],
                               [ps] if ci == 0 else [], [] if ci == 0 else [ps])
                        dv(lambda ps=ps, s=s, n=n, g=g, nch=nch: V.tensor_copy(out=BT[:, s // 128:s // 128 + nch, g * 128:(g + 1) * 128],
                                                                               in_=ps[:, :n].rearrange("p (c t) -> p c t", c=nch)), [ps], [], [BT])
        P.dma("sp", DT[:], DTS.rearrange("(c p) j -> p c j", p=128), reads=[DB["DTS"]], writes=[DT])
        ac(lambda: A.activation(out=nexpA[:], in_=rowp[:, 16:32], func=AF.Exp), [rowp], [nexpA])
        dv(lambda: V.tensor_scalar(out=nexpA[:], in0=nexpA[:], scalar1=-1.0, scalar2=None, op0=ALU.mult), [nexpA], [nexpA])
        for c in range(NT):
            dv(lambda c=c: V.tensor_tensor(out=Atm[:, c, :], in0=DT[:, c, :], in1=nexpA[:], op=ALU.mult), [DT, nexpA], [], [Atm])
        XBv = XBC.rearrange("(q p) t -> p q t", p=128)
        for d in range(2):
            order = list(range(NT)) if d == 0 else (list(range(NCT - 1, -1, -1)) + list(range(NT - 1, NCT - 1, -1)))
            tri = triF[d]
            dv(lambda: V.memset(S32[:], 0.0), [], [S32])
            po(lambda: G.memset(SbZ[:], 0.0), [], [SbZ])
            for ci_, c in enumerate(order):
                tc = slice(c * 128, (c + 1) * 128)
                xc = xsc[ci_ % 2]
                P.dma("sp", xc[:], XBv[:, 0:4, tc], reads=[DB["XBC"]], writes=[xc])
                psX = nps()
                for q in range(4):
                    pe(lambda q=q, xc=xc, psX=psX: PE.transpose(out=psX[:, q * 128:(q + 1) * 128], in_=xc[:, q, :], identity=ident), [xc, consts],
                       [psX] if q == 0 else [], [] if q == 0 else [psX])
                psS = nps()
                pe(lambda psS=psS: PE.matmul(psS[:, 0:8], lhsT=tri, rhs=Atm[:, c, d * 8:(d + 1) * 8], start=True, stop=True), [consts, Atm], [psS])
                pe(lambda psS=psS: PE.matmul(psS[:, 8:16], lhsT=onesF, rhs=Atm[:, c, d * 8:(d + 1) * 8], start=True, stop=True), [consts, Atm], [], [psS])
                dv(lambda psS=psS: V.tensor_copy(out=acs[:], in_=psS[:, 0:16]), [psS], [acs])
                dv(lambda: V.tensor_scalar(out=nacs[:], in0=acs[:, 0:8], scalar1=-1.0, scalar2=None, op0=ALU.mult), [acs], [nacs])
                dv(lambda: V.tensor_tensor(out=coef[:], in0=acs[:, 8:16], in1=acs[:, 0:8], op=ALU.subtract), [acs], [coef])
                ac(lambda: A.activation(out=coef[:], in_=coef[:], func=AF.Exp), [coef], [coef])
                dv(lambda: V.tensor_tensor(out=coef[:], in0=coef[:], in1=DT[:, c, d * 8:(d + 1) * 8], op=ALU.mult), [coef, DT], [coef])
                ac(lambda: A.activation(out=gtot[:], in_=acs[:, 8:16], func=AF.Exp), [acs], [gtot])
                psG = nps()
                for g in range(2):
                    pe(lambda g=g, psG=psG: PE.matmul(psG[:, g * 128:(g + 1) * 128], lhsT=Bf[:, g, tc], rhs=Cb[:, g, tc], start=True, stop=True), [Bf, Cb],
                       [psG] if g == 0 else [], [] if g == 0 else [psG])
                for g in range(2):
                    dv(lambda g=g, psG=psG: V.tensor_tensor(out=GM[:, g * 128:(g + 1) * 128], in0=psG[:, g * 128:(g + 1) * 128], in1=tri, op=ALU.mult), [psG, consts],
                       [GM] if g == 0 else [], [] if g == 0 else [GM])
                psB = [nps(), nps()]
                psY = nps()
                psSt = nps()
                for h in range(8):
                    j = d * 8 + h
                    g = h // 4
                    e = h % 2
                    hp = h // 2
                    bcol = slice((h % 4) * 128, (h % 4 + 1) * 128)
                    pb = psB[h // 4]
                    ta, dfx, lm, eb = trA[h % 2], df[h % 2], Lm[h % 2], EB[h % 2]
                    wt, cd = WT[h % 4], Cd[h % 4]
                    dv(lambda ta=ta, j=j: V.tensor_scalar(out=ta[:], in0=tri, scalar1=Atm[:, c, j:j + 1], scalar2=None, op0=ALU.mult), [consts, Atm], [ta])
                    pe(lambda ta=ta, pb=pb, bcol=bcol: PE.matmul(pb[:, bcol], lhsT=onesF, rhs=ta[:], start=True, stop=True), [consts, ta],
                       [pb] if h % 4 == 0 else [], [] if h % 4 == 0 else [pb])
                    dv(lambda dfx=dfx, pb=pb, bcol=bcol, h=h: V.tensor_scalar(out=dfx[:], in0=pb[:, bcol], scalar1=nacs[:, h:h + 1], scalar2=0.0, op0=ALU.add, op1=ALU.min), [pb, nacs], [dfx])
                    ac(lambda dfx=dfx, lm=lm: A.activation(out=lm[:], in_=dfx[:], func=AF.Exp), [dfx], [lm])
                    po(lambda lm=lm, wt=wt, g=g: G.tensor_tensor(out=wt[:], in0=lm[:], in1=GM[:, g * 128:(g + 1) * 128], op=ALU.mult), [lm, GM], [wt])
                    ac(lambda eb=eb, pb=pb, bcol=bcol: A.activation(out=eb[:], in_=pb[:, bcol], func=AF.Exp), [pb], [eb])
                    po(lambda eb=eb, cd=cd, g=g: G.tensor_tensor(out=cd[:], in0=Cb[:, g, tc], in1=eb[:], op=ALU.mult), [eb, Cb], [cd])
                    xz = xZ[hp % 2]
                    dv(lambda xz=xz, e=e, h=h, j=j, psX=psX: V.tensor_scalar(out=xz[:, e, e * 64:(e + 1) * 64], in0=psX[:, h * 64:(h + 1) * 64], scalar1=DT[:, c, j:j + 1], scalar2=None, op0=ALU.mult),
                       [psX, DT], [], [xz])
                    xdd = xd[h % 2]
                    dv(lambda xdd=xdd, h=h, psX=psX: V.tensor_scalar(out=xdd[:], in0=psX[:, h * 64:(h + 1) * 64], scalar1=coef[:, h:h + 1], scalar2=None, op0=ALU.mult), [psX, coef], [xdd])
                    ycol = slice(hp * 128, (hp + 1) * 128)
                    firsty = (h == 0)
                    pe(lambda xz=xz, e=e, wt=wt, ycol=ycol: PE.matmul(psY[:, ycol], lhsT=xz[:, e, :], rhs=wt[:], start=(e == 0), stop=False), [xz, wt],
                       [psY] if firsty else [], [] if firsty else [psY])
                    pe(lambda h=h, cd=cd, ycol=ycol, e=e: PE.matmul(psY[:, ycol], lhsT=SbZ[:, h, :], rhs=cd[:], start=False, stop=(e == 1)), [SbZ, cd], [], [psY])
                    pe(lambda h=h, g=g, xdd=xdd: PE.matmul(psSt[:, h * 64:(h + 1) * 64], lhsT=BT[:, c, g * 128:(g + 1) * 128], rhs=xdd[:], start=True, stop=True), [BT, xdd],
                       [psSt] if h == 0 else [], [] if h == 0 else [psSt])
                    dv(lambda h=h: V.scalar_tensor_tensor(out=S32[:, h, :], in0=S32[:, h, :], scalar=gtot[:, h:h + 1], in1=psSt[:, h * 64:(h + 1) * 64], op0=ALU.mult, op1=ALU.add),
                       [S32, gtot, psSt], [], [S32])
                    ac(lambda h=h, e=e: A.activation(out=SbZ[:, h, e * 64:(e + 1) * 64], in_=S32[:, h, :], func=AF.Copy), [S32], [], [SbZ])
                if d == 0:
                    ac(lambda psY=psY: A.activation(out=YSa[:, :, tc], in_=psY[:].rearrange("p (q t) -> p q t", q=4), func=AF.Copy), [psY], [], [YSa])
                else:
                    dv(lambda psY=psY: V.tensor_tensor(out=YSa[:, :, tc], in0=psY[:].rearrange("p (q t) -> p q t", q=4), in1=YSa[:, :, tc], op=ALU.add), [psY, YSa], [], [YSa])
        tt_ = [P.sb("sst%d" % i, [128, 512]) for i in range(2)]
        sq_ = [P.sb("ssq%d" % i, [128, 512]) for i in range(2)]
        xl = [P.sb("sxl%d" % i, [128, 512]) for i in range(2)]
        zl = [P.sb("szl%d" % i, [128, 512]) for i in range(2)]
        rs = P.sb("srs", [128, 512])
        yb_ = [P.sb("syb%d" % i, [128, 512], BF16) for i in range(2)]
        for (s, n) in tblocks:
            for gg in range(2):
                psn = nps()
                for i_ in range(2):
                    hp = gg * 2 + i_
                    hc = slice(hp * 128, (hp + 1) * 128)
                    P.dma("sp", xl[i_][:, :n], XBC[hc, s:s + n], reads=[DB["XBC"]], writes=[xl[i_]])
                    P.dma("sp", zl[i_][:, :n], ZS[hc, s:s + n], reads=[DB["ZS"]], writes=[zl[i_]])
                    dv(lambda i_=i_, hp=hp: V.scalar_tensor_tensor(out=tt_[i_][:, :n], in0=xl[i_][:, :n], scalar=pv[:, PV["sd"] + hp:PV["sd"] + hp + 1], in1=YSa[:, hp, s:s + n],
                                                                 op0=ALU.mult, op1=ALU.add), [xl[i_], pv, YSa], [tt_[i_]])
                    dv(lambda i_=i_: V.tensor_tensor(out=tt_[i_][:, :n], in0=tt_[i_][:, :n], in1=zl[i_][:, :n], op=ALU.mult), [tt_[i_], zl[i_]], [tt_[i_]])
                    ac(lambda i_=i_: A.activation(out=sq_[i_][:, :n], in_=tt_[i_][:, :n], func=AF.Square), [tt_[i_]], [sq_[i_]])
                    pe(lambda i_=i_, psn=psn: PE.matmul(psn[:, :n], lhsT=onesF, rhs=sq_[i_][:, :n], start=(i_ == 0), stop=(i_ == 1)), [consts, sq_[i_]],
                       [psn] if i_ == 0 else [], [] if i_ == 0 else [psn])
                ac(lambda psn=psn: A.activation(out=rs[:, :n], in_=psn[:, :n], func=AF.Sqrt, scale=1.0 / 256, bias=epst[:, 0:1]), [psn, epst], [rs])
                dv(lambda: V.reciprocal(out=rs[:, :n], in_=rs[:, :n]), [rs], [rs])
                for i_ in range(2):
                    hp = gg * 2 + i_
                    hc = slice(hp * 128, (hp + 1) * 128)
                    dv(lambda i_=i_: V.tensor_tensor(out=tt_[i_][:, :n], in0=tt_[i_][:, :n], in1=rs[:, :n], op=ALU.mult), [tt_[i_], rs], [tt_[i_]])
                    dv(lambda i_=i_, hp=hp: V.tensor_scalar(out=yb_[i_][:, :n], in0=tt_[i_][:, :n], scalar1=pv[:, PV["sng"] + hp:PV["sng"] + hp + 1], scalar2=None, op0=ALU.mult),
                       [tt_[i_], pv], [yb_[i_]])
                    P.dma("sp", YS[hc, s:s + n], yb_[i_][:, :n], reads=[yb_[i_]], pwrites=[DB["YS"]])
    GTv = GT.rearrange("(b q p) t -> p b q t", b=3, q=8)
    H2D = dscr("H2D", [D, T], BF16)
    H2v = H2D.rearrange("(c p) t -> p c t", p=128)
    DB["H2"] = P.buf("H2")

    def layer_back(l):
        ml = modL[l]

        def dv(fn, reads, writes=(), pwrites=()):
            return P.op("dve", fn, reads=reads, writes=writes, pwrites=pwrites)

        def ac(fn, reads, writes=(), pwrites=()):
            return P.op("act", fn, reads=reads, writes=writes, pwrites=pwrites)

        def pe(fn, reads, writes=(), pwrites=()):
            return P.op("pe", fn, reads=reads, writes=writes, pwrites=pwrites)

        def po(fn, reads, writes=(), pwrites=()):
            return P.op("pool", fn, reads=reads, writes=writes, pwrites=pwrites)

        h2s = P.sb("h2s", [128, 8, 512], BF16)
        Rt = P.sb("Rt", [128, 8, 512])
        sqr = [P.sb("sqr%d" % i, [128, 512]) for i in range(2)]
        mean = P.sb("mean", [128, 512])
        rstd = P.sb("rstd", [128, 512])
        xa = [P.sb("xa%d" % i, [128, 512]) for i in range(2)]

        def layer_norm(s, n, gname, bname, first):
            nidx = 0 if s >= CTX else 1
            psm = nps()
            for oc in range(8):
                pe(lambda oc=oc: PE.matmul(psm[:, :n], lhsT=onesF, rhs=Rt[:, oc, :n], start=(oc == 0), stop=(oc == 7)), [consts, Rt], [psm] if oc == 0 else [], [] if oc == 0 else [psm])
            ac(lambda: A.activation(out=mean[:, :n], in_=psm[:, :n], func=AF.Copy, scale=-1.0 / 1024) if False else A.activation(out=mean[:, :n], in_=psm[:, :n], func=AF.Identity, scale=-1.0 / 1024),
               [psm], [mean])
            psv = nps()
            for oc in range(8):
                po(lambda oc=oc: G.tensor_tensor(out=Rt[:, oc, :n], in0=Rt[:, oc, :n], in1=mean[:, :n], op=ALU.add), [Rt, mean], [], [Rt])
                sq = sqr[oc % 2]
                ac(lambda oc=oc, sq=sq: A.activation(out=sq[:, :n], in_=Rt[:, oc, :n], func=AF.Square), [Rt], [sq])
                pe(lambda oc=oc, sq=sq: PE.matmul(psv[:, :n], lhsT=onesF, rhs=sq[:, :n], start=(oc == 0), stop=(oc == 7)), [consts, sq], [psv] if oc == 0 else [], [] if oc == 0 else [psv])
            ac(lambda: A.activation(out=rstd[:, :n], in_=psv[:, :n], func=AF.Sqrt, scale=1.0 / 1024, bias=epst[:, 0:1]), [psv, epst], [rstd])
            dv(lambda: V.reciprocal(out=rstd[:, :n], in_=rstd[:, :n]), [rstd], [rstd])
            for oc in range(8):
                dv(lambda oc=oc: V.tensor_tensor(out=Rt[:, oc, :n], in0=Rt[:, oc, :n], in1=rstd[:, :n], op=ALU.mult), [Rt, rstd], [], [Rt])
                dv(lambda oc=oc: V.tensor_scalar(out=Rt[:, oc, :n], in0=Rt[:, oc, :n], scalar1=pv[:, PV[gname] + oc:PV[gname] + oc + 1], scalar2=pv[:, PV[bname] + oc:PV[bname] + oc + 1],
                                                 op0=ALU.mult, op1=ALU.add), [Rt, pv], [], [Rt])
                if first:
                    ac(lambda oc=oc: A.activation(out=h2s[:, oc, :n], in_=Rt[:, oc, :n], func=AF.Identity,
                                                   bias=ml[:, (24 + oc) * 2 + nidx:(24 + oc) * 2 + nidx + 1], scale=onep[:, (32 + oc) * 2 + nidx:(32 + oc) * 2 + nidx + 1]),
                       [Rt, ml, onep], [h2s] if oc == 0 else [], [] if oc == 0 else [h2s])
            P.dma("sp", XTv[:, :, s:s + n], Rt[:, :, :n], reads=[Rt], pwrites=XTb)
            if first:
                P.dma("act", H2v[:, :, s:s + n], h2s[:, :, :n], reads=[h2s], pwrites=[DB["H2"]])

        with P.scope():
            pw = P.sb("pw", [128, 12, 1024], BF16)
            wo = P.sb("wo", [128, 8, 1024], BF16)
            wst = [P.sb("wst%d" % i, [128, 4, 1024]) for i in range(2)]
            srcs = [(pw, 0, p_attn[l]), (pw, 4, p_rwkv[l]), (pw, 8, p_ssm[l]), (wo, 0, w_out[l][0:512, :]), (wo, 4, w_out[l][512:1024, :])]
            for i_, (dst_t, k0, src) in enumerate(srcs):
                sg = wst[i_ % 2]
                P.dma("sp", sg[:], src.rearrange("(k p) n -> p k n", p=128), writes=[sg])
                po(lambda dst_t=dst_t, k0=k0, sg=sg: G.tensor_copy(out=dst_t[:, k0:k0 + 4, :], in_=sg[:]), [sg], [], [dst_t])
            yb3 = [P.sb("y3_%d" % i, [128, 4, 512], BF16) for i in range(3)]
            gts = [P.sb("gts%d" % i, [128, 3, 512]) for i in range(2)]
            MT = P.sb("MT", [128, 8, 512], BF16)
            m1 = P.sb("m1", [128, 512])
            m2 = P.sb("m2", [128, 512])
            for (s, n) in tblocks:
                nidx = 0 if s >= CTX else 1
                for b_, (src, nm) in enumerate(((YA, "YA"), (YR, "YR"), (YS, "YS"))):
                    P.dma("sp", yb3[b_][:, :, :n], src.rearrange("(q p) t -> p q t", p=128)[:, :, s:s + n], reads=[DB[nm]], writes=[yb3[b_]])
                for oc in range(8):
                    gt_ = gts[oc % 2]
                    P.dma("sp", gt_[:, :, :n], GTv[:, :, oc, s:s + n], reads=[DB["GT"]], writes=[gt_])
                    pss = [nps(), nps(), nps()]
                    for b_ in range(3):
                        for k in range(4):
                            pe(lambda b_=b_, k=k, oc=oc, pss=pss: PE.matmul(pss[b_][:, :n], lhsT=pw[:, b_ * 4 + k, oc * 128:(oc + 1) * 128], rhs=yb3[b_][:, k, :n], start=(k == 0), stop=(k == 3)),
                               [pw, yb3[b_]], [pss[b_]] if k == 0 else [], [] if k == 0 else [pss[b_]])
                    dv(lambda pss=pss, gt_=gt_: V.tensor_tensor(out=m1[:, :n], in0=pss[0][:, :n], in1=gt_[:, 0, :n], op=ALU.mult), [pss[0], gt_], [m1])
                    dv(lambda pss=pss, gt_=gt_: V.tensor_tensor(out=m2[:, :n], in0=pss[1][:, :n], in1=gt_[:, 1, :n], op=ALU.mult), [pss[1], gt_], [m2])
                    po(lambda: G.tensor_tensor(out=m1[:, :n], in0=m1[:, :n], in1=m2[:, :n], op=ALU.add), [m1, m2], [m1])
                    dv(lambda pss=pss, gt_=gt_: V.tensor_tensor(out=m2[:, :n], in0=pss[2][:, :n], in1=gt_[:, 2, :n], op=ALU.mult), [pss[2], gt_], [m2])
                    po(lambda oc=oc: G.tensor_tensor(out=MT[:, oc, :n], in0=m1[:, :n], in1=m2[:, :n], op=ALU.add), [m1, m2], [MT] if oc == 0 else [], [] if oc == 0 else [MT])
                for oc in range(8):
                    ps = nps()
                    for k in range(8):
                        pe(lambda k=k, oc=oc, ps=ps: PE.matmul(ps[:, :n], lhsT=wo[:, k, oc * 128:(oc + 1) * 128], rhs=MT[:, k, :n], start=(k == 0), stop=(k == 7)), [wo, MT],
                           [ps] if k == 0 else [], [] if k == 0 else [ps])
                    x_ = xa[oc % 2]
                    P.dma("sp", x_[:, :n], XT[oc * 128:(oc + 1) * 128, s:s + n], reads=[XTb[oc]], writes=[x_])
                    po(lambda x_=x_: G.tensor_scalar(out=x_[:, :n], in0=x_[:, :n], scalar1=alpha, scalar2=None, op0=ALU.mult), [x_], [x_])
                    dv(lambda oc=oc, ps=ps, x_=x_: V.scalar_tensor_tensor(out=Rt[:, oc, :n], in0=ps[:, :n], scalar=ml[:, (16 + oc) * 2 + nidx:(16 + oc) * 2 + nidx + 1], in1=x_[:, :n],
                                                                        op0=ALU.mult, op1=ALU.add), [ps, ml, x_], [Rt] if oc == 0 else [], [] if oc == 0 else [Rt])
                layer_norm(s, n, "l1g", "l1b", True)
        if stop_after == "LN1":
            return
        with P.scope():
            groups = [[(s, n) for (s, n) in blocks(0, CTX)]]
            lat = blocks(CTX, T)
            for i in range(0, len(lat), 2):
                groups.append(lat[i:i + 2])
            HID = P.sb("HID", [128, 22, 1024], BF16)
            H2g = P.sb("H2g", [128, 8, 1024], BF16)
            RG = P.sb("RG", [128, 8, 1024])
            w13 = [P.sb("w13_%d" % i, [128, 8, 128], BF16) for i in range(4)]
            w13s = [P.sb("w13s%d" % i, [128, 8, 128]) for i in range(2)]
            w2b_ = [P.sb("w2b_%d" % i, [128, 22, 128], BF16) for i in range(2)]
            w2ss = [P.sb("w2s%d" % i, [128, 22, 128]) for i in range(2)]
            sl = [P.sb("sl%d" % i, [128, 512]) for i in range(2)]
            wc = [0]
            for grp in groups:
                g0 = grp[0][0]
                gn = sum(n_ for (_, n_) in grp)
                P.dma("sp", H2g[:, :, :gn], H2v[:, :, g0:g0 + gn], reads=[DB["H2"]], writes=[H2g])
                for j in range(22):
                    ws = []
                    for wsrc in (ffn_w1, ffn_w3):
                        wb = w13[wc[0] % 4]
                        sg = w13s[wc[0] % 2]
                        P.dma("sp", sg[:], wsrc[l][:, j * 128:(j + 1) * 128].rearrange("(k p) n -> p k n", p=128), writes=[sg])
                        po(lambda wb=wb, sg=sg: G.tensor_copy(out=wb[:], in_=sg[:]), [sg], [wb])
                        wc[0] += 1
                        ws.append(wb)
                    for bi_, (s, n) in enumerate(grp):
                        ps1, ps3 = nps(), nps()
                        for k in range(8):
                            pe(lambda k=k, ps1=ps1, s=s, n=n: PE.matmul(ps1[:, :n], lhsT=ws[0][:, k, :], rhs=H2g[:, k, s - g0:s - g0 + n], start=(k == 0), stop=(k == 7)), [ws[0], H2g],
                               [ps1] if k == 0 else [], [] if k == 0 else [ps1])
                        for k in range(8):
                            pe(lambda k=k, ps3=ps3, s=s, n=n: PE.matmul(ps3[:, :n], lhsT=ws[1][:, k, :], rhs=H2g[:, k, s - g0:s - g0 + n], start=(k == 0), stop=(k == 7)), [ws[1], H2g],
                               [ps3] if k == 0 else [], [] if k == 0 else [ps3])
                        st_ = sl[bi_ % 2]
                        ac(lambda ps1=ps1, st_=st_, n=n: A.activation(out=st_[:, :n], in_=ps1[:, :n], func=AF.Silu), [ps1], [st_])
                        dv(lambda ps3=ps3, st_=st_, s=s, n=n, j=j: V.tensor_tensor(out=HID[:, j, s - g0:s - g0 + n], in0=ps3[:, :n], in1=st_[:, :n], op=ALU.mult), [ps3, st_], [], [HID])
                for oc in range(8):
                    wb = w2b_[oc % 2]
                    w2s = w2ss[oc % 2]
                    P.dma("sp", w2s[:], ffn_w2[l][:, oc * 128:(oc + 1) * 128].rearrange("(j p) n -> p j n", p=128), writes=[w2s])
                    po(lambda wb=wb, w2s=w2s: G.tensor_copy(out=wb[:], in_=w2s[:]), [w2s], [wb])
                    for (s, n) in grp:
                        nidx = 0 if s >= CTX else 1
                        ps = nps()
                        for j in range(22):
                            pe(lambda j=j, ps=ps, s=s, n=n, wb=wb: PE.matmul(ps[:, :n], lhsT=wb[:, j, :], rhs=HID[:, j, s - g0:s - g0 + n], start=(j == 0), stop=(j == 21)), [wb, HID],
                               [ps] if j == 0 else [], [] if j == 0 else [ps])
                        x_ = xa[oc % 2]
                        P.dma("sp", x_[:, :n], XT[oc * 128:(oc + 1) * 128, s:s + n], reads=[XTb[oc]], writes=[x_])
                        po(lambda x_=x_, n=n: G.tensor_scalar(out=x_[:, :n], in0=x_[:, :n], scalar1=alpha, scalar2=None, op0=ALU.mult), [x_], [x_])
                        dv(lambda oc=oc, ps=ps, x_=x_, s=s, n=n, nidx=nidx: V.scalar_tensor_tensor(out=RG[:, oc, s - g0:s - g0 + n], in0=ps[:, :n],
                                                                                               scalar=ml[:, (40 + oc) * 2 + nidx:(40 + oc) * 2 + nidx + 1], in1=x_[:, :n],
                                                                                               op0=ALU.mult, op1=ALU.add), [ps, ml, x_], [], [RG])
                for (s, n) in grp:
                    for oc in range(8):
                        po(lambda oc=oc, s=s, n=n: G.tensor_copy(out=Rt[:, oc, :n], in_=RG[:, oc, s - g0:s - g0 + n]), [RG], [Rt] if oc == 0 else [], [] if oc == 0 else [Rt])
                    layer_norm(s, n, "l2g", "l2b", False)

    def final_out():
        xf = [P.sb("xf%d" % i, [128, 8, 128]) for i in range(2)]
        yo_ = [P.sb("yof%d" % i, [128, D]) for i in range(2)]
        toks = []
        for i in range(NCT, NT):
            xi, yo2 = xf[i % 2], yo_[i % 2]
            P.dma("sp", xi[:], XTv[:, :, i * 128:(i + 1) * 128], reads=XTb, writes=[xi])
            for half in range(2):
                ps = nps()
                for q in range(4):
                    c = half * 4 + q
                    P.op("pe", lambda c=c, q=q, ps=ps, xi=xi: PE.transpose(out=ps[:, q * 128:(q + 1) * 128], in_=xi[:, c, :], identity=ident),
                         reads=[xi, consts], pwrites=[ps] if q else [], writes=[] if q else [ps])
                if half:
                    P.op("act", lambda ps=ps, yo2=yo2: A.activation(out=yo2[:, 512:1024], in_=ps[:], func=AF.Copy), reads=[ps], pwrites=[yo2])
                else:
                    P.op("dve", lambda ps=ps, yo2=yo2: V.tensor_copy(out=yo2[:, 0:512], in_=ps[:]), reads=[ps], writes=[yo2])
            toks.append(P.dma("sp", y_out[(i - NCT) * 128:(i - NCT + 1) * 128, :], yo2[:], reads=[yo2]))
        return toks
    for l in range(DEPTH):
        with P.scope():
            layer_front(l)
        with P.scope():
            layer_attn(l)
        with P.scope():
            layer_rwkv(l)
        with P.scope():
            layer_ssm(l)
        with P.scope():
            layer_back(l)
    with P.scope():
        final_out()
    P.barrier()
    return nc, st, P


def _cols(v, n):
    return np.ascontiguousarray(np.asarray(v, np.float32).reshape(n, 128).T)


def _rope_perm():
    perm = np.zeros(128, np.int64)
    for base in range(0, 128, 32):
        for j in range(16):
            perm[base + j] = base + j + 16
            perm[base + 16 + j] = base + j
    return perm


def host_prep(inp, SEQ, CTX, DEPTH, GRID_W=64):
    T = CTX + SEQ
    perm = _rope_perm()
    common = {}
    consts = np.zeros((128, NCONST), np.float32)
    idx = np.arange(128)
    consts[:, CO["ident"]:CO["ident"] + 128] = np.eye(128, dtype=np.float32)
    consts[:, CO["msl"]:CO["msl"] + 128] = (idx[:, None] > idx[None, :])
    consts[:, CO["msu"]:CO["msu"] + 128] = (idx[:, None] < idx[None, :])
    consts[:, CO["mil"]:CO["mil"] + 128] = (idx[:, None] >= idx[None, :])
    consts[:, CO["miu"]:CO["miu"] + 128] = (idx[:, None] <= idx[None, :])
    consts[:, CO["bo64"]:CO["bo64"] + 128] = ((idx[:, None] // 64) == (idx[None, :] // 64))
    consts[:, CO["ones"]:CO["ones"] + 128] = 1.0
    common["consts"] = consts
    lv = np.zeros((128, 14, 128), np.float32)
    for k_ in range(7):
        bsz = 1 << k_
        blk = idx // (2 * bsz)
        half = (idx // bsz) % 2
        mo = ((blk[:, None] == blk[None, :]) & (half[:, None] == 1) & (half[None, :] == 0)).astype(np.float32)
        lv[:, k_, :] = mo
        lv[:, 7 + k_, :] = mo.T
    common["lvmask"] = lv
    rows_ = SEQ // GRID_W
    row = np.repeat(np.arange(rows_), GRID_W).astype(np.float32)
    col = np.tile(np.arange(GRID_W), rows_).astype(np.float32)
    nf = 16
    inv = (10000.0 ** (-np.arange(nf, dtype=np.float32) / nf)).astype(np.float32)
    ang_r = row[:, None] * inv
    ang_c = col[:, None] * inv
    Ct = np.ones((128, T), np.float32)
    St = np.zeros((128, T), np.float32)
    for p in range(128):
        d = p % 64
        ang = ang_r if d < 32 else ang_c
        j = d % 32
        f = j % 16
        Ct[p, CTX:] = np.cos(ang[:, f])
        St[p, CTX:] = (-np.sin(ang[:, f])) if j < 16 else np.sin(ang[:, f])
    common["rope"] = np.stack([Ct, St])
    f32 = lambda a: np.ascontiguousarray(np.asarray(a, np.float32))
    w_in = f32(inp["w_in"])
    common["w_in"] = w_in
    wq = w_in[:, :, 0:512].reshape(DEPTH, 1024, 4, 128)[:, :, :, perm].reshape(DEPTH, 1024, 512)
    wk = w_in[:, :, 512:1024].reshape(DEPTH, 1024, 4, 128)[:, :, :, perm].reshape(DEPTH, 1024, 512)
    common["w_perm"] = np.ascontiguousarray(np.concatenate([wq, wk], axis=2))
    common["ada_w"] = f32(inp["ada_w"])
    pvec = np.zeros((DEPTH, 128, NPV), np.float32)
    rowp = np.zeros((DEPTH, 128, 32), np.float32)
    for l in range(DEPTH):
        def put(name, v, n):
            pvec[l, :, PV[name]:PV[name] + n] = _cols(v, n)
        put("mu", inp["rw_mu"][l], 15)
        put("w0", np.asarray(inp["rw_w0"][l]).reshape(-1), 8)
        put("a0", np.asarray(inp["rw_a0"][l]).reshape(-1), 8)
        put("kk", inp["rw_kk"][l], 4)
        put("ka", inp["rw_ka"][l], 4)
        put("rk", np.asarray(inp["rw_rk"][l]).reshape(-1), 4)
        put("lng", inp["rw_ln_g"][l], 4)
        put("lnb", inp["rw_ln_b"][l], 4)
        put("cw", np.asarray(inp["ssm_conv_w"][l]).reshape(-1), 24)
        put("cb", inp["ssm_conv_b"][l], 8)
        put("sd", np.repeat(np.asarray(inp["ssm_d"][l]), 64), 4)
        put("sng", inp["ssm_norm_g"][l], 4)
        put("dag", inp["da_norm_g"][l], 1)
        put("l1g", inp["ln1_g"][l], 8)
        put("l1b", inp["ln1_b"][l], 8)
        put("l2g", inp["ln2_g"][l], 8)
        put("l2b", inp["ln2_b"][l], 8)
        put("adab", inp["ada_b"][l], 48)
        for i, nm in enumerate(("da_lq1", "da_lk1", "da_lq2", "da_lk2")):
            pvec[l, 0:64, PV["dal"] + i] = np.asarray(inp[nm][l], np.float32)
        rowp[l, :, 0:16] = np.broadcast_to(np.asarray(inp["ssm_dt_bias"][l], np.float32).reshape(1, 16), (128, 16))
        rowp[l, :, 16:32] = np.broadcast_to(np.asarray(inp["ssm_a_log"][l], np.float32).reshape(1, 16), (128, 16))
    common["pvec"] = pvec
    common["rowp"] = rowp
    common["rw_w2"] = f32(inp["rw_w2"]).reshape(DEPTH, 128, 512)
    common["rw_a2"] = f32(inp["rw_a2"]).reshape(DEPTH, 128, 512)
    common["rw_g2"] = f32(inp["rw_g2"])
    for nm in ("p_attn", "p_rwkv", "p_ssm", "w_out", "ffn_w1", "ffn_w3", "ffn_w2"):
        common[nm] = f32(inp[nm])
    in_maps = []
    x = np.asarray(inp["x"], np.float32)
    c = np.asarray(inp["c"], np.float32)
    ctx = np.asarray(inp["ctx"], np.float32)
    c_ctx = np.asarray(inp["c_ctx"], np.float32)
    for b in range(x.shape[0]):
        m = dict(common)
        m["x"] = np.ascontiguousarray(x[b])
        m["ctx"] = np.ascontiguousarray(ctx[b])
        cc = np.zeros((128, 16), np.float32)
        cc[:, 0::2] = _cols(c[b], 8)
        cc[:, 1::2] = _cols(c_ctx, 8)
        m["cc"] = cc
        in_maps.append(m)
    return in_maps


SEQ_FULL, CTX_FULL, DEPTH_FULL = 4096, 256, 4


def kernel(**inputs):
    in_maps = host_prep(inputs, SEQ_FULL, CTX_FULL, DEPTH_FULL)
    nc, st, P = build(SEQ_FULL, CTX_FULL, DEPTH_FULL)
    res = run_bass_kernel_spmd(nc, in_maps, core_ids=list(range(len(in_maps))))
    y = np.stack([np.asarray(r["y"], dtype=np.float32) for r in res.results], axis=0)
    return y
```

```python
from contextlib import ExitStack
from concourse.bass_utils import run_bass_kernel_spmd
import numpy as np
import concourse.bass as bass
import concourse.mybir as mybir

F32 = mybir.dt.float32
BF16 = mybir.dt.bfloat16
AF = mybir.ActivationFunctionType
ALU = mybir.AluOpType

ENGS = ("pe", "act", "dve", "pool", "sp")
SAME_ENGINE_SYNC = True
N_DMA_SEMS = 40


class Buf:
    __slots__ = ("name", "w", "r", "fw")

    def __init__(self, name):
        self.name = name
        self.fw = None
        self.w = {}
        self.r = {}


class Tile:
    __slots__ = ("t", "buf", "shape", "dtype")

    def __init__(self, t, name, shape, dtype):
        self.t = t
        self.buf = Buf(name)
        self.shape = shape
        self.dtype = dtype

    def __getitem__(self, idx):
        return self.t[idx]


class Tok:
    __slots__ = ("kind", "eng", "n", "clock", "seen")

    def __init__(self, kind, eng, n, clock):
        self.kind = kind
        self.eng = eng
        self.n = n
        self.clock = clock
        self.seen = set()


class Prog:
    def __init__(self, nc, stack):
        self.nc = nc
        self.stack = stack
        self.E = {"pe": nc.tensor, "act": nc.scalar, "dve": nc.vector,
                  "pool": nc.gpsimd, "sp": nc.sync}
        self.sem = {e: stack.enter_context(nc.semaphore("s_" + e)) for e in ENGS}
        self.cnt = {e: 0 for e in ENGS}
        self.know = {e: {f: 0 for f in ENGS} for e in ENGS}
        self.dsem = [stack.enter_context(nc.semaphore("d%d" % i)) for i in range(N_DMA_SEMS)]
        self.dcnt = [0] * N_DMA_SEMS
        self.dlast = [None] * N_DMA_SEMS
        self.dnext = 0
        self.nbuf = 0
        self.pend = {e: [] for e in ENGS}
        self.n_wait = 0
        self.n_ins = 0

    def sb(self, name, shape, dtype=F32):
        self.nname = getattr(self, "nname", 0) + 1
        name = "%s_%d" % (name, self.nname)
        stk = self.scopes[-1] if getattr(self, "scopes", None) else self.stack
        t = stk.enter_context(self.nc.sbuf_tensor(name, list(shape), dtype))
        return Tile(t, name, list(shape), dtype)

    def barrier(self):
        for f in ENGS:
            if f != "sp" and self.cnt[f] > 0:
                self._need("sp", Tok("c", f, self.cnt[f], {}))
        for tok in self.dlast:
            if tok is not None:
                self._need("sp", tok)
        tok = self.op("sp", lambda: self.nc.sync.nop())
        for e in ENGS:
            if e != "sp":
                self._need(e, tok)

    def scope(self):
        import contextlib
        prog = self

        @contextlib.contextmanager
        def cm():
            if not getattr(prog, "scopes", None):
                prog.scopes = [prog.stack]
            es = contextlib.ExitStack()
            prog.scopes.append(es)
            try:
                yield
            finally:
                prog.barrier()
                prog.scopes.pop()
                es.close()
        return cm()

    def ps(self, name, shape, dtype=F32):
        t = self.stack.enter_context(self.nc.psum_tensor(name, list(shape), dtype))
        return Tile(t, name, list(shape), dtype)

    def buf(self, name):
        return Buf(name)

    def _need(self, eng, tok):
        if tok is None:
            return
        if tok.kind == "c":
            if tok.eng == eng:
                if eng == "pe" or not SAME_ENGINE_SYNC:
                    return
            if self.know[eng][tok.eng] >= tok.n:
                return
            self.pend[eng].append((self.sem[tok.eng], tok.n))
            self.n_wait += 1
            k = self.know[eng]
            for f, v in tok.clock.items():
                if v > k[f]:
                    k[f] = v
            if tok.n > k[tok.eng]:
                k[tok.eng] = tok.n
        else:
            if eng in tok.seen:
                return
            self.pend[eng].append((self.dsem[tok.eng], tok.n))
            self.n_wait += 1
            tok.seen.add(eng)
            k = self.know[eng]
            for f, v in tok.clock.items():
                if v > k[f]:
                    k[f] = v

    def _deps(self, eng, reads, writes, pwrites=()):
        for b in reads:
            for t in list(b.w.values()):
                self._need(eng, t)
        for b in writes:
            for t in list(b.w.values()):
                self._need(eng, t)
            for t in list(b.r.values()):
                self._need(eng, t)
        for b in pwrites:
            self._need(eng, b.fw)
            for t in list(b.r.values()):
                self._need(eng, t)

    @staticmethod
    def _key(tok):
        return tok.eng if tok.kind == "c" else ("d", tok.eng)

    def _commit(self, tok, reads, writes, pwrites=()):
        k = self._key(tok)
        for b in reads:
            b.r[k] = tok
        for b in writes:
            b.w = {k: tok}
            b.fw = tok
            b.r = {}
        for b in pwrites:
            b.w[k] = tok

    @staticmethod
    def _bufs(xs):
        out = []
        for x in xs:
            if x is None:
                continue
            if isinstance(x, Buf):
                out.append(x)
            else:
                out.append(x.buf)
        return out

    def op(self, eng, fn, reads=(), writes=(), pwrites=()):
        reads = self._bufs(reads)
        writes = self._bufs(writes)
        pwrites = self._bufs(pwrites)
        self._deps(eng, reads, writes, pwrites)
        waits = self.pend[eng]
        self.pend[eng] = []
        for (sm, vv) in waits[:-1]:
            self.E[eng].wait_ge(sm, vv)
        ins = fn()
        if waits:
            ins._wait_ge(waits[-1][0], waits[-1][1])
        self.cnt[eng] += 1
        n = self.cnt[eng]
        ins.then_inc(self.sem[eng], 1)
        self.n_ins += 1
        tok = Tok("c", eng, n, dict(self.know[eng]))
        self._commit(tok, reads, writes, pwrites)
        return tok

    def dma(self, q, out, in_, reads=(), writes=(), pwrites=(), **kw):
        reads = self._bufs(reads)
        writes = self._bufs(writes)
        pwrites = self._bufs(pwrites)
        self._deps(q, reads, writes, pwrites)
        si = self.dnext
        self.dnext = (self.dnext + 1) % N_DMA_SEMS
        prev = self.dlast[si]
        if prev is not None:
            self._need(q, prev)
        self.dcnt[si] += 16
        for (sm, vv) in self.pend[q]:
            self.E[q].wait_ge(sm, vv)
        self.pend[q] = []
        ins = self.E[q].dma_start(out=out, in_=in_, **kw)
        ins.then_inc(self.dsem[si], 16)
        self.n_ins += 1
        tok = Tok("d", si, self.dcnt[si], dict(self.know[q]))
        self.dlast[si] = tok
        self._commit(tok, reads, writes, pwrites)
        return tok

    def finish(self, toks, eng="sp"):
        for t in toks:
            self._need(eng, t)
D = 1024
NQKV = 512
FFN = 2816
RWC = 1920
W_IN_COLS = 8080
PV = {}
_o = 0
for _n, _c in [("mu", 15), ("w0", 8), ("a0", 8), ("kk", 4), ("ka", 4), ("rk", 4), ("lng", 4), ("lnb", 4),
               ("cw", 24), ("cb", 8), ("sd", 4), ("sng", 4), ("dag", 1), ("l1g", 8), ("l1b", 8), ("l2g", 8),
               ("l2b", 8), ("adab", 48), ("dal", 4)]:
    PV[_n] = _o
    _o += _c
NPV = _o
CO = {"ident": 0, "msl": 128, "msu": 256, "mil": 384, "miu": 512, "bo64": 640, "ones": 768}
NCONST = 896


def lam_init_of(layer):
    import math
    return 0.8 - 0.6 * math.exp(-0.3 * layer)


def build(SEQ, CTX, DEPTH, debug=False, stop_after=None):
    T = CTX + SEQ
    NT = T // 128
    NCT = CTX // 128
    alpha = (2.0 * DEPTH) ** 0.25
    nc = bass.Bass("TRN2", target_bir_lowering=False)
    ikind = "ExternalOutput" if debug else "Internal"

    def din(name, shape, dt=F32):
        return nc.dram_tensor(name, list(shape), dt, kind="ExternalInput").ap()

    def dscr(name, shape, dt=F32):
        return nc.dram_tensor(name, list(shape), dt, kind=ikind).ap()

    x_in = din("x", [SEQ, D])
    ctx_in = din("ctx", [CTX, D])
    cc_in = din("cc", [128, 16])
    ada_w = din("ada_w", [DEPTH, D, 6 * D])
    w_in = din("w_in", [DEPTH, D, W_IN_COLS])
    w_perm = din("w_perm", [DEPTH, D, 1024])
    pvec_in = din("pvec", [DEPTH, 128, NPV])
    rowp_in = din("rowp", [DEPTH, 128, 32])
    consts_in = din("consts", [128, NCONST])
    rope_in = din("rope", [2, 128, T])
    lvmask_in = din("lvmask", [128, 14, 128])
    rw_w2 = din("rw_w2", [DEPTH, 128, 512])
    rw_a2 = din("rw_a2", [DEPTH, 128, 512])
    rw_g2 = din("rw_g2", [DEPTH, 128, 512])
    p_attn = din("p_attn", [DEPTH, 512, D])
    p_rwkv = din("p_rwkv", [DEPTH, 512, D])
    p_ssm = din("p_ssm", [DEPTH, 512, D])
    w_out = din("w_out", [DEPTH, D, D])
    ffn_w1 = din("ffn_w1", [DEPTH, D, FFN])
    ffn_w3 = din("ffn_w3", [DEPTH, D, FFN])
    ffn_w2 = din("ffn_w2", [DEPTH, FFN, D])
    y_out = nc.dram_tensor("y", [SEQ, D], F32, kind="ExternalOutput").ap()

    XT = dscr("XT", [D, T])
    QR = dscr("QR", [512, T], BF16)
    KR = dscr("KR", [512, T], BF16)
    VA = dscr("VA", [T, 512], BF16)
    RW = dscr("RW", [RWC, T])
    ZS = dscr("ZS", [512, T])
    XBC = dscr("XBC", [1024, T])
    DTS = dscr("DTS", [T, 16])
    GT = dscr("GT", [3072, T])
    YA = dscr("YA", [512, T], BF16)
    YR = dscr("YR", [512, T], BF16)
    YS = dscr("YS", [512, T], BF16)

    st = ExitStack()
    P = Prog(nc, st)
    V, A, G, PE = nc.vector, nc.scalar, nc.gpsimd, nc.tensor

    def blocks(a, b, n=512):
        out = []
        s = a
        while s < b:
            m = min(n, b - s)
            out.append((s, m))
            s += m
        return out
    tblocks = blocks(0, CTX) + blocks(CTX, T)
    streams = [(0, CTX), (CTX, T)]

    consts = P.sb("consts", [128, NCONST])
    cbf = P.sb("cbf", [128, NCONST], BF16)
    pv = P.sb("pv", [128, NPV])
    rowp = P.sb("rowp", [128, 32])
    mod = P.sb("mod", [128, 96])
    onep = P.sb("onep", [128, 96])
    PS = [P.ps("ps%d" % i, [128, 512]) for i in range(8)]
    psi = [0]

    def nps():
        t = PS[psi[0] % 8]
        psi[0] += 1
        return t

    ident = consts[:, CO["ident"]:CO["ident"] + 128]
    onesF = consts[:, CO["ones"]:CO["ones"] + 128]
    bo64F = consts[:, CO["bo64"]:CO["bo64"] + 128]
    onesB = cbf[:, CO["ones"]:CO["ones"] + 128]

    P.dma("sp", consts[:], consts_in, writes=[consts])
    P.op("dve", lambda: V.tensor_copy(out=cbf[:], in_=consts[:]), reads=[consts], writes=[cbf])

    XTb = [P.buf("XT%d" % c) for c in range(8)]
    XTv = XT.rearrange("(c p) t -> p c t", p=128)

    scX = P.scope()
    scX.__enter__()
    xin = [P.sb("xin%d" % i, [128, D]) for i in range(2)]
    xo = [P.sb("xo%d" % i, [128, 8, 128]) for i in range(2)]
    for i in range(NT):
        src = ctx_in[i * 128:(i + 1) * 128, :] if i < NCT else x_in[(i - NCT) * 128:(i - NCT + 1) * 128, :]
        xi, xoo = xin[i % 2], xo[i % 2]
        P.dma("sp", xi[:], src, writes=[xi])
        for half in range(2):
            ps = nps()
            for q in range(4):
                c = half * 4 + q
                P.op("pe", lambda c=c, q=q, ps=ps: PE.transpose(out=ps[:, q * 128:(q + 1) * 128], in_=xi[:, c * 128:(c + 1) * 128], identity=ident),
                     reads=[xi, consts], pwrites=[ps] if q else [], writes=[] if q else [ps])
            eng = "act" if half else "dve"
            if half:
                P.op("act", lambda ps=ps: A.copy(out=xoo[:, 4:8, :], in_=ps[:].rearrange("p (q t) -> p q t", q=4)), reads=[ps], pwrites=[xoo])
            else:
                P.op("dve", lambda ps=ps: V.tensor_copy(out=xoo[:, 0:4, :], in_=ps[:].rearrange("p (q t) -> p q t", q=4)), reads=[ps], writes=[xoo])
        P.dma("sp", XTv[:, :, i * 128:(i + 1) * 128], xoo[:], reads=[xoo], pwrites=XTb)
    scX.__exit__(None, None, None)
    def acopy(out, in_):
        return A.activation(out=out, in_=in_, func=AF.Copy)

    scc = P.sb("scc", [128, 16])
    epst = P.sb("epst", [128, 4])
    P.op("dve", lambda: V.memset(epst[:, 0:1], 1e-5), writes=[epst])
    P.op("dve", lambda: V.memset(epst[:, 1:2], 64e-5), pwrites=[epst])
    P.op("dve", lambda: V.memset(epst[:, 2:3], 0.0), pwrites=[epst])
    epsc = epst[:, 0:1]
    modL = [P.sb("modL%d" % l, [128, 96]) for l in range(DEPTH)]
    pvx = P.sb("pvx", [128, 32])
    DB = {n: P.buf(n) for n in ("QR", "KR", "VA", "RW", "ZS", "XBC", "DTS", "GT", "YA", "YR", "YS")}
    scM = P.scope()
    scM.__enter__()
    P.dma("sp", scc[:], cc_in, writes=[scc])
    P.op("act", lambda: A.activation(out=scc[:], in_=scc[:], func=AF.Silu), reads=[scc], writes=[scc])
    awb = [P.sb("awb%d" % i, [128, 8, 512]) for i in range(2)]
    for l in range(DEPTH):
        P.dma("sp", pv[:], pvec_in[l], writes=[pv])
        psm = nps()
        for blk in range(12):
            aw = awb[blk % 2]
            P.dma("sp", aw[:], ada_w[l, :, blk * 512:(blk + 1) * 512].rearrange("(k p) n -> p k n", p=128), writes=[aw])
            for jj in range(4):
                j = blk * 4 + jj
                for k in range(8):
                    P.op("pe", lambda jj=jj, j=j, k=k, aw=aw: PE.matmul(psm[:, j * 2:j * 2 + 2], lhsT=aw[:, k, jj * 128:(jj + 1) * 128],
                                                                     rhs=scc[:, k * 2:k * 2 + 2], start=(k == 0), stop=(k == 7)),
                         reads=[aw, scc], writes=[psm] if (j == 0 and k == 0) else [], pwrites=[] if (j == 0 and k == 0) else [psm])
        ml = modL[l]
        for n in range(2):
            P.op("dve", lambda n=n, ml=ml: V.tensor_tensor(out=ml[:].rearrange("p (j n) -> p j n", n=2)[:, :, n],
                                                        in0=psm[:, 0:96].rearrange("p (j n) -> p j n", n=2)[:, :, n],
                                                        in1=pv[:, PV["adab"]:PV["adab"] + 48], op=ALU.add),
                 reads=[psm, pv], pwrites=[ml])

    scM.__exit__(None, None, None)
    cnt = {"w": 0, "row": 0, "obf": 0, "ev": 0}

    def load_cast(dst, dview, src, stg):
        sg, sview = stg
        cnt["w"] += 1
        q = "sp"
        P.dma(q, sview, src, writes=[sg])
        P.op("pool", lambda: G.tensor_copy(out=dview, in_=sview), reads=[sg], writes=[dst])

    def fm_project(wsrc, row, func, Hsrc, wblk):
        wb = wblk[cnt["w"] % 4]
        sg = wblk[4 + cnt["w"] % 2]
        load_cast(wb, wb[:], wsrc.rearrange("(k p) n -> p k n", p=128), (sg, sg[:]))
        first = True
        for (s, n) in tblocks:
            ps = nps()
            for k in range(8):
                P.op("pe", lambda k=k, s=s, n=n, ps=ps: PE.matmul(ps[:, :n], lhsT=wb[:, k, :], rhs=Hsrc[:, k, s:s + n], start=(k == 0), stop=(k == 7)),
                     reads=[wb, Hsrc], writes=[ps] if k == 0 else [], pwrites=[] if k == 0 else [ps])
            use_act = (func != AF.Copy) or (cnt["ev"] % 2 == 0)
            cnt["ev"] += 1
            if use_act:
                P.op("act", lambda s=s, n=n, ps=ps: A.activation(out=row[:, s:s + n], in_=ps[:, :n], func=func),
                     reads=[ps], writes=[row] if first else [], pwrites=[] if first else [row])
            else:
                P.op("dve", lambda s=s, n=n, ps=ps: V.tensor_copy(out=row[:, s:s + n], in_=ps[:, :n]),
                     reads=[ps], writes=[row] if first else [], pwrites=[] if first else [row])
            first = False

    def layer_front(l):
        ml = modL[l]
        HT = P.sb("HT", [128, 8, T], BF16)
        CTt = P.sb("CTt", [128, T])
        STt = P.sb("STt", [128, T])
        P.dma("sp", CTt[:], rope_in[0], writes=[CTt])
        P.dma("sp", STt[:], rope_in[1], writes=[STt])
        rows = [P.sb("row%d" % i, [128, T]) for i in range(3)]
        obf = [P.sb("obf%d" % i, [128, T], BF16) for i in range(1)]
        wblk = [P.sb("wblk%d" % i, [128, 8, 128], BF16) for i in range(4)] + [P.sb("wstg%d" % i, [128, 8, 128]) for i in range(2)]
        wtm = P.sb("wtm", [128, 8, 512], BF16)
        wtms = P.sb("wtms", [128, 4, 512])
        dtws = P.sb("dtws", [128, 8, 16])
        vst = [P.sb("vst0", [128, 512], BF16), P.sb("vst1", [128, 512], BF16)]
        dtw = P.sb("dtw", [128, 8, 16], BF16)
        dst_ = [P.sb("dst0", [128, 16]), P.sb("dst1", [128, 16])]
        P.dma("sp", pv[:], pvec_in[l], writes=[pv])
        P.dma("sp", rowp[:], rowp_in[l], writes=[rowp])
        P.op("dve", lambda: V.tensor_scalar(out=onep[:], in0=ml[:], scalar1=1.0, scalar2=None, op0=ALU.add), reads=[ml], writes=[onep])
        P.op("dve", lambda: V.tensor_scalar(out=pvx[:, 0:15], in0=pv[:, PV["mu"]:PV["mu"] + 15], scalar1=-1.0, scalar2=1.0, op0=ALU.mult, op1=ALU.add),
             reads=[pv], writes=[pvx])
        P.op("dve", lambda: V.tensor_scalar(out=pvx[:, 15:30], in0=pv[:, PV["mu"]:PV["mu"] + 15], scalar1=0.5, scalar2=None, op0=ALU.mult),
             reads=[pv], pwrites=[pvx])
        for c in range(8):
            xr = rows[c % 3]
            P.dma("sp", xr[:], XT[c * 128:(c + 1) * 128, :], reads=[XTb[c]], writes=[xr])
            for n_, (a, b) in enumerate(((CTX, T), (0, CTX))):
                P.op("act", lambda c=c, n_=n_, a=a, b=b, xr=xr: A.activation(out=HT[:, c, a:b], in_=xr[:, a:b], func=AF.Identity,
                                                                          bias=ml[:, c * 2 + n_:c * 2 + n_ + 1],
                                                                          scale=onep[:, (8 + c) * 2 + n_:(8 + c) * 2 + n_ + 1]),
                     reads=[xr, ml, onep], pwrites=[HT] if (c or n_) else [], writes=[] if (c or n_) else [HT])
        wl = w_in[l]
        wp = w_perm[l]
        ri = [0]

        def nrow():
            t = rows[ri[0] % 3]
            ri[0] += 1
            return t

        def nobf():
            t = obf[0]
            cnt["obf"] += 1
            return t
        for which, (c0, dst) in enumerate(((0, QR), (512, KR))):
            for h in range(4):
                ra, rb = nrow(), nrow()
                fm_project(wl[:, c0 + h * 128:c0 + (h + 1) * 128], ra, AF.Copy, HT, wblk)
                fm_project(wp[:, which * 512 + h * 128:which * 512 + (h + 1) * 128], rb, AF.Copy, HT, wblk)
                ob = nobf()
                P.op("dve", lambda ra=ra: V.tensor_tensor(out=ra[:], in0=ra[:], in1=CTt[:], op=ALU.mult), reads=[ra, CTt], writes=[ra])
                P.op("pool", lambda rb=rb: G.tensor_tensor(out=rb[:], in0=rb[:], in1=STt[:], op=ALU.mult), reads=[rb, STt], writes=[rb])
                P.op("dve", lambda ra=ra, rb=rb, ob=ob: V.tensor_tensor(out=ob[:], in0=ra[:], in1=rb[:], op=ALU.add), reads=[ra, rb], writes=[ob])
                P.dma("sp", dst[h * 128:(h + 1) * 128, :], ob[:], reads=[ob], pwrites=[DB["KR" if which else "QR"]])
        for kh in range(2):
            load_cast(wtm, wtm[:, kh * 4:(kh + 1) * 4, :], wl[kh * 512:(kh + 1) * 512, 1024:1536].rearrange("(k p) n -> p k n", p=128), (wtms, wtms[:]))
        load_cast(dtw, dtw[:], wl[:, 4992:5008].rearrange("(k p) n -> p k n", p=128), (dtws, dtws[:]))
        for i in range(NT):
            ps = nps()
            for k in range(8):
                P.op("pe", lambda k=k, i=i, ps=ps: PE.matmul(ps[:, :], lhsT=HT[:, k, i * 128:(i + 1) * 128], rhs=wtm[:, k, :], start=(k == 0), stop=(k == 7)),
                     reads=[wtm, HT], writes=[ps] if k == 0 else [], pwrites=[] if k == 0 else [ps])
            vs = vst[i % 2]
            P.op("act", lambda ps=ps, vs=vs: acopy(vs[:], ps[:]), reads=[ps], writes=[vs])
            P.dma("sp", VA[i * 128:(i + 1) * 128, :], vs[:], reads=[vs], pwrites=[DB["VA"]])
            ps2 = nps()
            for k in range(8):
                P.op("pe", lambda k=k, i=i, ps2=ps2: PE.matmul(ps2[:, 0:16], lhsT=HT[:, k, i * 128:(i + 1) * 128], rhs=dtw[:, k, :], start=(k == 0), stop=(k == 7)),
                     reads=[dtw, HT], writes=[ps2] if k == 0 else [], pwrites=[] if k == 0 else [ps2])
            ds = dst_[i % 2]
            P.op("dve", lambda ps2=ps2, ds=ds: V.tensor_tensor(out=ds[:], in0=ps2[:, 0:16], in1=rowp[:, 0:16], op=ALU.add), reads=[ps2, rowp], writes=[ds])
            P.op("act", lambda ds=ds: A.activation(out=ds[:], in_=ds[:], func=AF.Exp), reads=[ds], writes=[ds])
            P.op("act", lambda ds=ds: A.activation(out=ds[:], in_=ds[:], func=AF.Ln, bias=1.0), reads=[ds], writes=[ds])
            P.dma("sp", DTS[i * 128:(i + 1) * 128, :], ds[:], reads=[ds], pwrites=[DB["DTS"]])
        for j in range(15):
            ra = nrow()
            sb_ = nrow()
            fm_project(wl[:, 1536 + j * 128:1536 + (j + 1) * 128], ra, AF.Copy, HT, wblk)
            for si, (a, b) in enumerate(streams):
                P.op("pool", lambda a=a, b=b, ra=ra, sb_=sb_: G.tensor_tensor(out=sb_[:, a + 1:b - 1], in0=ra[:, a:b - 2], in1=ra[:, a + 2:b], op=ALU.add),
                     reads=[ra], writes=[sb_] if si == 0 else [], pwrites=[] if si == 0 else [sb_])
                P.op("pool", lambda a=a, b=b, ra=ra, sb_=sb_: G.tensor_copy(out=sb_[:, a:a + 1], in_=ra[:, a + 1:a + 2]), reads=[ra], pwrites=[sb_])
                P.op("pool", lambda a=a, b=b, ra=ra, sb_=sb_: G.tensor_copy(out=sb_[:, b - 1:b], in_=ra[:, b - 2:b - 1]), reads=[ra], pwrites=[sb_])
            P.op("dve", lambda j=j, ra=ra: V.tensor_scalar(out=ra[:], in0=ra[:], scalar1=pvx[:, j:j + 1], scalar2=None, op0=ALU.mult), reads=[ra, pvx], writes=[ra])
            P.op("dve", lambda j=j, ra=ra, sb_=sb_: V.scalar_tensor_tensor(out=ra[:], in0=sb_[:], scalar=pvx[:, 15 + j:16 + j], in1=ra[:], op0=ALU.mult, op1=ALU.add),
                 reads=[ra, sb_, pvx], writes=[ra])
            P.dma("sp", RW[j * 128:(j + 1) * 128, :], ra[:], reads=[ra], pwrites=[DB["RW"]])
        for j in range(4):
            ra = nrow()
            fm_project(wl[:, 3456 + j * 128:3456 + (j + 1) * 128], ra, AF.Silu, HT, wblk)
            P.dma("sp", ZS[j * 128:(j + 1) * 128, :], ra[:], reads=[ra], pwrites=[DB["ZS"]])
        for j in range(8):
            ra = nrow()
            o_ = nrow()
            fm_project(wl[:, 3968 + j * 128:3968 + (j + 1) * 128], ra, AF.Copy, HT, wblk)
            cw = PV["cw"]
            P.op("dve", lambda j=j, ra=ra, o_=o_: V.tensor_scalar(out=o_[:], in0=ra[:], scalar1=pv[:, cw + 8 + j:cw + 9 + j], scalar2=pv[:, PV["cb"] + j:PV["cb"] + j + 1],
                                                               op0=ALU.mult, op1=ALU.add), reads=[ra, pv], writes=[o_])
            for (a, b) in streams:
                P.op("dve", lambda j=j, a=a, b=b, ra=ra, o_=o_: V.scalar_tensor_tensor(out=o_[:, a + 1:b], in0=ra[:, a:b - 1], scalar=pv[:, cw + j:cw + j + 1], in1=o_[:, a + 1:b],
                                                                                    op0=ALU.mult, op1=ALU.add), reads=[ra, pv, o_], writes=[o_])
                P.op("dve", lambda j=j, a=a, b=b, ra=ra, o_=o_: V.scalar_tensor_tensor(out=o_[:, a:b - 1], in0=ra[:, a + 1:b], scalar=pv[:, cw + 16 + j:cw + 17 + j], in1=o_[:, a:b - 1],
                                                                                    op0=ALU.mult, op1=ALU.add), reads=[ra, pv, o_], writes=[o_])
            P.op("act", lambda o_=o_: A.activation(out=o_[:], in_=o_[:], func=AF.Silu), reads=[o_], writes=[o_])
            P.dma("sp", XBC[j * 128:(j + 1) * 128, :], o_[:], reads=[o_], pwrites=[DB["XBC"]])
        for j in range(24):
            ra = nrow()
            fm_project(wl[:, 5008 + j * 128:5008 + (j + 1) * 128], ra, AF.Sigmoid, HT, wblk)
            P.dma("sp", GT[j * 128:(j + 1) * 128, :], ra[:], reads=[ra], pwrites=[DB["GT"]])
    def layer_attn(l):
        lam_init = lam_init_of(l)
        dal = PV["dal"]
        lam = P.sb("lam", [128, 4])
        prod = P.sb("prod", [128, 2])
        P.op("dve", lambda: V.tensor_tensor(out=prod[0:64, 0:1], in0=pv[0:64, dal:dal + 1], in1=pv[0:64, dal + 1:dal + 2], op=ALU.mult), reads=[pv], writes=[prod])
        P.op("dve", lambda: V.tensor_tensor(out=prod[0:64, 1:2], in0=pv[0:64, dal + 2:dal + 3], in1=pv[0:64, dal + 3:dal + 4], op=ALU.mult), reads=[pv], pwrites=[prod])
        psl = PS[0]
        P.op("pe", lambda: PE.matmul(psl[:, 0:2], lhsT=consts[0:64, CO["ones"]:CO["ones"] + 128], rhs=prod[0:64, 0:2], start=True, stop=True), reads=[consts, prod], writes=[psl])
        P.op("act", lambda: A.activation(out=lam[:, 0:2], in_=psl[:, 0:2], func=AF.Exp), reads=[psl], writes=[lam])
        P.op("dve", lambda: V.scalar_tensor_tensor(out=lam[:, 2:3], in0=lam[:, 1:2], scalar=-lam_init, in1=lam[:, 0:1], op0=ALU.add, op1=ALU.subtract), reads=[lam], pwrites=[lam])
        neglam = lam[:, 2:3]
        KRh = [P.sb("KRh%d" % i, [128, T], BF16) for i in range(2)]
        QRh = [P.sb("QRh%d" % i, [128, T], BF16) for i in range(2)]
        Vh = [P.sb("Vh%d" % i, [128, NT, 128], BF16) for i in range(2)]
        ee = [[P.sb("e%d_%d" % (m, i), [128, 512], BF16) for i in range(2)] for m in range(2)]
        r0 = P.sb("r0", [128, 512])
        r1 = P.sb("r1", [128, 512])
        t0 = P.sb("t0", [128, 512])
        t1 = P.sb("t1", [128, 512])
        sq = P.sb("sq", [128, 512])
        yo = [P.sb("yo%d" % i, [128, 512], BF16) for i in range(2)]
        acc = [PS[4], PS[5]]
        zz = [PS[6], PS[7]]
        VAv = VA.rearrange("(i p) e -> p i e", p=128)
        qbs = [(s, n, NCT) for (s, n) in blocks(0, CTX)] + [(s, n, NT) for (s, n) in blocks(CTX, T)]
        step = 0
        for h in range(4):
            kr, qr, vh = KRh[h % 2], QRh[h % 2], Vh[h % 2]
            P.dma("sp", kr[:], KR[h * 128:(h + 1) * 128, :], reads=[DB["KR"]], writes=[kr])
            P.dma("sp", qr[:], QR[h * 128:(h + 1) * 128, :], reads=[DB["QR"]], writes=[qr])
            P.dma("sp", vh[:], VAv[:, :, h * 128:(h + 1) * 128], reads=[DB["VA"]], writes=[vh])
            for bi, (s, n, nk) in enumerate(qbs):
                def emit_s(kt, pr):
                    pss = [PS[pr * 2], PS[pr * 2 + 1]]
                    es = [ee[0][pr], ee[1][pr]]
                    for m in range(2):
                        P.op("pe", lambda m=m: PE.matmul(pss[m][:, :n], lhsT=kr[m * 64:(m + 1) * 64, kt * 128:(kt + 1) * 128],
                                                        rhs=qr[m * 64:(m + 1) * 64, s:s + n], start=True, stop=True),
                             reads=[kr, qr], writes=[pss[m]])
                    for m in range(2):
                        P.op("act", lambda m=m: A.activation(out=es[m][:, :n], in_=pss[m][:, :n], func=AF.Exp, scale=0.125),
                             reads=[pss[m]], writes=[es[m]])

                def emit_av(kt, pr):
                    es = [ee[0][pr], ee[1][pr]]
                    first = (kt == 0)
                    for m in range(2):
                        P.op("pe", lambda m=m: PE.matmul(acc[m][:, :n], lhsT=vh[:, kt, :], rhs=es[m][:, :n], start=(kt == 0), stop=(kt == nk - 1)),
                             reads=[vh, es[m]], writes=[acc[m]] if first else [], pwrites=[] if first else [acc[m]])
                        P.op("pe", lambda m=m: PE.matmul(zz[m][:, :n], lhsT=onesB, rhs=es[m][:, :n], start=(kt == 0), stop=(kt == nk - 1)),
                             reads=[cbf, es[m]], writes=[zz[m]] if first else [], pwrites=[] if first else [zz[m]])

                emit_s(0, step % 2)
                for kt in range(nk):
                    pr = step % 2
                    step += 1
                    if kt + 1 < nk:
                        emit_s(kt + 1, step % 2)
                    emit_av(kt, pr)
                P.op("dve", lambda n=n: V.reciprocal(out=r0[:, :n], in_=zz[0][:, :n]), reads=[zz[0]], writes=[r0])
                P.op("dve", lambda n=n: V.tensor_tensor(out=t0[:, :n], in0=acc[0][:, :n], in1=r0[:, :n], op=ALU.mult), reads=[acc[0], r0], writes=[t0])
                P.op("dve", lambda n=n: V.reciprocal(out=r1[:, :n], in_=zz[1][:, :n]), reads=[zz[1]], writes=[r1])
                P.op("dve", lambda n=n: V.tensor_tensor(out=t1[:, :n], in0=acc[1][:, :n], in1=r1[:, :n], op=ALU.mult), reads=[acc[1], r1], writes=[t1])
                P.op("dve", lambda n=n: V.scalar_tensor_tensor(out=t0[:, :n], in0=t1[:, :n], scalar=neglam, in1=t0[:, :n], op0=ALU.mult, op1=ALU.add),
                     reads=[t0, t1, lam], writes=[t0])
                P.op("act", lambda n=n: A.activation(out=sq[:, :n], in_=t0[:, :n], func=AF.Square), reads=[t0], writes=[sq])
                pst = PS[(step % 2) * 2]
                P.op("pe", lambda n=n, pst=pst: PE.matmul(pst[:, :n], lhsT=onesF, rhs=sq[:, :n], start=True, stop=True), reads=[consts, sq], writes=[pst])
                P.op("act", lambda n=n, pst=pst: A.activation(out=r0[:, :n], in_=pst[:, :n], func=AF.Sqrt, scale=1.0 / 128, bias=epsc), reads=[pst, epst], writes=[r0])
                P.op("dve", lambda n=n: V.reciprocal(out=r0[:, :n], in_=r0[:, :n]), reads=[r0], writes=[r0])
                P.op("dve", lambda n=n: V.tensor_tensor(out=t0[:, :n], in0=t0[:, :n], in1=r0[:, :n], op=ALU.mult), reads=[t0, r0], writes=[t0])
                yy = yo[bi % 2]
                P.op("dve", lambda n=n, yy=yy: V.tensor_scalar(out=yy[:, :n], in0=t0[:, :n], scalar1=pv[:, PV["dag"]:PV["dag"] + 1], scalar2=1.0 - lam_init,
                                                            op0=ALU.mult, op1=ALU.mult), reads=[t0, pv], writes=[yy])
                P.dma("sp", YA[h * 128:(h + 1) * 128, s:s + n], yy[:, :n], reads=[yy], pwrites=[DB["YA"]])
    _RWSTOP = globals().get("RWSTOP")
    BG = dscr("BG", [512, T])
    GG = dscr("GG", [512, T])
    DB["BG"] = P.buf("BG")
    DB["GG"] = P.buf("GG")

    def layer_rwkv(l):
        sgf = P.sb("sgf", [128, 512])
        w2b = P.sb("w2b", [128, 512], BF16)
        a2b = P.sb("a2b", [128, 512], BF16)
        g2b = P.sb("g2b", [128, 512], BF16)
        for dst_t, src in ((w2b, rw_w2[l]), (a2b, rw_a2[l]), (g2b, rw_g2[l])):
            P.dma("sp", sgf[:], src, writes=[sgf])
            P.op("pool", lambda dst_t=dst_t: G.tensor_copy(out=dst_t[:], in_=sgf[:]), reads=[sgf], writes=[dst_t])
        omka = P.sb("omka", [128, 4])
        P.op("dve", lambda: V.tensor_scalar(out=omka[:], in0=pv[:, PV["ka"]:PV["ka"] + 4], scalar1=-1.0, scalar2=1.0, op0=ALU.mult, op1=ALU.add), reads=[pv], writes=[omka])
        mk = {}
        for nm in ("msl", "msu", "ident", "bo64"):
            mt = P.sb("mk_" + nm, [128, 512])
            for q in range(4):
                P.op("pool", lambda q=q, mt=mt, nm=nm: G.tensor_copy(out=mt[:, q * 128:(q + 1) * 128], in_=consts[:, CO[nm]:CO[nm] + 128]), reads=[consts], pwrites=[mt])
            mk[nm] = mt
        lvs = P.sb("lvs", [128, 2, 128])
        lvm = P.sb("lvm", [128, 14, 512], BF16)
        for hf in range(7):
            P.dma("sp", lvs[:], lvmask_in[:, hf * 2:(hf + 1) * 2, :], writes=[lvs])
            for q in range(4):
                P.op("pool", lambda q=q, hf=hf: G.tensor_copy(out=lvm[:, hf * 2:(hf + 1) * 2, q * 128:(q + 1) * 128], in_=lvs[:]), reads=[lvs], pwrites=[lvm])
        Rd = P.sb("Rd", [128, T], BF16)
        Bi = P.sb("Bi", [128, T], BF16)
        Ki = P.sb("Ki", [128, T], BF16)
        KKd = P.sb("KKd", [128, T], BF16)
        BiT = P.sb("BiT", [128, NT, 128], BF16)
        KiT = P.sb("KiT", [128, NT, 128], BF16)
        VZ = P.sb("VZ", [128, NT, 2, 128], BF16)
        GCt = P.sb("GCt", [128, NT])
        YT = P.sb("YT", [128, T])
        TT = P.sb("TT", [128, NT, 256], BF16)
        tn = ["r32", "k32", "v32", "wd32", "ad32", "gd32", "kk32", "t32", "rn32", "p32", "a32", "kd32", "kb32", "b32", "pre", "arr", "ex", "x32"]
        tm = {n_: P.sb(n_, [128, 512]) for n_ in tn}
        twb = P.sb("twb", [128, 512], BF16)
        adb = P.sb("adb", [128, 512], BF16)
        sgb = P.sb("sgb", [128, 512], BF16)
        p1sets = [{nm_: P.sb("%s_%d" % (nm_, k_), [128, 512], BF16) for nm_ in ("Nb", "NTb", "Ok", "OTk", "Wb", "W2b", "Tm", "TTm")} for k_ in range(2)]
        AKL2 = [[P.sb("AKL%d_%d" % (k_, e), [128, 384], BF16) for e in range(2)] for k_ in range(2)]
        mcat = [P.sb("mcat%d" % i, [128, 384]) for i in range(2)]
        RHSb = P.sb("RHSb", [128, 128], BF16)
        UZ = P.sb("UZ", [128, 2, 128], BF16)
        M32 = P.sb("M32", [128, 128])
        Mt32 = P.sb("Mt32", [128, 128])
        Mb = P.sb("Mb", [128, 128], BF16)
        yob = [P.sb("yrb%d" % i, [128, 512], BF16) for i in range(2)]

        def dv(fn, reads, writes=(), pwrites=()):
            return P.op("dve", fn, reads=reads, writes=writes, pwrites=pwrites)

        def ac(fn, reads, writes=(), pwrites=()):
            return P.op("act", fn, reads=reads, writes=writes, pwrites=pwrites)

        def pe(fn, reads, writes=(), pwrites=()):
            return P.op("pe", fn, reads=reads, writes=writes, pwrites=pwrites)

        def po(fn, reads, writes=(), pwrites=()):
            return P.op("pool", fn, reads=reads, writes=writes, pwrites=pwrites)

        for i_, (a_, b_) in enumerate((("msu", "miu"), ("msl", "mil"))):
            po(lambda i_=i_, a_=a_: G.tensor_copy(out=mcat[i_][:, 0:128], in_=consts[:, CO[a_]:CO[a_] + 128]), [consts], [], [mcat[i_]])
            po(lambda i_=i_, b_=b_: G.tensor_copy(out=mcat[i_][:, 128:256], in_=consts[:, CO[b_]:CO[b_] + 128]), [consts], [], [mcat[i_]])
            po(lambda i_=i_, b_=b_: G.tensor_copy(out=mcat[i_][:, 256:384], in_=consts[:, CO[b_]:CO[b_] + 128]), [consts], [], [mcat[i_]])
        po(lambda: G.memset(UZ[:], 0.0), [], [UZ])
        po(lambda: G.memset(VZ[:], 0.0), [], [VZ])

        for hp in range(4):
            hc = slice(hp * 128, (hp + 1) * 128)
            for d in range(2):
                dh = slice(d * 64, (d + 1) * 64)
                sgc = -1.0 if d == 0 else 1.0
                for bi_, (s, n) in enumerate(tblocks):
                    nch = n // 128
                    t0_ = s // 128
                    r32, k32, v32, wd32, ad32, gd32 = (tm[x] for x in ("r32", "k32", "v32", "wd32", "ad32", "gd32"))
                    kk32, t32, rn32, p32, a32, kd32, kb32, b32, pre, arr, ex, x32 = (tm[x] for x in ("kk32", "t32", "rn32", "p32", "a32", "kd32", "kb32", "b32", "pre", "arr", "ex", "x32"))
                    lds = [(r32, hp * 128), (k32, 512 + hp * 128), (wd32, 1536), (ad32, 1664)]
                    if d == 0:
                        lds += [(v32, 1024 + hp * 128), (gd32, 1792)]
                    for ti, (dst_t, r0_) in enumerate(lds):
                        P.dma("sp", dst_t[:, :n], RW[r0_:r0_ + 128, s:s + n], reads=[DB["RW"]], writes=[dst_t])
                    ac(lambda: A.activation(out=twb[:, :n], in_=wd32[:, :n], func=AF.Tanh), [wd32], [twb])
                    ac(lambda: A.activation(out=adb[:, :n], in_=ad32[:, :n], func=AF.Copy), [ad32], [adb])
                    dv(lambda: V.tensor_scalar(out=kk32[:, :n], in0=k32[:, :n], scalar1=pv[:, PV["kk"] + hp:PV["kk"] + hp + 1], scalar2=None, op0=ALU.mult), [k32, pv], [kk32])
                    ac(lambda: A.activation(out=t32[:, :n], in_=kk32[:, :n], func=AF.Square), [kk32], [t32])
                    ps = nps()
                    pe(lambda ps=ps: PE.matmul(ps[:, :n], lhsT=bo64F, rhs=t32[:, :n], start=True, stop=True), [consts, t32], [ps])
                    dv(lambda ps=ps: V.tensor_scalar(out=rn32[:, :n], in0=ps[:, :n], scalar1=1e-24, scalar2=None, op0=ALU.max), [ps], [rn32])
                    ac(lambda: A.activation(out=rn32[:, :n], in_=rn32[:, :n], func=AF.Sqrt), [rn32], [rn32])
                    dv(lambda: V.reciprocal(out=rn32[:, :n], in_=rn32[:, :n]), [rn32], [rn32])
                    dv(lambda: V.tensor_tensor(out=kk32[:, :n], in0=kk32[:, :n], in1=rn32[:, :n], op=ALU.mult), [kk32, rn32], [kk32])

                    def a_and_kd(dd, a_out, kd_out):
                        ddh = slice(dd * 64, (dd + 1) * 64)
                        ps_ = nps()
                        pe(lambda: PE.matmul(ps_[:, :n], lhsT=a2b[ddh, hc], rhs=adb[ddh, :n], start=True, stop=True), [a2b, adb], [ps_])
                        ac(lambda: A.activation(out=a_out[:, :n], in_=ps_[:, :n], func=AF.Sigmoid, bias=pv[:, PV["a0"] + dd * 4 + hp:PV["a0"] + dd * 4 + hp + 1]), [ps_, pv], [a_out])
                        dv(lambda: V.tensor_scalar(out=t32[:, :n], in0=a_out[:, :n], scalar1=pv[:, PV["ka"] + hp:PV["ka"] + hp + 1], scalar2=omka[:, hp:hp + 1], op0=ALU.mult, op1=ALU.add),
                           [a_out, pv, omka], [t32])
                        dv(lambda: V.tensor_tensor(out=kd_out[:, :n], in0=t32[:, :n], in1=k32[:, :n], op=ALU.mult), [t32, k32], [kd_out])

                    if d == 0:
                        ac(lambda: A.activation(out=sgb[:, :n], in_=gd32[:, :n], func=AF.Sigmoid), [gd32], [sgb])
                        ps = nps()
                        pe(lambda ps=ps: PE.matmul(ps[:, :n], lhsT=g2b[:, hc], rhs=sgb[:, :n], start=True, stop=True), [g2b, sgb], [ps])
                        ac(lambda ps=ps: A.activation(out=x32[:, :n], in_=ps[:, :n], func=AF.Copy), [ps], [x32])
                        P.dma("sp", GG[hc, s:s + n], x32[:, :n], reads=[x32], pwrites=[DB["GG"]])
                        ps = nps()
                        for ci in range(nch):
                            pe(lambda ps=ps, ci=ci: PE.transpose(out=ps[:, ci * 128:(ci + 1) * 128], in_=v32[:, ci * 128:(ci + 1) * 128], identity=ident), [v32, consts],
                               [ps] if ci == 0 else [], [] if ci == 0 else [ps])
                        for e in range(2):
                            ac(lambda ps=ps, e=e: A.activation(out=VZ[:, t0_:t0_ + nch, e, e * 64:(e + 1) * 64],
                                                              in_=ps[:, :n].rearrange("p (c t) -> p c t", c=nch)[:, :, e * 64:(e + 1) * 64], func=AF.Copy), [ps], [], [VZ])
                        a_and_kd(1, a32, kb32)
                        a_and_kd(0, a32, kd32)
                        po(lambda: G.tensor_tensor(out=kb32[:, :n], in0=kb32[:, :n], in1=kd32[:, :n], op=ALU.add), [kd32, kb32], [kb32])
                        dv(lambda: V.tensor_tensor(out=x32[:, :n], in0=r32[:, :n], in1=kb32[:, :n], op=ALU.mult), [r32, kb32], [x32])
                        dv(lambda: V.tensor_scalar(out=x32[:, :n], in0=x32[:, :n], scalar1=pv[:, PV["rk"] + hp:PV["rk"] + hp + 1], scalar2=0.5, op0=ALU.mult, op1=ALU.mult), [x32, pv], [x32])
                        ps = nps()
                        pe(lambda ps=ps: PE.matmul(ps[:, :n], lhsT=bo64F, rhs=x32[:, :n], start=True, stop=True), [consts, x32], [ps])
                        dv(lambda ps=ps: V.tensor_tensor(out=x32[:, :n], in0=ps[:, :n], in1=v32[:, :n], op=ALU.mult), [ps, v32], [x32])
                        P.dma("sp", BG[hc, s:s + n], x32[:, :n], reads=[x32], pwrites=[DB["BG"]])
                    else:
                        a_and_kd(1, a32, kd32)
                    ps = nps()
                    pe(lambda ps=ps: PE.matmul(ps[:, :n], lhsT=w2b[dh, hc], rhs=twb[dh, :n], start=True, stop=True), [w2b, twb], [ps])
                    ac(lambda ps=ps: A.activation(out=p32[:, :n], in_=ps[:, :n], func=AF.Sigmoid, bias=pv[:, PV["w0"] + d * 4 + hp:PV["w0"] + d * 4 + hp + 1]), [ps, pv], [p32])
                    dv(lambda: V.tensor_scalar(out=p32[:, :n], in0=p32[:, :n], scalar1=0.6065306597126334, scalar2=None, op0=ALU.mult), [p32], [p32])
                    dv(lambda: V.tensor_tensor(out=b32[:, :n], in0=kk32[:, :n], in1=a32[:, :n], op=ALU.mult), [kk32, a32], [b32])
                    for ci in range(nch):
                        cs = slice(ci * 128, (ci + 1) * 128)
                        dv(lambda cs=cs: V.tensor_tensor_scan(out=pre[:, cs], data0=onesF, data1=p32[:, cs], initial=0.0, op0=ALU.mult, op1=ALU.add), [p32, consts],
                           [pre] if ci == 0 else [], [] if ci == 0 else [pre])
                    for ci in range(nch):
                        ac(lambda ci=ci: A.activation(out=GCt[:, t0_ + ci:t0_ + ci + 1], in_=pre[:, ci * 128 + 127:ci * 128 + 128], func=AF.Exp, scale=-1.0), [pre], [], [GCt])
                    if d == 0:
                        cur = pre
                    else:
                        for ci in range(nch):
                            cs = slice(ci * 128, (ci + 1) * 128)
                            dv(lambda cs=cs, ci=ci: V.scalar_tensor_tensor(out=arr[:, cs], in0=pre[:, cs], scalar=pre[:, ci * 128 + 127:ci * 128 + 128], in1=p32[:, cs],
                                                                         op0=ALU.subtract, op1=ALU.subtract), [pre, p32], [arr] if ci == 0 else [], [] if ci == 0 else [arr])
                        cur = arr
                    ac(lambda cur=cur: A.activation(out=ex[:, :n], in_=cur[:, :n], func=AF.Exp, scale=sgc), [cur], [ex])
                    dv(lambda: V.tensor_tensor(out=Rd[:, s:s + n], in0=r32[:, :n], in1=ex[:, :n], op=ALU.mult), [r32, ex], [] if bi_ else [Rd], [Rd] if bi_ else [])
                    ac(lambda cur=cur: A.activation(out=ex[:, :n], in_=cur[:, :n], func=AF.Exp, scale=-sgc), [cur], [ex])
                    for (src32, dstb, dstT) in ((b32, Bi, BiT), (kd32, Ki, KiT)):
                        dv(lambda src32=src32: V.tensor_tensor(out=t32[:, :n], in0=src32[:, :n], in1=ex[:, :n], op=ALU.mult), [src32, ex], [t32])
                        ac(lambda dstb=dstb: A.activation(out=dstb[:, s:s + n], in_=t32[:, :n], func=AF.Copy), [t32], [] if bi_ else [dstb], [dstb] if bi_ else [])
                        ps = nps()
                        for ci in range(nch):
                            pe(lambda ps=ps, ci=ci: PE.transpose(out=ps[:, ci * 128:(ci + 1) * 128], in_=t32[:, ci * 128:(ci + 1) * 128], identity=ident), [t32, consts],
                               [ps] if ci == 0 else [], [] if ci == 0 else [ps])
                        dv(lambda ps=ps, dstT=dstT: V.tensor_copy(out=dstT[:, t0_:t0_ + nch, :], in_=ps[:, :n].rearrange("p (c t) -> p c t", c=nch)), [ps], [], [dstT])
                    dv(lambda cur=cur: V.scalar_tensor_tensor(out=t32[:, :n], in0=p32[:, :n], scalar=sgc, in1=cur[:, :n], op0=ALU.mult, op1=ALU.add), [p32, cur], [t32])
                    ac(lambda: A.activation(out=ex[:, :n], in_=t32[:, :n], func=AF.Exp, scale=sgc), [t32], [ex])
                    dv(lambda: V.tensor_tensor(out=KKd[:, s:s + n], in0=kk32[:, :n], in1=ex[:, :n], op=ALU.mult), [kk32, ex], [] if bi_ else [KKd], [KKd] if bi_ else [])
                if _RWSTOP == 1:
                    return
                if d == 0:
                    order = list(range(NT))
                    mN, mT, mc = mk["msl"], mk["msu"], mcat[0]
                    lv_o, lv_ot = 0, 7
                else:
                    order = list(range(NCT - 1, -1, -1)) + list(range(NT - 1, NCT - 1, -1))
                    mN, mT, mc = mk["msu"], mk["msl"], mcat[1]
                    lv_o, lv_ot = 7, 0
                def p1_group(g0, B_):
                    grp = list(range(g0, min(g0 + 2, NT)))
                    ng = len(grp)
                    W_ = ng * 128
                    bN = [nps(), nps()]
                    bT = [nps(), nps()]
                    for e in range(2):
                        ph = slice(e * 64, (e + 1) * 64)
                        for cj, c in enumerate(grp):
                            tc = slice(c * 128, (c + 1) * 128)
                            oc = slice(cj * 128, (cj + 1) * 128)
                            pe(lambda e=e, tc=tc, oc=oc, ph=ph: PE.matmul(bN[e][:, oc], lhsT=KKd[ph, tc], rhs=Bi[ph, tc], start=True, stop=True), [KKd, Bi],
                               [bN[e]] if cj == 0 else [], [] if cj == 0 else [bN[e]])
                            pe(lambda e=e, tc=tc, oc=oc, ph=ph: PE.matmul(bT[e][:, oc], lhsT=Bi[ph, tc], rhs=KKd[ph, tc], start=True, stop=True), [KKd, Bi],
                               [bT[e]] if cj == 0 else [], [] if cj == 0 else [bT[e]])
                    yield
                    if ng < 2:
                        po(lambda: G.memset(B_['Nb'][:], 0.0), [], [B_['Nb']])
                        po(lambda: G.memset(B_['NTb'][:], 0.0), [], [B_['NTb']])
                    for e in range(2):
                        full = (e == 0 and ng == 2)
                        dv(lambda e=e: V.tensor_tensor(out=B_['Nb'][:, e * 256:e * 256 + W_], in0=bN[e][:, :W_], in1=mN[:, :W_], op=ALU.mult), [bN[e], mN], [B_['Nb']] if full else [], [] if full else [B_['Nb']])
                        dv(lambda e=e: V.tensor_tensor(out=B_['NTb'][:, e * 256:e * 256 + W_], in0=bT[e][:, :W_], in1=mT[:, :W_], op=ALU.mult), [bT[e], mT], [B_['NTb']] if full else [], [] if full else [B_['NTb']])
                    mats = [(e, cj) for e in range(2) for cj in range(ng)]
                    po(lambda: G.tensor_tensor(out=B_['Ok'][:], in0=B_['Nb'][:], in1=lvm[:, lv_o, :], op=ALU.mult), [B_['Nb'], lvm], [B_['Ok']])
                    po(lambda: G.tensor_tensor(out=B_['OTk'][:], in0=B_['NTb'][:], in1=lvm[:, lv_ot, :], op=ALU.mult), [B_['NTb'], lvm], [B_['OTk']])
                    po(lambda: G.tensor_tensor(out=B_['Tm'][:], in0=mk["ident"][:], in1=B_['Ok'][:], op=ALU.subtract), [mk["ident"], B_['Ok']], [B_['Tm']])
                    po(lambda: G.tensor_tensor(out=B_['TTm'][:], in0=mk["ident"][:], in1=B_['OTk'][:], op=ALU.subtract), [mk["ident"], B_['OTk']], [B_['TTm']])
                    yield
                    for lev in range(1, 7):
                        last = (lev == 6)
                        po(lambda lev=lev: G.tensor_tensor(out=B_['Ok'][:], in0=B_['Nb'][:], in1=lvm[:, lv_o + lev, :], op=ALU.mult), [B_['Nb'], lvm], [B_['Ok']])
                        if not last:
                            po(lambda lev=lev: G.tensor_tensor(out=B_['OTk'][:], in0=B_['NTb'][:], in1=lvm[:, lv_ot + lev, :], op=ALU.mult), [B_['NTb'], lvm], [B_['OTk']])
                        psW2 = nps()
                        for qi, (e, cj) in enumerate(mats):
                            oc = slice(e * 256 + cj * 128, e * 256 + (cj + 1) * 128)
                            pe(lambda oc=oc, psW2=psW2: PE.matmul(psW2[:, oc], lhsT=B_['Ok'][:, oc], rhs=B_['TTm'][:, oc], start=True, stop=True), [B_['Ok'], B_['TTm']], [psW2] if qi == 0 else [], [] if qi == 0 else [psW2])
                        if not last:
                            psW = nps()
                            for qi, (e, cj) in enumerate(mats):
                                oc = slice(e * 256 + cj * 128, e * 256 + (cj + 1) * 128)
                                pe(lambda oc=oc, psW=psW: PE.matmul(psW[:, oc], lhsT=B_['OTk'][:, oc], rhs=B_['Tm'][:, oc], start=True, stop=True), [B_['OTk'], B_['Tm']], [psW] if qi == 0 else [], [] if qi == 0 else [psW])
                        yield
                        ac(lambda psW2=psW2: A.activation(out=B_['W2b'][:], in_=psW2[:], func=AF.Copy), [psW2], [B_['W2b']])
                        if not last:
                            dv(lambda psW=psW: V.tensor_copy(out=B_['Wb'][:], in_=psW[:]), [psW], [B_['Wb']])
                        yield
                        psX2 = nps()
                        for qi, (e, cj) in enumerate(mats):
                            oc = slice(e * 256 + cj * 128, e * 256 + (cj + 1) * 128)
                            pe(lambda oc=oc, psX2=psX2: PE.matmul(psX2[:, oc], lhsT=B_['Tm'][:, oc], rhs=B_['W2b'][:, oc], start=True, stop=True), [B_['Tm'], B_['W2b']], [psX2] if qi == 0 else [], [] if qi == 0 else [psX2])
                        yield
                        if not last:
                            psX = nps()
                            for qi, (e, cj) in enumerate(mats):
                                oc = slice(e * 256 + cj * 128, e * 256 + (cj + 1) * 128)
                                pe(lambda oc=oc, psX=psX: PE.matmul(psX[:, oc], lhsT=B_['TTm'][:, oc], rhs=B_['Wb'][:, oc], start=True, stop=True), [B_['TTm'], B_['Wb']], [psX] if qi == 0 else [], [] if qi == 0 else [psX])
                            dv(lambda psX2=psX2: V.tensor_tensor(out=B_['TTm'][:], in0=B_['TTm'][:], in1=psX2[:], op=ALU.subtract), [psX2, B_['TTm']], [B_['TTm']])
                            dv(lambda psX=psX: V.tensor_tensor(out=B_['Tm'][:], in0=B_['Tm'][:], in1=psX[:], op=ALU.subtract), [psX, B_['Tm']], [B_['Tm']])
                        else:
                            for e in range(2):
                                dv(lambda e=e, psX2=psX2: V.tensor_tensor(out=TT[:, g0:g0 + ng, e * 128:(e + 1) * 128],
                                                                        in0=psX2[:, e * 256:e * 256 + W_].rearrange("p (c t) -> p c t", c=ng),
                                                                        in1=B_['TTm'][:, e * 256:e * 256 + W_].rearrange("p (c t) -> p c t", c=ng), op=ALU.subtract), [psX2, B_['TTm']], [], [TT])
                glist = list(range(0, NT, 2))
                for gi in range(0, len(glist), 2):
                    gens = [p1_group(g0_, p1sets[k_]) for k_, g0_ in enumerate(glist[gi:gi + 2])]
                    while gens:
                        for g_ in list(gens):
                            try:
                                next(g_)
                            except StopIteration:
                                gens.remove(g_)
                if _RWSTOP == 2:
                    return
                dv(lambda: V.memset(M32[:], 0.0), [], [M32])
                dv(lambda: V.memset(Mb[:], 0.0), [], [Mb])
                psR, psU, psY, psM = PS[2], PS[3], PS[4], PS[5]
                psAe = [[PS[0], PS[1]], [PS[6], PS[7]]]

                def emit_akl(c, k):
                    tc = slice(c * 128, (c + 1) * 128)
                    for e in range(2):
                        ph = slice(e * 64, (e + 1) * 64)
                        pa = psAe[k][e]
                        pe(lambda ph=ph, pa=pa: PE.matmul(pa[:, 0:128], lhsT=Ki[ph, tc], rhs=KKd[ph, tc], start=True, stop=True), [Ki, KKd], [pa])
                        pe(lambda ph=ph, pa=pa: PE.matmul(pa[:, 128:256], lhsT=Bi[ph, tc], rhs=Rd[ph, tc], start=True, stop=True), [Bi, Rd], [], [pa])
                        pe(lambda ph=ph, pa=pa: PE.matmul(pa[:, 256:384], lhsT=Ki[ph, tc], rhs=Rd[ph, tc], start=True, stop=True), [Ki, Rd], [], [pa])
                        dv(lambda e=e, pa=pa: V.tensor_tensor(out=AKL2[k][e][:], in0=pa[:, 0:384], in1=mc[:], op=ALU.mult), [pa, mc], [AKL2[k][e]])

                emit_akl(order[0], 0)
                for i_, c in enumerate(order):
                    k = i_ % 2
                    AK = AKL2[k]
                    tc = slice(c * 128, (c + 1) * 128)
                    pe(lambda: PE.matmul(psR[:, 0:128], lhsT=KKd[:, tc], rhs=Mb[:], start=True, stop=False), [KKd, Mb], [psR])
                    for e in range(2):
                        pe(lambda e=e: PE.matmul(psR[:, 0:128], lhsT=AK[e][:, 0:128], rhs=VZ[:, c, e, :], start=False, stop=(e == 1)), [AK[e], VZ], [], [psR])
                    ac(lambda: A.activation(out=RHSb[:], in_=psR[:, 0:128], func=AF.Copy), [psR], [RHSb])
                    if i_ + 1 < len(order):
                        emit_akl(order[i_ + 1], 1 - k)
                    for e in range(2):
                        pe(lambda e=e: PE.matmul(psU[:, e * 64:(e + 1) * 64], lhsT=TT[:, c, e * 128:(e + 1) * 128], rhs=RHSb[:, e * 64:(e + 1) * 64], start=True, stop=True), [TT, RHSb],
                           [psU] if e == 0 else [], [] if e == 0 else [psU])
                    for e in range(2):
                        dv(lambda e=e: V.tensor_copy(out=UZ[:, e, e * 64:(e + 1) * 64], in_=psU[:, e * 64:(e + 1) * 64]), [psU], [], [UZ])
                    pe(lambda: PE.matmul(psM[:, 0:128], lhsT=BiT[:, c, :], rhs=UZ[:, 0, :], start=True, stop=False), [BiT, UZ], [psM])
                    pe(lambda: PE.matmul(psM[:, 0:128], lhsT=BiT[:, c, :], rhs=UZ[:, 1, :], start=False, stop=False), [BiT, UZ], [], [psM])
                    for e in range(2):
                        pe(lambda e=e: PE.matmul(psM[:, 0:128], lhsT=KiT[:, c, :], rhs=VZ[:, c, e, :], start=False, stop=(e == 1)), [KiT, VZ], [], [psM])
                    pe(lambda: PE.matmul(psY[:, 0:128], lhsT=Mb[:], rhs=Rd[:, tc], start=True, stop=False), [Mb, Rd], [psY])
                    for e in range(2):
                        pe(lambda e=e: PE.matmul(psY[:, 0:128], lhsT=UZ[:, e, :], rhs=AK[e][:, 128:256], start=False, stop=False), [UZ, AK[e]], [], [psY])
                        pe(lambda e=e: PE.matmul(psY[:, 0:128], lhsT=VZ[:, c, e, :], rhs=AK[e][:, 256:384], start=False, stop=(e == 1)), [VZ, AK[e]], [], [psY])
                    dv(lambda: V.tensor_tensor(out=Mt32[:], in0=psM[:, 0:128], in1=mk["bo64"][:, 0:128], op=ALU.mult), [psM, mk["bo64"]], [Mt32])
                    dv(lambda: V.tensor_tensor(out=M32[:], in0=M32[:], in1=Mt32[:], op=ALU.add), [M32, Mt32], [M32])
                    dv(lambda: V.tensor_scalar(out=M32[:], in0=M32[:], scalar1=GCt[:, c:c + 1], scalar2=None, op0=ALU.mult), [M32, GCt], [M32])
                    ac(lambda: A.activation(out=Mb[:], in_=M32[:], func=AF.Copy), [M32], [Mb])
                    if d == 0:
                        ac(lambda: A.activation(out=YT[:, tc], in_=psY[:, 0:128], func=AF.Copy), [psY], [], [YT])
                    else:
                        dv(lambda: V.tensor_tensor(out=YT[:, tc], in0=psY[:, 0:128], in1=YT[:, tc], op=ALU.add), [psY, YT], [], [YT])
            if _RWSTOP == 3:
                return
            for bi_, (s, n) in enumerate(tblocks):
                t32, x32, rn32, ex, a32 = tm["t32"], tm["x32"], tm["rn32"], tm["ex"], tm["a32"]
                ps = nps()
                pe(lambda ps=ps: PE.matmul(ps[:, :n], lhsT=bo64F, rhs=YT[:, s:s + n], start=True, stop=True), [consts, YT], [ps])
                dv(lambda ps=ps: V.scalar_tensor_tensor(out=t32[:, :n], in0=ps[:, :n], scalar=-1.0 / 64, in1=YT[:, s:s + n], op0=ALU.mult, op1=ALU.add), [ps, YT], [t32])
                ac(lambda: A.activation(out=x32[:, :n], in_=t32[:, :n], func=AF.Square), [t32], [x32])
                ps = nps()
                pe(lambda ps=ps: PE.matmul(ps[:, :n], lhsT=bo64F, rhs=x32[:, :n], start=True, stop=True), [consts, x32], [ps])
                ac(lambda ps=ps: A.activation(out=rn32[:, :n], in_=ps[:, :n], func=AF.Sqrt, scale=1.0 / 64, bias=epst[:, 1:2]), [ps, epst], [rn32])
                dv(lambda: V.reciprocal(out=rn32[:, :n], in_=rn32[:, :n]), [rn32], [rn32])
                dv(lambda: V.tensor_tensor(out=t32[:, :n], in0=t32[:, :n], in1=rn32[:, :n], op=ALU.mult), [t32, rn32], [t32])
                dv(lambda: V.tensor_scalar(out=t32[:, :n], in0=t32[:, :n], scalar1=pv[:, PV["lng"] + hp:PV["lng"] + hp + 1], scalar2=pv[:, PV["lnb"] + hp:PV["lnb"] + hp + 1],
                                            op0=ALU.mult, op1=ALU.add), [t32, pv], [t32])
                P.dma("sp", ex[:, :n], BG[hc, s:s + n], reads=[DB["BG"]], writes=[ex])
                P.dma("sp", a32[:, :n], GG[hc, s:s + n], reads=[DB["GG"]], writes=[a32])
                dv(lambda: V.tensor_tensor(out=t32[:, :n], in0=t32[:, :n], in1=ex[:, :n], op=ALU.add), [t32, ex], [t32])
                yb = yob[bi_ % 2]
                dv(lambda yb=yb: V.tensor_tensor(out=yb[:, :n], in0=t32[:, :n], in1=a32[:, :n], op=ALU.mult), [t32, a32], [yb])
                P.dma("sp", YR[hc, s:s + n], yb[:, :n], reads=[yb], pwrites=[DB["YR"]])
    def layer_ssm(l):
        def dv(fn, reads, writes=(), pwrites=()):
            return P.op("dve", fn, reads=reads, writes=writes, pwrites=pwrites)

        def ac(fn, reads, writes=(), pwrites=()):
            return P.op("act", fn, reads=reads, writes=writes, pwrites=pwrites)

        def pe(fn, reads, writes=(), pwrites=()):
            return P.op("pe", fn, reads=reads, writes=writes, pwrites=pwrites)

        def po(fn, reads, writes=(), pwrites=()):
            return P.op("pool", fn, reads=reads, writes=writes, pwrites=pwrites)

        Bf = P.sb("Bf", [128, 2, T], BF16)
        Cb = P.sb("Cb", [128, 2, T], BF16)
        BT = P.sb("BT", [128, NT, 256], BF16)
        YSa = P.sb("YSa", [128, 4, T])
        DT = P.sb("DT", [128, NT, 16])
        Atm = P.sb("Atm", [128, NT, 16])
        nexpA = P.sb("nexpA", [128, 16])
        stg = [P.sb("sstg%d" % i, [128, 512]) for i in range(2)]
        xsc = [P.sb("xsc%d" % i, [128, 4, 128]) for i in range(2)]
        acs = P.sb("acs", [128, 16])
        nacs = P.sb("nacs", [128, 8])
        coef = P.sb("coef", [128, 8])
        gtot = P.sb("gtot", [128, 8])
        GM = P.sb("GM", [128, 256])
        trA = [P.sb("trA%d" % i, [128, 128]) for i in range(2)]
        df = [P.sb("df%d" % i, [128, 128]) for i in range(2)]
        Lm = [P.sb("Lm%d" % i, [128, 128]) for i in range(2)]
        EB = [P.sb("EB%d" % i, [128, 128]) for i in range(2)]
        WT = [P.sb("WT%d" % i, [128, 128], BF16) for i in range(4)]
        Cd = [P.sb("Cd%d" % i, [128, 128], BF16) for i in range(4)]
        xZ = [P.sb("xZ%d" % i, [128, 2, 128], BF16) for i in range(2)]
        xd = [P.sb("xd%d" % i, [128, 64], BF16) for i in range(2)]
        S32 = P.sb("S32", [128, 8, 64])
        SbZ = P.sb("SbZ", [128, 8, 128], BF16)
        triF = {0: consts[:, CO["miu"]:CO["miu"] + 128], 1: consts[:, CO["mil"]:CO["mil"] + 128]}
        for t_ in xZ:
            po(lambda t_=t_: G.memset(t_[:], 0.0), [], [t_])
        for g in range(2):
            for (dst_t, r0_) in ((Bf, 512 + g * 128), (Cb, 768 + g * 128)):
                for bi_, (s, n) in enumerate(tblocks):
                    sg = stg[bi_ % 2]
                    P.dma("sp", sg[:, :n], XBC[r0_:r0_ + 128, s:s + n], reads=[DB["XBC"]], writes=[sg])
                    po(lambda dst_t=dst_t, sg=sg, s=s, n=n, g=g: G.tensor_copy(out=dst_t[:, g, s:s + n], in_=sg[:, :n]), [sg], [], [dst_t])
                    if dst_t is Bf:
                        nch = n // 128
                        ps = nps()
                        for ci in range(nch):
                            pe(lambda ps=ps, ci=ci, sg=sg: PE.transpose(out=ps[:, ci * 128:(ci + 1) * 128], in_=sg[:, ci * 128:(ci + 1) * 128], identity=ident), [sg, consts],
                               [ps] if ci == 0 else [], [] if ci == 0 else [ps])
                        dv(lambda ps=ps, s=s, n=n, g=g, nch=nch: V.tensor_copy(out=BT[:, s // 128:s // 128 + nch, g * 128:(g + 1) * 128],
                                                                               in_=ps[:, :n].rearrange("p (c t) -> p c t", c=nch)), [ps], [], [BT])
        P.dma("sp", DT[:], DTS.rearrange("(c p) j -> p c j", p=128), reads=[DB["DTS"]], writes=[DT])
        ac(lambda: A.activation(out=nexpA[:], in_=rowp[:, 16:32], func=AF.Exp), [rowp], [nexpA])
        dv(lambda: V.tensor_scalar(out=nexpA[:], in0=nexpA[:], scalar1=-1.0, scalar2=None, op0=ALU.mult), [nexpA], [nexpA])
        for c in range(NT):
            dv(lambda c=c: V.tensor_tensor(out=Atm[:, c, :], in0=DT[:, c, :], in1=nexpA[:], op=ALU.mult), [DT, nexpA], [], [Atm])
        XBv = XBC.rearrange("(q p) t -> p q t", p=128)
        for d in range(2):
            order = list(range(NT)) if d == 0 else (list(range(NCT - 1, -1, -1)) + list(range(NT - 1, NCT - 1, -1)))
            tri = triF[d]
            dv(lambda: V.memset(S32[:], 0.0), [], [S32])
            po(lambda: G.memset(SbZ[:], 0.0), [], [SbZ])
            for ci_, c in enumerate(order):
                tc = slice(c * 128, (c + 1) * 128)
                xc = xsc[ci_ % 2]
                P.dma("sp", xc[:], XBv[:, 0:4, tc], reads=[DB["XBC"]], writes=[xc])
                psX = nps()
                for q in range(4):
                    pe(lambda q=q, xc=xc, psX=psX: PE.transpose(out=psX[:, q * 128:(q + 1) * 128], in_=xc[:, q, :], identity=ident), [xc, consts],
                       [psX] if q == 0 else [], [] if q == 0 else [psX])
                psS = nps()
                pe(lambda psS=psS: PE.matmul(psS[:, 0:8], lhsT=tri, rhs=Atm[:, c, d * 8:(d + 1) * 8], start=True, stop=True), [consts, Atm], [psS])
                pe(lambda psS=psS: PE.matmul(psS[:, 8:16], lhsT=onesF, rhs=Atm[:, c, d * 8:(d + 1) * 8], start=True, stop=True), [consts, Atm], [], [psS])
                dv(lambda psS=psS: V.tensor_copy(out=acs[:], in_=psS[:, 0:16]), [psS], [acs])
                dv(lambda: V.tensor_scalar(out=nacs[:], in0=acs[:, 0:8], scalar1=-1.0, scalar2=None, op0=ALU.mult), [acs], [nacs])
                dv(lambda: V.tensor_tensor(out=coef[:], in0=acs[:, 8:16], in1=acs[:, 0:8], op=ALU.subtract), [acs], [coef])
                ac(lambda: A.activation(out=coef[:], in_=coef[:], func=AF.Exp), [coef], [coef])
                dv(lambda: V.tensor_tensor(out=coef[:], in0=coef[:], in1=DT[:, c, d * 8:(d + 1) * 8], op=ALU.mult), [coef, DT], [coef])
                ac(lambda: A.activation(out=gtot[:], in_=acs[:, 8:16], func=AF.Exp), [acs], [gtot])
                psG = nps()
                for g in range(2):
                    pe(lambda g=g, psG=psG: PE.matmul(psG[:, g * 128:(g + 1) * 128], lhsT=Bf[:, g, tc], rhs=Cb[:, g, tc], start=True, stop=True), [Bf, Cb],
                       [psG] if g == 0 else [], [] if g == 0 else [psG])
                for g in range(2):
                    dv(lambda g=g, psG=psG: V.tensor_tensor(out=GM[:, g * 128:(g + 1) * 128], in0=psG[:, g * 128:(g + 1) * 128], in1=tri, op=ALU.mult), [psG, consts],
                       [GM] if g == 0 else [], [] if g == 0 else [GM])
                psB = [nps(), nps()]
                psY = nps()
                psSt = nps()
                for h in range(8):
                    j = d * 8 + h
                    g = h // 4
                    e = h % 2
                    hp = h // 2
                    bcol = slice((h % 4) * 128, (h % 4 + 1) * 128)
                    pb = psB[h // 4]
                    ta, dfx, lm, eb = trA[h % 2], df[h % 2], Lm[h % 2], EB[h % 2]
                    wt, cd = WT[h % 4], Cd[h % 4]
                    dv(lambda ta=ta, j=j: V.tensor_scalar(out=ta[:], in0=tri, scalar1=Atm[:, c, j:j + 1], scalar2=None, op0=ALU.mult), [consts, Atm], [ta])
                    pe(lambda ta=ta, pb=pb, bcol=bcol: PE.matmul(pb[:, bcol], lhsT=onesF, rhs=ta[:], start=True, stop=True), [consts, ta],
                       [pb] if h % 4 == 0 else [], [] if h % 4 == 0 else [pb])
                    dv(lambda dfx=dfx, pb=pb, bcol=bcol, h=h: V.tensor_scalar(out=dfx[:], in0=pb[:, bcol], scalar1=nacs[:, h:h + 1], scalar2=0.0, op0=ALU.add, op1=ALU.min), [pb, nacs], [dfx])
                    ac(lambda dfx=dfx, lm=lm: A.activation(out=lm[:], in_=dfx[:], func=AF.Exp), [dfx], [lm])
                    po(lambda lm=lm, wt=wt, g=g: G.tensor_tensor(out=wt[:], in0=lm[:], in1=GM[:, g * 128:(g + 1) * 128], op=ALU.mult), [lm, GM], [wt])
                    ac(lambda eb=eb, pb=pb, bcol=bcol: A.activation(out=eb[:], in_=pb[:, bcol], func=AF.Exp), [pb], [eb])
                    po(lambda eb=eb, cd=cd, g=g: G.tensor_tensor(out=cd[:], in0=Cb[:, g, tc], in1=eb[:], op=ALU.mult), [eb, Cb], [cd])
                    xz = xZ[hp % 2]
                    dv(lambda xz=xz, e=e, h=h, j=j, psX=psX: V.tensor_scalar(out=xz[:, e, e * 64:(e + 1) * 64], in0=psX[:, h * 64:(h + 1) * 64], scalar1=DT[:, c, j:j + 1], scalar2=None, op0=ALU.mult),
                       [psX, DT], [], [xz])
                    xdd = xd[h % 2]
                    dv(lambda xdd=xdd, h=h, psX=psX: V.tensor_scalar(out=xdd[:], in0=psX[:, h * 64:(h + 1) * 64], scalar1=coef[:, h:h + 1], scalar2=None, op0=ALU.mult), [psX, coef], [xdd])
                    ycol = slice(hp * 128, (hp + 1) * 128)
                    firsty = (h == 0)
                    pe(lambda xz=xz, e=e, wt=wt, ycol=ycol: PE.matmul(psY[:, ycol], lhsT=xz[:, e, :], rhs=wt[:], start=(e == 0), stop=False), [xz, wt],
                       [psY] if firsty else [], [] if firsty else [psY])
                    pe(lambda h=h, cd=cd, ycol=ycol, e=e: PE.matmul(psY[:, ycol], lhsT=SbZ[:, h, :], rhs=cd[:], start=False, stop=(e == 1)), [SbZ, cd], [], [psY])
                    pe(lambda h=h, g=g, xdd=xdd: PE.matmul(psSt[:, h * 64:(h + 1) * 64], lhsT=BT[:, c, g * 128:(g + 1) * 128], rhs=xdd[:], start=True, stop=True), [BT, xdd],
                       [psSt] if h == 0 else [], [] if h == 0 else [psSt])
                    dv(lambda h=h: V.scalar_tensor_tensor(out=S32[:, h, :], in0=S32[:, h, :], scalar=gtot[:, h:h + 1], in1=psSt[:, h * 64:(h + 1) * 64], op0=ALU.mult, op1=ALU.add),
                       [S32, gtot, psSt], [], [S32])
                    ac(lambda h=h, e=e: A.activation(out=SbZ[:, h, e * 64:(e + 1) * 64], in_=S32[:, h, :], func=AF.Copy), [S32], [], [SbZ])
                if d == 0:
                    ac(lambda psY=psY: A.activation(out=YSa[:, :, tc], in_=psY[:].rearrange("p (q t) -> p q t", q=4), func=AF.Copy), [psY], [], [YSa])
                else:
                    dv(lambda psY=psY: V.tensor_tensor(out=YSa[:, :, tc], in0=psY[:].rearrange("p (q t) -> p q t", q=4), in1=YSa[:, :, tc], op=ALU.add), [psY, YSa], [], [YSa])
        tt_ = [P.sb("sst%d" % i, [128, 512]) for i in range(2)]
        sq_ = [P.sb("ssq%d" % i, [128, 512]) for i in range(2)]
        xl = [P.sb("sxl%d" % i, [128, 512]) for i in range(2)]
        zl = [P.sb("szl%d" % i, [128, 512]) for i in range(2)]
        rs = P.sb("srs", [128, 512])
        yb_ = [P.sb("syb%d" % i, [128, 512], BF16) for i in range(2)]
        for (s, n) in tblocks:
            for gg in range(2):
                psn = nps()
                for i_ in range(2):
                    hp = gg * 2 + i_
                    hc = slice(hp * 128, (hp + 1) * 128)
                    P.dma("sp", xl[i_][:, :n], XBC[hc, s:s + n], reads=[DB["XBC"]], writes=[xl[i_]])
                    P.dma("sp", zl[i_][:, :n], ZS[hc, s:s + n], reads=[DB["ZS"]], writes=[zl[i_]])
                    dv(lambda i_=i_, hp=hp: V.scalar_tensor_tensor(out=tt_[i_][:, :n], in0=xl[i_][:, :n], scalar=pv[:, PV["sd"] + hp:PV["sd"] + hp + 1], in1=YSa[:, hp, s:s + n],
                                                                 op0=ALU.mult, op1=ALU.add), [xl[i_], pv, YSa], [tt_[i_]])
                    dv(lambda i_=i_: V.tensor_tensor(out=tt_[i_][:, :n], in0=tt_[i_][:, :n], in1=zl[i_][:, :n], op=ALU.mult), [tt_[i_], zl[i_]], [tt_[i_]])
                    ac(lambda i_=i_: A.activation(out=sq_[i_][:, :n], in_=tt_[i_][:, :n], func=AF.Square), [tt_[i_]], [sq_[i_]])
                    pe(lambda i_=i_, psn=psn: PE.matmul(psn[:, :n], lhsT=onesF, rhs=sq_[i_][:, :n], start=(i_ == 0), stop=(i_ == 1)), [consts, sq_[i_]],
                       [psn] if i_ == 0 else [], [] if i_ == 0 else [psn])
                ac(lambda psn=psn: A.activation(out=rs[:, :n], in_=psn[:, :n], func=AF.Sqrt, scale=1.0 / 256, bias=epst[:, 0:1]), [psn, epst], [rs])
                dv(lambda: V.reciprocal(out=rs[:, :n], in_=rs[:, :n]), [rs], [rs])
                for i_ in range(2):
                    hp = gg * 2 + i_
                    hc = slice(hp * 128, (hp + 1) * 128)
                    dv(lambda i_=i_: V.tensor_tensor(out=tt_[i_][:, :n], in0=tt_[i_][:, :n], in1=rs[:, :n], op=ALU.mult), [tt_[i_], rs], [tt_[i_]])
                    dv(lambda i_=i_, hp=hp: V.tensor_scalar(out=yb_[i_][:, :n], in0=tt_[i_][:, :n], scalar1=pv[:, PV["sng"] + hp:PV["sng"] + hp + 1], scalar2=None, op0=ALU.mult),
                       [tt_[i_], pv], [yb_[i_]])
                    P.dma("sp", YS[hc, s:s + n], yb_[i_][:, :n], reads=[yb_[i_]], pwrites=[DB["YS"]])
    GTv = GT.rearrange("(b q p) t -> p b q t", b=3, q=8)
    H2D = dscr("H2D", [D, T], BF16)
    H2v = H2D.rearrange("(c p) t -> p c t", p=128)
    DB["H2"] = P.buf("H2")

    def layer_back(l):
        ml = modL[l]

        def dv(fn, reads, writes=(), pwrites=()):
            return P.op("dve", fn, reads=reads, writes=writes, pwrites=pwrites)

        def ac(fn, reads, writes=(), pwrites=()):
            return P.op("act", fn, reads=reads, writes=writes, pwrites=pwrites)

        def pe(fn, reads, writes=(), pwrites=()):
            return P.op("pe", fn, reads=reads, writes=writes, pwrites=pwrites)

        def po(fn, reads, writes=(), pwrites=()):
            return P.op("pool", fn, reads=reads, writes=writes, pwrites=pwrites)

        h2s = P.sb("h2s", [128, 8, 512], BF16)
        Rt = P.sb("Rt", [128, 8, 512])
        sqr = [P.sb("sqr%d" % i, [128, 512]) for i in range(2)]
        mean = P.sb("mean", [128, 512])
        rstd = P.sb("rstd", [128, 512])
        xa = [P.sb("xa%d" % i, [128, 512]) for i in range(2)]

        def layer_norm(s, n, gname, bname, first):
            nidx = 0 if s >= CTX else 1
            psm = nps()
            for oc in range(8):
                pe(lambda oc=oc: PE.matmul(psm[:, :n], lhsT=onesF, rhs=Rt[:, oc, :n], start=(oc == 0), stop=(oc == 7)), [consts, Rt], [psm] if oc == 0 else [], [] if oc == 0 else [psm])
            ac(lambda: A.activation(out=mean[:, :n], in_=psm[:, :n], func=AF.Copy, scale=-1.0 / 1024) if False else A.activation(out=mean[:, :n], in_=psm[:, :n], func=AF.Identity, scale=-1.0 / 1024),
               [psm], [mean])
            psv = nps()
            for oc in range(8):
                po(lambda oc=oc: G.tensor_tensor(out=Rt[:, oc, :n], in0=Rt[:, oc, :n], in1=mean[:, :n], op=ALU.add), [Rt, mean], [], [Rt])
                sq = sqr[oc % 2]
                ac(lambda oc=oc, sq=sq: A.activation(out=sq[:, :n], in_=Rt[:, oc, :n], func=AF.Square), [Rt], [sq])
                pe(lambda oc=oc, sq=sq: PE.matmul(psv[:, :n], lhsT=onesF, rhs=sq[:, :n], start=(oc == 0), stop=(oc == 7)), [consts, sq], [psv] if oc == 0 else [], [] if oc == 0 else [psv])
            ac(lambda: A.activation(out=rstd[:, :n], in_=psv[:, :n], func=AF.Sqrt, scale=1.0 / 1024, bias=epst[:, 0:1]), [psv, epst], [rstd])
            dv(lambda: V.reciprocal(out=rstd[:, :n], in_=rstd[:, :n]), [rstd], [rstd])
            for oc in range(8):
                dv(lambda oc=oc: V.tensor_tensor(out=Rt[:, oc, :n], in0=Rt[:, oc, :n], in1=rstd[:, :n], op=ALU.mult), [Rt, rstd], [], [Rt])
                dv(lambda oc=oc: V.tensor_scalar(out=Rt[:, oc, :n], in0=Rt[:, oc, :n], scalar1=pv[:, PV[gname] + oc:PV[gname] + oc + 1], scalar2=pv[:, PV[bname] + oc:PV[bname] + oc + 1],
                                                 op0=ALU.mult, op1=ALU.add), [Rt, pv], [], [Rt])
                if first:
                    ac(lambda oc=oc: A.activation(out=h2s[:, oc, :n], in_=Rt[:, oc, :n], func=AF.Identity,
                                                   bias=ml[:, (24 + oc) * 2 + nidx:(24 + oc) * 2 + nidx + 1], scale=onep[:, (32 + oc) * 2 + nidx:(32 + oc) * 2 + nidx + 1]),
                       [Rt, ml, onep], [h2s] if oc == 0 else [], [] if oc == 0 else [h2s])
            P.dma("sp", XTv[:, :, s:s + n], Rt[:, :, :n], reads=[Rt], pwrites=XTb)
            if first:
                P.dma("act", H2v[:, :, s:s + n], h2s[:, :, :n], reads=[h2s], pwrites=[DB["H2"]])

        with P.scope():
            pw = P.sb("pw", [128, 12, 1024], BF16)
            wo = P.sb("wo", [128, 8, 1024], BF16)
            wst = [P.sb("wst%d" % i, [128, 4, 1024]) for i in range(2)]
            srcs = [(pw, 0, p_attn[l]), (pw, 4, p_rwkv[l]), (pw, 8, p_ssm[l]), (wo, 0, w_out[l][0:512, :]), (wo, 4, w_out[l][512:1024, :])]
            for i_, (dst_t, k0, src) in enumerate(srcs):
                sg = wst[i_ % 2]
                P.dma("sp", sg[:], src.rearrange("(k p) n -> p k n", p=128), writes=[sg])
                po(lambda dst_t=dst_t, k0=k0, sg=sg: G.tensor_copy(out=dst_t[:, k0:k0 + 4, :], in_=sg[:]), [sg], [], [dst_t])
            yb3 = [P.sb("y3_%d" % i, [128, 4, 512], BF16) for i in range(3)]
            gts = [P.sb("gts%d" % i, [128, 3, 512]) for i in range(2)]
            MT = P.sb("MT", [128, 8, 512], BF16)
            m1 = P.sb("m1", [128, 512])
            m2 = P.sb("m2", [128, 512])
            for (s, n) in tblocks:
                nidx = 0 if s >= CTX else 1
                for b_, (src, nm) in enumerate(((YA, "YA"), (YR, "YR"), (YS, "YS"))):
                    P.dma("sp", yb3[b_][:, :, :n], src.rearrange("(q p) t -> p q t", p=128)[:, :, s:s + n], reads=[DB[nm]], writes=[yb3[b_]])
                for oc in range(8):
                    gt_ = gts[oc % 2]
                    P.dma("sp", gt_[:, :, :n], GTv[:, :, oc, s:s + n], reads=[DB["GT"]], writes=[gt_])
                    pss = [nps(), nps(), nps()]
                    for b_ in range(3):
                        for k in range(4):
                            pe(lambda b_=b_, k=k, oc=oc, pss=pss: PE.matmul(pss[b_][:, :n], lhsT=pw[:, b_ * 4 + k, oc * 128:(oc + 1) * 128], rhs=yb3[b_][:, k, :n], start=(k == 0), stop=(k == 3)),
                               [pw, yb3[b_]], [pss[b_]] if k == 0 else [], [] if k == 0 else [pss[b_]])
                    dv(lambda pss=pss, gt_=gt_: V.tensor_tensor(out=m1[:, :n], in0=pss[0][:, :n], in1=gt_[:, 0, :n], op=ALU.mult), [pss[0], gt_], [m1])
                    dv(lambda pss=pss, gt_=gt_: V.tensor_tensor(out=m2[:, :n], in0=pss[1][:, :n], in1=gt_[:, 1, :n], op=ALU.mult), [pss[1], gt_], [m2])
                    po(lambda: G.tensor_tensor(out=m1[:, :n], in0=m1[:, :n], in1=m2[:, :n], op=ALU.add), [m1, m2], [m1])
                    dv(lambda pss=pss, gt_=gt_: V.tensor_tensor(out=m2[:, :n], in0=pss[2][:, :n], in1=gt_[:, 2, :n], op=ALU.mult), [pss[2], gt_], [m2])
                    po(lambda oc=oc: G.tensor_tensor(out=MT[:, oc, :n], in0=m1[:, :n], in1=m2[:, :n], op=ALU.add), [m1, m2], [MT] if oc == 0 else [], [] if oc == 0 else [MT])
                for oc in range(8):
                    ps = nps()
                    for k in range(8):
                        pe(lambda k=k, oc=oc, ps=ps: PE.matmul(ps[:, :n], lhsT=wo[:, k, oc * 128:(oc + 1) * 128], rhs=MT[:, k, :n], start=(k == 0), stop=(k == 7)), [wo, MT],
                           [ps] if k == 0 else [], [] if k == 0 else [ps])
                    x_ = xa[oc % 2]
                    P.dma("sp", x_[:, :n], XT[oc * 128:(oc + 1) * 128, s:s + n], reads=[XTb[oc]], writes=[x_])
                    po(lambda x_=x_: G.tensor_scalar(out=x_[:, :n], in0=x_[:, :n], scalar1=alpha, scalar2=None, op0=ALU.mult), [x_], [x_])
                    dv(lambda oc=oc, ps=ps, x_=x_: V.scalar_tensor_tensor(out=Rt[:, oc, :n], in0=ps[:, :n], scalar=ml[:, (16 + oc) * 2 + nidx:(16 + oc) * 2 + nidx + 1], in1=x_[:, :n],
                                                                        op0=ALU.mult, op1=ALU.add), [ps, ml, x_], [Rt] if oc == 0 else [], [] if oc == 0 else [Rt])
                layer_norm(s, n, "l1g", "l1b", True)
        if stop_after == "LN1":
            return
        with P.scope():
            groups = [[(s, n) for (s, n) in blocks(0, CTX)]]
            lat = blocks(CTX, T)
            for i in range(0, len(lat), 2):
                groups.append(lat[i:i + 2])
            HID = P.sb("HID", [128, 22, 1024], BF16)
            H2g = P.sb("H2g", [128, 8, 1024], BF16)
            RG = P.sb("RG", [128, 8, 1024])
            w13 = [P.sb("w13_%d" % i, [128, 8, 128], BF16) for i in range(4)]
            w13s = [P.sb("w13s%d" % i, [128, 8, 128]) for i in range(2)]
            w2b_ = [P.sb("w2b_%d" % i, [128, 22, 128], BF16) for i in range(2)]
            w2ss = [P.sb("w2s%d" % i, [128, 22, 128]) for i in range(2)]
            sl = [P.sb("sl%d" % i, [128, 512]) for i in range(2)]
            wc = [0]
            for grp in groups:
                g0 = grp[0][0]
                gn = sum(n_ for (_, n_) in grp)
                P.dma("sp", H2g[:, :, :gn], H2v[:, :, g0:g0 + gn], reads=[DB["H2"]], writes=[H2g])
                for j in range(22):
                    ws = []
                    for wsrc in (ffn_w1, ffn_w3):
                        wb = w13[wc[0] % 4]
                        sg = w13s[wc[0] % 2]
                        P.dma("sp", sg[:], wsrc[l][:, j * 128:(j + 1) * 128].rearrange("(k p) n -> p k n", p=128), writes=[sg])
                        po(lambda wb=wb, sg=sg: G.tensor_copy(out=wb[:], in_=sg[:]), [sg], [wb])
                        wc[0] += 1
                        ws.append(wb)
                    for bi_, (s, n) in enumerate(grp):
                        ps1, ps3 = nps(), nps()
                        for k in range(8):
                            pe(lambda k=k, ps1=ps1, s=s, n=n: PE.matmul(ps1[:, :n], lhsT=ws[0][:, k, :], rhs=H2g[:, k, s - g0:s - g0 + n], start=(k == 0), stop=(k == 7)), [ws[0], H2g],
                               [ps1] if k == 0 else [], [] if k == 0 else [ps1])
                        for k in range(8):
                            pe(lambda k=k, ps3=ps3, s=s, n=n: PE.matmul(ps3[:, :n], lhsT=ws[1][:, k, :], rhs=H2g[:, k, s - g0:s - g0 + n], start=(k == 0), stop=(k == 7)), [ws[1], H2g],
                               [ps3] if k == 0 else [], [] if k == 0 else [ps3])
                        st_ = sl[bi_ % 2]
                        ac(lambda ps1=ps1, st_=st_, n=n: A.activation(out=st_[:, :n], in_=ps1[:, :n], func=AF.Silu), [ps1], [st_])
                        dv(lambda ps3=ps3, st_=st_, s=s, n=n, j=j: V.tensor_tensor(out=HID[:, j, s - g0:s - g0 + n], in0=ps3[:, :n], in1=st_[:, :n], op=ALU.mult), [ps3, st_], [], [HID])
                for oc in range(8):
                    wb = w2b_[oc % 2]
                    w2s = w2ss[oc % 2]
                    P.dma("sp", w2s[:], ffn_w2[l][:, oc * 128:(oc + 1) * 128].rearrange("(j p) n -> p j n", p=128), writes=[w2s])
                    po(lambda wb=wb, w2s=w2s: G.tensor_copy(out=wb[:], in_=w2s[:]), [w2s], [wb])
                    for (s, n) in grp:
                        nidx = 0 if s >= CTX else 1
                        ps = nps()
                        for j in range(22):
                            pe(lambda j=j, ps=ps, s=s, n=n, wb=wb: PE.matmul(ps[:, :n], lhsT=wb[:, j, :], rhs=HID[:, j, s - g0:s - g0 + n], start=(j == 0), stop=(j == 21)), [wb, HID],
                               [ps] if j == 0 else [], [] if j == 0 else [ps])
                        x_ = xa[oc % 2]
                        P.dma("sp", x_[:, :n], XT[oc * 128:(oc + 1) * 128, s:s + n], reads=[XTb[oc]], writes=[x_])
                        po(lambda x_=x_, n=n: G.tensor_scalar(out=x_[:, :n], in0=x_[:, :n], scalar1=alpha, scalar2=None, op0=ALU.mult), [x_], [x_])
                        dv(lambda oc=oc, ps=ps, x_=x_, s=s, n=n, nidx=nidx: V.scalar_tensor_tensor(out=RG[:, oc, s - g0:s - g0 + n], in0=ps[:, :n],
                                                                                               scalar=ml[:, (40 + oc) * 2 + nidx:(40 + oc) * 2 + nidx + 1], in1=x_[:, :n],
                                                                                               op0=ALU.mult, op1=ALU.add), [ps, ml, x_], [], [RG])
                for (s, n) in grp:
                    for oc in range(8):
                        po(lambda oc=oc, s=s, n=n: G.tensor_copy(out=Rt[:, oc, :n], in_=RG[:, oc, s - g0:s - g0 + n]), [RG], [Rt] if oc == 0 else [], [] if oc == 0 else [Rt])
                    layer_norm(s, n, "l2g", "l2b", False)

    def final_out():
        xf = [P.sb("xf%d" % i, [128, 8, 128]) for i in range(2)]
        yo_ = [P.sb("yof%d" % i, [128, D]) for i in range(2)]
        toks = []
        for i in range(NCT, NT):
            xi, yo2 = xf[i % 2], yo_[i % 2]
            P.dma("sp", xi[:], XTv[:, :, i * 128:(i + 1) * 128], reads=XTb, writes=[xi])
            for half in range(2):
                ps = nps()
                for q in range(4):
                    c = half * 4 + q
                    P.op("pe", lambda c=c, q=q, ps=ps, xi=xi: PE.transpose(out=ps[:, q * 128:(q + 1) * 128], in_=xi[:, c, :], identity=ident),
                         reads=[xi, consts], pwrites=[ps] if q else [], writes=[] if q else [ps])
                if half:
                    P.op("act", lambda ps=ps, yo2=yo2: A.activation(out=yo2[:, 512:1024], in_=ps[:], func=AF.Copy), reads=[ps], pwrites=[yo2])
                else:
                    P.op("dve", lambda ps=ps, yo2=yo2: V.tensor_copy(out=yo2[:, 0:512], in_=ps[:]), reads=[ps], writes=[yo2])
            toks.append(P.dma("sp", y_out[(i - NCT) * 128:(i - NCT + 1) * 128, :], yo2[:], reads=[yo2]))
        return toks
    for l in range(DEPTH):
        with P.scope():
            layer_front(l)
        with P.scope():
            layer_attn(l)
        with P.scope():
            layer_rwkv(l)
        with P.scope():
            layer_ssm(l)
        with P.scope():
            layer_back(l)
    with P.scope():
        final_out()
    P.barrier()
    return nc, st, P


def _cols(v, n):
    return np.ascontiguousarray(np.asarray(v, np.float32).reshape(n, 128).T)


def _rope_perm():
    perm = np.zeros(128, np.int64)
    for base in range(0, 128, 32):
        for j in range(16):
            perm[base + j] = base + j + 16
            perm[base + 16 + j] = base + j
    return perm


def host_prep(inp, SEQ, CTX, DEPTH, GRID_W=64):
    T = CTX + SEQ
    perm = _rope_perm()
    common = {}
    consts = np.zeros((128, NCONST), np.float32)
    idx = np.arange(128)
    consts[:, CO["ident"]:CO["ident"] + 128] = np.eye(128, dtype=np.float32)
    consts[:, CO["msl"]:CO["msl"] + 128] = (idx[:, None] > idx[None, :])
    consts[:, CO["msu"]:CO["msu"] + 128] = (idx[:, None] < idx[None, :])
    consts[:, CO["mil"]:CO["mil"] + 128] = (idx[:, None] >= idx[None, :])
    consts[:, CO["miu"]:CO["miu"] + 128] = (idx[:, None] <= idx[None, :])
    consts[:, CO["bo64"]:CO["bo64"] + 128] = ((idx[:, None] // 64) == (idx[None, :] // 64))
    consts[:, CO["ones"]:CO["ones"] + 128] = 1.0
    common["consts"] = consts
    lv = np.zeros((128, 14, 128), np.float32)
    for k_ in range(7):
        bsz = 1 << k_
        blk = idx // (2 * bsz)
        half = (idx // bsz) % 2
        mo = ((blk[:, None] == blk[None, :]) & (half[:, None] == 1) & (half[None, :] == 0)).astype(np.float32)
        lv[:, k_, :] = mo
        lv[:, 7 + k_, :] = mo.T
    common["lvmask"] = lv
    rows_ = SEQ // GRID_W
    row = np.repeat(np.arange(rows_), GRID_W).astype(np.float32)
    col = np.tile(np.arange(GRID_W), rows_).astype(np.float32)
    nf = 16
    inv = (10000.0 ** (-np.arange(nf, dtype=np.float32) / nf)).astype(np.float32)
    ang_r = row[:, None] * inv
    ang_c = col[:, None] * inv
    Ct = np.ones((128, T), np.float32)
    St = np.zeros((128, T), np.float32)
    for p in range(128):
        d = p % 64
        ang = ang_r if d < 32 else ang_c
        j = d % 32
        f = j % 16
        Ct[p, CTX:] = np.cos(ang[:, f])
        St[p, CTX:] = (-np.sin(ang[:, f])) if j < 16 else np.sin(ang[:, f])
    common["rope"] = np.stack([Ct, St])
    f32 = lambda a: np.ascontiguousarray(np.asarray(a, np.float32))
    w_in = f32(inp["w_in"])
    common["w_in"] = w_in
    wq = w_in[:, :, 0:512].reshape(DEPTH, 1024, 4, 128)[:, :, :, perm].reshape(DEPTH, 1024, 512)
    wk = w_in[:, :, 512:1024].reshape(DEPTH, 1024, 4, 128)[:, :, :, perm].reshape(DEPTH, 1024, 512)
    common["w_perm"] = np.ascontiguousarray(np.concatenate([wq, wk], axis=2))
    common["ada_w"] = f32(inp["ada_w"])
    pvec = np.zeros((DEPTH, 128, NPV), np.float32)
    rowp = np.zeros((DEPTH, 128, 32), np.float32)
    for l in range(DEPTH):
        def put(name, v, n):
            pvec[l, :, PV[name]:PV[name] + n] = _cols(v, n)
        put("mu", inp["rw_mu"][l], 15)
        put("w0", np.asarray(inp["rw_w0"][l]).reshape(-1), 8)
        put("a0", np.asarray(inp["rw_a0"][l]).reshape(-1), 8)
        put("kk", inp["rw_kk"][l], 4)
        put("ka", inp["rw_ka"][l], 4)
        put("rk", np.asarray(inp["rw_rk"][l]).reshape(-1), 4)
        put("lng", inp["rw_ln_g"][l], 4)
        put("lnb", inp["rw_ln_b"][l], 4)
        put("cw", np.asarray(inp["ssm_conv_w"][l]).reshape(-1), 24)
        put("cb", inp["ssm_conv_b"][l], 8)
        put("sd", np.repeat(np.asarray(inp["ssm_d"][l]), 64), 4)
        put("sng", inp["ssm_norm_g"][l], 4)
        put("dag", inp["da_norm_g"][l], 1)
        put("l1g", inp["ln1_g"][l], 8)
        put("l1b", inp["ln1_b"][l], 8)
        put("l2g", inp["ln2_g"][l], 8)
        put("l2b", inp["ln2_b"][l], 8)
        put("adab", inp["ada_b"][l], 48)
        for i, nm in enumerate(("da_lq1", "da_lk1", "da_lq2", "da_lk2")):
            pvec[l, 0:64, PV["dal"] + i] = np.asarray(inp[nm][l], np.float32)
        rowp[l, :, 0:16] = np.broadcast_to(np.asarray(inp["ssm_dt_bias"][l], np.float32).reshape(1, 16), (128, 16))
        rowp[l, :, 16:32] = np.broadcast_to(np.asarray(inp["ssm_a_log"][l], np.float32).reshape(1, 16), (128, 16))
    common["pvec"] = pvec
    common["rowp"] = rowp
    common["rw_w2"] = f32(inp["rw_w2"]).reshape(DEPTH, 128, 512)
    common["rw_a2"] = f32(inp["rw_a2"]).reshape(DEPTH, 128, 512)
    common["rw_g2"] = f32(inp["rw_g2"])
    for nm in ("p_attn", "p_rwkv", "p_ssm", "w_out", "ffn_w1", "ffn_w3", "ffn_w2"):
        common[nm] = f32(inp[nm])
    in_maps = []
    x = np.asarray(inp["x"], np.float32)
    c = np.asarray(inp["c"], np.float32)
    ctx = np.asarray(inp["ctx"], np.float32)
    c_ctx = np.asarray(inp["c_ctx"], np.float32)
    for b in range(x.shape[0]):
        m = dict(common)
        m["x"] = np.ascontiguousarray(x[b])
        m["ctx"] = np.ascontiguousarray(ctx[b])
        cc = np.zeros((128, 16), np.float32)
        cc[:, 0::2] = _cols(c[b], 8)
        cc[:, 1::2] = _cols(c_ctx, 8)
        m["cc"] = cc
        in_maps.append(m)
    return in_maps


SEQ_FULL, CTX_FULL, DEPTH_FULL = 4096, 256, 4


def kernel(**inputs):
    in_maps = host_prep(inputs, SEQ_FULL, CTX_FULL, DEPTH_FULL)
    nc, st, P = build(SEQ_FULL, CTX_FULL, DEPTH_FULL)
    res = run_bass_kernel_spmd(nc, in_maps, core_ids=list(range(len(in_maps))))
    y = np.stack([np.asarray(r["y"], dtype=np.float32) for r in res.results], axis=0)
    return y
```
